# Optimizing a Trainium2 kernel written in Bass

```python
import math
import jax, jax.numpy as jnp
from jax import lax
import numpy as np

D_MODEL = 1024
BATCH = 8
SEQ = 2048
DEPTH = 4
DEC_BATCH = 32
DEC_SEQ = 2048
PAST_LEN = 128

A_HEADS = 8
A_HEAD_DIM = 64
B_GROUPS = ((128, 1), (512, 4), (2048, 16))
N_GROUPS = 3
B_HEADS = 8
B_HEAD_DIM = 64
X_HEADS = 4
X_HEAD_DIM = 64
N_MEM = 256
D_FF = 2816
CONV_WIDTH = 3
ROPE_THETA = 500000.0
Q_BLOCK = 128
BAND_BLOCK = 64
EPS = 1e-6
NEG_INF = -1e30
N_A_LAYERS = (DEPTH + 1) // 2
N_B_LAYERS = DEPTH // 2
X_WIDTH = X_HEADS * X_HEAD_DIM
A_IN = 4 * A_HEADS * A_HEAD_DIM + 2 * A_HEADS * A_HEAD_DIM + X_WIDTH
A_OUT = 2 * A_HEADS * A_HEAD_DIM + X_WIDTH
B_IN = 3 * N_GROUPS * B_HEADS * B_HEAD_DIM + X_WIDTH
B_OUT = B_HEADS * B_HEAD_DIM + X_WIDTH

kernel_name = 'hybrid_diff_dilated_memory_encoder'


def rms_norm(x, g):
    xf = x.astype(jnp.float32)
    y = xf * lax.rsqrt(jnp.mean(xf * xf, axis=-1, keepdims=True) + EPS)
    return (y * g.astype(jnp.float32)).astype(x.dtype)


def rope_tables(S, dh):
    rot = dh // 4
    half = rot // 2
    inv = ROPE_THETA ** (-(jnp.arange(half, dtype=jnp.float32) * 2.0 / rot))
    ang = jnp.arange(S, dtype=jnp.float32)[:, None] * inv[None, :]
    return jnp.cos(ang), jnp.sin(ang)


def apply_rope(t, cos, sin):
    half = cos.shape[-1]
    rot = 2 * half
    tf = t[..., :rot].astype(jnp.float32)
    t1, t2 = tf[..., :half], tf[..., half:]
    c = cos[None, :, None, :]
    s = sin[None, :, None, :]
    r = jnp.concatenate([t1 * c - t2 * s, t2 * c + t1 * s], axis=-1).astype(t.dtype)
    return jnp.concatenate([r, t[..., rot:]], axis=-1)


def diff_mixer(p, cos, sin, q_g, k_g, lam_p, subln_g, lam_init):
    B, S, _ = p.shape
    H, dh = A_HEADS, A_HEAD_DIM
    qk = p[..., :4 * H * dh].reshape(B, S, 4 * H, dh)
    q = apply_rope(rms_norm(qk[:, :, :2 * H], q_g), cos, sin)
    k = apply_rope(rms_norm(qk[:, :, 2 * H:], k_g), cos, sin)
    q = q.reshape(B, S, 2, H, dh).transpose(2, 0, 1, 3, 4)
    k = k.reshape(B, S, 2, H, dh).transpose(2, 0, 1, 3, 4)
    v = p[..., 4 * H * dh:].reshape(B, S, H, 2 * dh)
    lp = lam_p.astype(jnp.float32)
    lam = jnp.exp(jnp.sum(lp[0] * lp[1])) - jnp.exp(jnp.sum(lp[2] * lp[3])) + lam_init
    scale = 1.0 / math.sqrt(dh)
    nb = S // Q_BLOCK
    qb = q.reshape(2, B, nb, Q_BLOCK, H, dh).transpose(2, 0, 1, 3, 4, 5)

    def block(qblk):
        s = jnp.einsum('mbqhd,mbkhd->mbhqk', qblk, k).astype(jnp.float32) * scale
        pr = jax.nn.softmax(s, axis=-1)
        a = pr[0] - lam * pr[1]
        return jnp.einsum('bhqk,bkhe->bqhe', a.astype(v.dtype), v)

    o = lax.map(block, qb)
    o = o.transpose(1, 0, 2, 3, 4).reshape(B, S, H, 2 * dh)
    o = rms_norm(o, subln_g) * (1.0 - lam_init)
    return o.reshape(B, S, H * 2 * dh)


def dilated_group(q, k, v, dil, radius):
    B, S, H, dh = q.shape
    L = S // dil

    def to_res(t):
        return t.reshape(B, L, dil, H, dh).transpose(0, 2, 1, 3, 4).reshape(B * dil, L, H, dh)

    qr, kr, vr = to_res(q), to_res(k), to_res(v)
    bq = math.gcd(L, BAND_BLOCK)
    nb = L // bq
    W = bq + 2 * radius
    pad = ((0, 0), (radius, radius), (0, 0), (0, 0))
    kp = jnp.pad(kr, pad)
    vp = jnp.pad(vr, pad)
    idx = jnp.arange(nb)[:, None] * bq + jnp.arange(W)[None, :]
    kb = kp[:, idx]
    vb = vp[:, idx]
    qb = qr.reshape(B * dil, nb, bq, H, dh)
    s = jnp.einsum('bnqhd,bnkhd->bnhqk', qb, kb).astype(jnp.float32) * (1.0 / math.sqrt(dh))
    kpos = idx - radius
    qpos = jnp.arange(nb)[:, None] * bq + jnp.arange(bq)[None, :]
    rel = kpos[:, None, :] - qpos[:, :, None]
    valid = (jnp.abs(rel) <= radius) & (kpos[:, None, :] >= 0) & (kpos[:, None, :] < L)
    s = jnp.where(valid[None, :, None], s, NEG_INF)
    lse = jax.nn.logsumexp(s, axis=-1)
    pr = jnp.exp(s - lse[..., None])
    o = jnp.einsum('bnhqk,bnkhd->bnqhd', pr.astype(v.dtype), vb)
    o = o.reshape(B, dil, L, H, dh).transpose(0, 2, 1, 3, 4).reshape(B, S, H, dh)
    lse = lse.transpose(0, 1, 3, 2).reshape(B, dil, L, H).transpose(0, 2, 1, 3).reshape(B, S, H)
    return o, lse


def dilated_mixer(p, cos, sin, q_g, k_g):
    B, S, _ = p.shape
    G, H, dh = N_GROUPS, B_HEADS, B_HEAD_DIM
    n = G * H * dh
    q = p[..., :n].reshape(B, S, G, H, dh)
    k = p[..., n:2 * n].reshape(B, S, G, H, dh)
    v = p[..., 2 * n:].reshape(B, S, G, H, dh)
    outs, lses = [], []
    for g, (window, dil) in enumerate(B_GROUPS):
        qg = apply_rope(rms_norm(q[:, :, g], q_g[g]), cos, sin)
        kg = apply_rope(rms_norm(k[:, :, g], k_g[g]), cos, sin)
        o, lse = dilated_group(qg, kg, v[:, :, g], dil, (window // 2) // dil)
        outs.append(o)
        lses.append(lse)
    w = jax.nn.softmax(jnp.stack(lses, axis=0), axis=0)
    o = jnp.einsum('gbsh,gbshd->bshd', w.astype(p.dtype), jnp.stack(outs, axis=0))
    return o.reshape(B, S, H * dh)


def memory_attention(xq, mk, mv):
    s = jnp.einsum('bqhd,bkhd->bhqk', xq, mk).astype(jnp.float32) * (1.0 / math.sqrt(X_HEAD_DIM))
    pr = jax.nn.softmax(s, axis=-1)
    return jnp.einsum('bhqk,bkhd->bqhd', pr.astype(mv.dtype), mv)


def conv_ffn(h, w_up, conv_w, conv_b, w_down):
    u = h @ w_up
    up = jnp.pad(u, ((0, 0), (1, 1), (0, 0)))
    u = up[:, :-2] * conv_w[0] + up[:, 1:-1] * conv_w[1] + up[:, 2:] * conv_w[2] + conv_b
    a, b = u[..., :D_FF], u[..., D_FF:]
    return (jax.nn.silu(a) * b) @ w_down


def trunk(x, mem, norm_mix, norm_mem, w_mem_kv, xq_norm, xk_norm, a_w_in, a_w_out, a_q_norm, a_k_norm,
          a_lambda, a_subln, b_w_in, b_w_out, b_q_norm, b_k_norm, norm_ffn, w_up, conv_w, conv_b, w_down):
    B, S, _ = x.shape
    M = mem.shape[1]
    cos_a, sin_a = rope_tables(S, A_HEAD_DIM)
    cos_b, sin_b = rope_tables(S, B_HEAD_DIM)
    for i in range(DEPTH):
        j = i // 2
        h = rms_norm(x, norm_mix[i])
        kv = (rms_norm(mem, norm_mem[i]) @ w_mem_kv[i]).reshape(B, M, 2, X_HEADS, X_HEAD_DIM)
        mk = rms_norm(kv[:, :, 0], xk_norm[i])
        mv = kv[:, :, 1]
        if i % 2 == 0:
            p = h @ a_w_in[j]
            lam_init = 0.8 - 0.6 * math.exp(-0.3 * i)
            mixed = diff_mixer(p[..., :-X_WIDTH], cos_a, sin_a, a_q_norm[j], a_k_norm[j],
                               a_lambda[j], a_subln[j], lam_init)
            w_out = a_w_out[j]
        else:
            p = h @ b_w_in[j]
            mixed = dilated_mixer(p[..., :-X_WIDTH], cos_b, sin_b, b_q_norm[j], b_k_norm[j])
            w_out = b_w_out[j]
        xq = rms_norm(p[..., -X_WIDTH:].reshape(B, S, X_HEADS, X_HEAD_DIM), xq_norm[i])
        cross = memory_attention(xq, mk, mv).reshape(B, S, X_WIDTH)
        x = x + jnp.concatenate([mixed, cross], axis=-1) @ w_out
        x = x + conv_ffn(rms_norm(x, norm_ffn[i]), w_up[i], conv_w[i], conv_b[i], w_down[i])
    return x


def setup_inputs(seed: int = 0) -> dict:
    key = jax.random.key(seed)
    keys = iter(jax.random.split(key, 32))
    f32 = jnp.float32

    def nrm(shape, scale):
        return jax.random.normal(next(keys), shape, f32) * scale

    def gain(shape):
        return 1.0 + nrm(shape, 0.1)

    return {
        'x_prompt': nrm((BATCH, SEQ, D_MODEL), 1.0),
        'x_sample': nrm((DEC_BATCH, DEC_SEQ, D_MODEL), 1.0),
        'mem_prompt': nrm((BATCH, N_MEM, D_MODEL), 1.0),
        'mem_sample': nrm((DEC_BATCH, N_MEM, D_MODEL), 1.0),
        'norm_mix': gain((DEPTH, D_MODEL)),
        'norm_mem': gain((DEPTH, D_MODEL)),
        'w_mem_kv': nrm((DEPTH, D_MODEL, 2 * X_WIDTH), D_MODEL ** -0.5),
        'xq_norm': gain((DEPTH, X_HEAD_DIM)),
        'xk_norm': gain((DEPTH, X_HEAD_DIM)),
        'a_w_in': nrm((N_A_LAYERS, D_MODEL, A_IN), D_MODEL ** -0.5),
        'a_w_out': nrm((N_A_LAYERS, A_OUT, D_MODEL), A_OUT ** -0.5),
        'a_q_norm': gain((N_A_LAYERS, A_HEAD_DIM)),
        'a_k_norm': gain((N_A_LAYERS, A_HEAD_DIM)),
        'a_lambda': nrm((N_A_LAYERS, 4, A_HEAD_DIM), 0.1),
        'a_subln': gain((N_A_LAYERS, 2 * A_HEAD_DIM)),
        'b_w_in': nrm((N_B_LAYERS, D_MODEL, B_IN), D_MODEL ** -0.5),
        'b_w_out': nrm((N_B_LAYERS, B_OUT, D_MODEL), B_OUT ** -0.5),
        'b_q_norm': gain((N_B_LAYERS, N_GROUPS, B_HEAD_DIM)),
        'b_k_norm': gain((N_B_LAYERS, N_GROUPS, B_HEAD_DIM)),
        'norm_ffn': gain((DEPTH, D_MODEL)),
        'w_up': nrm((DEPTH, D_MODEL, 2 * D_FF), D_MODEL ** -0.5),
        'conv_w': jnp.array([0.0, 1.0, 0.0], f32)[None, :, None] + nrm((DEPTH, CONV_WIDTH, 2 * D_FF), 0.3),
        'conv_b': nrm((DEPTH, 2 * D_FF), 0.02),
        'w_down': nrm((DEPTH, D_FF, D_MODEL), D_FF ** -0.5),
    }


def reference(x_prompt, x_sample, mem_prompt, mem_sample, norm_mix, norm_mem, w_mem_kv, xq_norm, xk_norm,
              a_w_in, a_w_out, a_q_norm, a_k_norm, a_lambda, a_subln, b_w_in, b_w_out, b_q_norm, b_k_norm,
              norm_ffn, w_up, conv_w, conv_b, w_down):
    y_prompt = trunk(x_prompt, mem_prompt, norm_mix, norm_mem, w_mem_kv, xq_norm, xk_norm, a_w_in, a_w_out,
                     a_q_norm, a_k_norm, a_lambda, a_subln, b_w_in, b_w_out, b_q_norm, b_k_norm,
                     norm_ffn, w_up, conv_w, conv_b, w_down)
    y_sample = trunk(x_sample, mem_sample, norm_mix, norm_mem, w_mem_kv, xq_norm, xk_norm, a_w_in, a_w_out,
                     a_q_norm, a_k_norm, a_lambda, a_subln, b_w_in, b_w_out, b_q_norm, b_k_norm,
                     norm_ffn, w_up, conv_w, conv_b, w_down)
    return (y_prompt, y_sample)
```

```python
import contextlib
import numpy as np
import ml_dtypes
import concourse.bass as bass
import concourse.mybir as mybir
from concourse.bass_utils import run_bass_kernel_spmd

F32 = mybir.dt.float32
BF16 = mybir.dt.bfloat16
AF = mybir.ActivationFunctionType
ALU = mybir.AluOpType
AX = mybir.AxisListType

ENGS = ("pe", "act", "dve", "pool", "sp")
EPS = 1e-6


class Op:
    __slots__ = ("eng", "fn", "deps", "flag", "cnt", "idx", "dma", "dsem", "dval", "dprev")

    def __init__(self, eng, fn):
        self.eng = eng
        self.fn = fn
        self.deps = []
        self.flag = False
        self.cnt = 0
        self.idx = 0
        self.dma = False
        self.dsem = None
        self.dval = 0
        self.dprev = None


def _base(k):
    return k[0] if isinstance(k, tuple) else k


class Sched:
    def __init__(self, nc, n_dma_sems=32):
        self.nc = nc
        self.ops = {e: [] for e in ENGS}
        self.last_w = {}
        self.readers = {}
        self.by_base = {}
        self.alias = {}
        self.n_dma_sems = n_dma_sems
        self.dma_rr = 0
        self.dma_rr_sw = 0
        self.dma_cnt = [0] * n_dma_sems
        self.dma_last = [None] * n_dma_sems
        self.all_dma_out = []

    def set_alias(self, a, b):
        self.alias.setdefault(a, set()).add(b)
        self.alias.setdefault(b, set()).add(a)

    def _deps(self, op, reads, writes):
        deps = {}

        def add(d):
            if d is None or d is op:
                return
            if d.dma:
                deps[("dma", id(d))] = d
                return
            if d.eng == op.eng:
                if op.eng == "pe":
                    return
                if op.idx - d.idx > 2:
                    return
            k = d.eng
            if k not in deps or deps[k].idx < d.idx:
                deps[k] = d

        for r in reads:
            add(self.last_w.get(r))
            al = self.alias.get(_base(r))
            if al:
                for ab in al:
                    for k2 in self.by_base.get(ab, ()):
                        add(self.last_w.get(k2))
        for w in writes:
            add(self.last_w.get(w))
            for rd in self.readers.get(w, {}).values():
                add(rd)
            al = self.alias.get(_base(w))
            if al:
                for ab in al:
                    for k2 in self.by_base.get(ab, ()):
                        add(self.last_w.get(k2))
                        for rd in self.readers.get(k2, {}).values():
                            add(rd)
        rk = ("dma", id(op)) if op.dma else op.eng
        for r in reads:
            self.readers.setdefault(r, {})[rk] = op
            self.by_base.setdefault(_base(r), set()).add(r)
        for w in writes:
            self.last_w[w] = op
            self.readers[w] = {}
            self.by_base.setdefault(_base(w), set()).add(w)
        op.deps = list(deps.values())
        for d in op.deps:
            d.flag = True

    PSUM_BASES = ("SCA", "SCB", "ACC", "PJ", "TP")

    def add(self, eng, fn, reads=(), writes=()):
        op = Op(eng, fn)
        op.idx = len(self.ops[eng])
        rl = [("rl", r) for r in reads if _base(r) in self.PSUM_BASES]
        if rl:
            writes = list(writes) + rl
        self._deps(op, reads, writes)
        self.ops[eng].append(op)
        return op

    def dma(self, fn, reads=(), writes=(), queue="sp", is_output=False):
        op = Op(queue, fn)
        op.dma = True
        op.idx = len(self.ops[queue])
        half = self.n_dma_sems // 2
        if queue == "pool":
            s = half + self.dma_rr_sw
            self.dma_rr_sw = (self.dma_rr_sw + 1) % (self.n_dma_sems - half)
        else:
            s = self.dma_rr
            self.dma_rr = (self.dma_rr + 1) % half
        self.dma_cnt[s] += 1
        op.dsem = s
        op.dval = 16 * self.dma_cnt[s]
        op.dprev = self.dma_last[s]
        self.dma_last[s] = op
        self._deps(op, reads, writes)
        self.ops[queue].append(op)
        if is_output:
            self.all_dma_out.append(op)
        return op

    def emit(self):
        nc = self.nc
        for e in ENGS:
            c = 0
            for op in self.ops[e]:
                if op.dma:
                    continue
                if op.flag:
                    c += 1
                    op.cnt = c
        with contextlib.ExitStack() as st:
            esem = {e: st.enter_context(nc.semaphore("s_" + e)) for e in ENGS}
            dsem = [st.enter_context(nc.semaphore("d_%d" % i)) for i in range(self.n_dma_sems)]
            block = st.enter_context(nc.Block())

            def run(e, eng):
                seen = {}
                for op in self.ops[e]:
                    waits = []
                    for d in op.deps:
                        if d.dma:
                            key = ("d", d.dsem)
                            if seen.get(key, 0) < d.dval:
                                seen[key] = d.dval
                                waits.append((dsem[d.dsem], d.dval))
                        else:
                            key = ("e", d.eng)
                            if seen.get(key, 0) < d.cnt:
                                seen[key] = d.cnt
                                waits.append((esem[d.eng], d.cnt))
                    if op.dma and op.dprev is not None:
                        key = ("d", op.dsem)
                        if seen.get(key, 0) < op.dprev.dval:
                            seen[key] = op.dprev.dval
                            waits.append((dsem[op.dsem], op.dprev.dval))
                    for (s, v) in waits:
                        eng.wait_ge(s, v)
                    ins = op.fn(eng)
                    if op.dma:
                        ins.then_inc(dsem[op.dsem], 16)
                    elif op.flag:
                        ins.then_inc(esem[e], 1)
                if e == "sp":
                    fin = {}
                    for op in self.all_dma_out:
                        fin[op.dsem] = max(fin.get(op.dsem, 0), op.dval)
                    for s, v in fin.items():
                        eng.wait_ge(dsem[s], v)

            @block.tensor
            def _(t):
                run("pe", t)

            @block.scalar
            def _(a):
                run("act", a)

            @block.vector
            def _(v):
                run("dve", v)

            @block.gpsimd
            def _(g):
                run("pool", g)

            @block.sync
            def _(s):
                run("sp", s)


D = 1024
SEQ = 2048
TT = 16
DC = 8
NMEM = 256
DFF = 2816
NFC = 22
B_GROUPS = ((128, 1), (512, 4), (2048, 16))
N_CORES = 8
SEQ_PER_CORE = 5
FFN_GROUPS = [(0, 3), (3, 3), (6, 3), (9, 3), (12, 3), (15, 3), (18, 3), (21, 1)]


def build_program(nseq, layers):
    nc = bass.Bass("TRN2", target_bir_lowering=False)

    def din(name, shape, dt=F32):
        return nc.dram_tensor(name, list(shape), dt, kind="ExternalInput").ap()

    x_d = din("x", [nseq, SEQ, D])
    mem_d = din("mem", [nseq, NMEM, D])
    y_d = nc.dram_tensor("y", [nseq, SEQ, D], F32, kind="ExternalOutput").ap()
    wkv_d = din("w_mem_kv", [4, D, 512])
    awin_d = din("a_w_in", [2, D, 3328])
    awout_d = din("a_w_out", [2, 1280, D])
    bwin_d = din("b_w_in", [2, D, 4864])
    bwout_d = din("b_w_out", [2, 768, D])
    wup_d = din("w_up", [4, D, 2 * DFF])
    wdown_d = din("w_down", [4, DFF, D])
    gains_d = din("gains", [4, 3, 128, D])
    hg_d = din("hg", [4, 128, 1664])
    cw_d = din("cw", [4, 128, 44 * 4])
    rope_d = din("rope", [3, 2, 128, TT * 16])
    ident_d = din("ident", [128, 128])
    mask_d = din("mask", [128, 384])
    sel_d = din("sel", [2, 128])

    with contextlib.ExitStack() as st:
        def sb(name, shape, dt):
            return st.enter_context(nc.sbuf_tensor(name, list(shape), dt))

        def ps(name, shape, dt):
            return st.enter_context(nc.psum_tensor(name, list(shape), dt))

        S = Sched(nc)
        X = sb("X", [128, TT, D], F32)
        HT = sb("HT", [128, DC, SEQ], BF16)
        HG = sb("HG", [128, 1664], F32)
        CW = sb("CW", [128, 44, 4], F32)
        ROPE = sb("ROPE", [128, 3, 2, TT, 16], F32)
        IDENTF = sb("IDENTF", [128, 128], F32)
        IDENT = sb("IDENT", [128, 128], BF16)
        MASK = sb("MASK", [128, 384], BF16)
        SEL = sb("SEL", [2, 128], F32)
        MKT = sb("MKT", [128, 2, 256], BF16)
        MV1 = sb("MV1", [128, 2, 4, 65], BF16)
        SS = sb("SS", [128, 16], F32)
        RS = sb("RS", [128, 16], F32)
        NH = sb("NH", [128, 16], F32)
        ST8 = sb("ST8", [128, 8], F32)
        RT8 = sb("RT8", [128, 8], F32)
        ST8b = sb("ST8b", [128, 8], F32)
        RT8b = sb("RT8b", [128, 8], F32)
        LAM = sb("LAM", [128, 8], F32)
        SLGS = sb("SLGS", [128, 128], F32)
        ARENA_ELEMS = 46400
        AR = sb("AR", [128, ARENA_ELEMS], BF16)
        cursor = {"attn": 0, "ffn": 0}

        def carve(phase, nelem_bf16, shape, dt):
            off = cursor[phase]
            n = int(nelem_bf16)
            n = (n + 15) // 16 * 16
            cursor[phase] = off + n
            assert cursor[phase] <= ARENA_ELEMS, (phase, cursor[phase])
            v = AR[:, off:off + int(nelem_bf16)]
            if dt == F32:
                v = v.bitcast(F32)
            if len(shape) == 2:
                return v
            if len(shape) == 3:
                return v.rearrange("p (a b) -> p a b", a=shape[1])
            if len(shape) == 4:
                return v.rearrange("p (a b c) -> p a b c", a=shape[1], b=shape[2])
            if len(shape) == 5:
                return v.rearrange("p (a b c d) -> p a b c d", a=shape[1], b=shape[2], c=shape[3])
            raise ValueError

        def cb(phase, shape):
            return carve(phase, int(np.prod(shape[1:])), shape, BF16)

        def cf(phase, shape):
            return carve(phase, 2 * int(np.prod(shape[1:])), shape, F32)

        mixt_off = cursor["attn"]
        MIXT = cb("attn", [128, 10, SEQ])
        WIN = cb("attn", [128, DC, 768])
        QTB = cb("attn", [128, 2, SEQ])
        KTB = cb("attn", [128, 2, SEQ])
        v_off = cursor["attn"]
        _V = cb("attn", [128, 4160])
        V1A = AR[:, v_off:v_off + TT * 2 * 129].rearrange("p (a b c) -> p a b c", a=TT, b=2)
        VB = AR[:, v_off:v_off + 2 * TT * 2 * 65].rearrange("p (g a b c) -> p g a b c", g=2, a=TT, b=2)
        sq_off = cursor["attn"]
        SQ = cf("attn", [128, 512])
        QN = cf("attn", [128, 512])
        GAIN = AR[:, sq_off:sq_off + 2048].bitcast(F32)
        QKB = cb("attn", [128, 512])
        QKB2 = cb("attn", [128, 512])
        RA = cf("attn", [128, 128])
        RB = cf("attn", [128, 128])
        et_off = cursor["attn"]
        ET0 = cb("attn", [128, 2, 512])
        ET1 = cb("attn", [128, 2, 512])
        JK = AR[:, et_off:et_off + 1024]
        HB0 = AR[:, et_off + 1024:et_off + 2048]
        a0_off = cursor["attn"]
        A0 = cf("attn", [128, 4, 128])
        HB1 = AR[:, a0_off:a0_off + 1024]
        R0 = cf("attn", [128, 4])
        R1 = cf("attn", [128, 4])
        SSE = cf("attn", [128, 4])
        RSE = cf("attn", [128, 4])
        MB = cb("attn", [128, 4, 128])
        attn_end = cursor["attn"]
        nb_off = mixt_off + 6 * SEQ
        NUMB = AR[:, nb_off:nb_off + 3 * TT * 128].rearrange("p (g a b) -> p g a b", g=3, a=TT)
        df_off = nb_off + 3 * TT * 128
        DENF = AR[:, df_off:df_off + 2 * 3 * TT * 2].bitcast(F32).rearrange("p (g a b) -> p g a b", g=3, a=TT)
        RDEN = QN[0:2, :]
        RDB = SQ
        MEMX = AR[:, mixt_off:mixt_off + 2 * SEQ].bitcast(F32).rearrange("p (a b) -> p a b", a=2)
        WO_off = mixt_off + 10 * SEQ
        WO = AR[:, WO_off:WO_off + 10 * D].rearrange("p (a b) -> p a b", a=10)
        assert 10 * D <= DC * 768 + 2 * SEQ
        WU0 = cb("ffn", [128, DC, 2, 384])
        WU1 = cb("ffn", [128, DC, 2, 384])
        WD = cb("ffn", [128, 3, D])
        G0 = cb("ffn", [128, 3, SEQ])
        G1 = cb("ffn", [128, 3, SEQ])
        u_off = cursor["ffn"]
        U = cf("ffn", [128, SEQ + 2])
        FGAIN = AR[:, u_off:u_off + 2048].bitcast(F32)
        ca_off = cursor["ffn"]
        CA = cf("ffn", [128, SEQ])
        FJK = AR[:, ca_off:ca_off + 1024]
        cb_off = cursor["ffn"]
        CB = cf("ffn", [128, SEQ])
        FHB0 = AR[:, cb_off:cb_off + 1024]
        FHB1 = AR[:, cb_off + 1024:cb_off + 2048]
        ATTN_KEYS = ["MIXT", "MIXH", "WIN", "QTB", "KTB", "V1A", "SQ", "QN", "QKB", "ROPESCR", "ET0", "ET1", "EPI", "A0", "MB",
                     "JK", "HB0", "HB1", "NUMB", "DENF", "RDEN", "RDB", "MEMX", "WO", "GAIN", "SSE", "RSE"]
        FFN_KEYS = ["WU0", "WU1", "WD", "G0", "G1", "U", "CA", "CB", "FJK", "FHB0", "FHB1", "FGAIN"]
        for a_ in ATTN_KEYS:
            for b_ in FFN_KEYS:
                S.set_alias(a_, b_)
        for a_ in ("WIN", "QTB", "KTB"):
            S.set_alias("WO", a_)
        for a_, b_ in [("GAIN", "SQ"), ("GAIN", "QN"), ("JK", "ET0"), ("HB0", "ET1"), ("HB1", "A0"), ("NUMB", "MIXH"),
                       ("DENF", "MIXH"), ("MEMX", "MIXT"), ("RDEN", "QN"), ("RDB", "SQ"),
                       ("FGAIN", "U"), ("FJK", "CA"), ("FHB0", "CB"), ("FHB1", "CB")]:
            S.set_alias(a_, b_)

        def mixkey(c):
            return ("MIXT", c) if c < 6 else ("MIXH", c)

        SCA = ps("SCA", [128, 2, 512], F32)
        SCB = ps("SCB", [128, 2, 512], F32)
        ACC = ps("ACC", [128, 2, 512], F32)
        PJ = ps("PJ", [128, 512], F32)
        TP = ps("TP", [128, 8, 128], BF16)

        SCA_K = [("SCA", 0), ("SCA", 1)]
        SCB_K = [("SCB", 0), ("SCB", 1)]
        ACC_K = [("ACC", 0), ("ACC", 1)]
        S.dma(lambda e: e.dma_start(out=IDENTF[:], in_=ident_d), writes=["IDENTF"])
        S.dma(lambda e: nc.gpsimd.dma_start(out=MASK[:], in_=mask_d), writes=["MASK"], queue="pool")
        S.dma(lambda e: e.dma_start(out=SEL[:], in_=sel_d), writes=["SEL"])
        S.dma(lambda e: e.dma_start(out=ROPE[:].rearrange("p a b c d -> p a b (c d)"),
                                    in_=rope_d.rearrange("a b p n -> p a b n")), writes=["ROPE"])
        S.add("dve", lambda e: e.tensor_copy(IDENT[:], IDENTF[:]), reads=["IDENTF"], writes=["IDENT"])
        S.add("pool", lambda e: e.memset(NH[:], -0.5), writes=["NH"])
        S.add("pool", lambda e: e.memset(MV1[:], 1.0), writes=["MV1"])

        def rsqrt_pool(dst, src, n, scale, rkeys, wkeys):
            S.add("pool", lambda e: e.tensor_scalar(dst, src, scale, EPS, ALU.mult, ALU.add), reads=rkeys, writes=wkeys)
            S.add("pool", lambda e: e.tensor_tensor(dst, dst, NH[:, 0:n], ALU.pow), reads=wkeys + ["NH"], writes=wkeys)

        WIN_K = [("WIN", 0), ("WIN", 1)]

        def load_w(dst, src, wkeys):
            if not isinstance(wkeys, list):
                wkeys = [wkeys]
            S.dma(lambda e: nc.gpsimd.dma_start(out=dst, in_=src), writes=wkeys, queue="pool")

        def norm_to_HT(src3, ntiles, gain_key, hbs, jk, jkkey, hbkeys, dstT, dstkey, xkey, gain_ap):
            for tt in range(ntiles):
                S.add("act", lambda e, tt=tt: e.activation(jk, src3[:, tt, :], AF.Square, accum_out=SS[:, tt:tt + 1]),
                      reads=[(xkey, tt)], writes=[jkkey, ("SS", tt)])
            rsqrt_pool(RS[:, 0:ntiles], SS[:, 0:ntiles], ntiles, 1.0 / D, [("SS", t) for t in range(ntiles)], ["RS"])
            for tt in range(ntiles):
                hb = hbs[tt % 2]
                hk = hbkeys[tt % 2]
                S.add("dve", lambda e, tt=tt, hb=hb: e.scalar_tensor_tensor(hb, src3[:, tt, :], RS[:, tt:tt + 1], gain_ap,
                                                                        ALU.mult, ALU.mult),
                      reads=[(xkey, tt), "RS", gain_key], writes=[hk])
                for c in range(DC):
                    S.add("pe", lambda e, c=c, hb=hb: e.transpose(TP[:, c, :], hb[:, c * 128:(c + 1) * 128], IDENT[:]),
                          reads=[hk, "IDENT"], writes=[("TP", 0), ("TP", 1), "TPB"])
                S.add("act", lambda e, tt=tt: e.copy(dstT[:, :, tt * 128:(tt + 1) * 128], TP[:, :, :]),
                      reads=[("TP", 0), ("TP", 1)], writes=[(dstkey, tt), "TPB"] + (["HTall"] if dstkey == "HT" else []))

        STs = [ST8, ST8b]
        RTs = [RT8, RT8b]
        QKBs = [QKB, QKB2]
        TP_K = [("TP", 0), ("TP", 1)]

        def prepA(k, sqb, sqkey, psv, nh, rkeys):
            sq3 = sqb[:, 0:nh * 64].rearrange("p (a b) -> p a b", a=nh)
            S.add("act", lambda e: e.activation(sq3, psv, AF.Square), reads=rkeys, writes=[sqkey])
            S.add("dve", lambda e: e.tensor_reduce(STs[k][:, 0:nh], sq3, AX.X, ALU.add), reads=[sqkey], writes=[("ST8", k)])
            rsqrt_pool(RTs[k][:, 0:nh], STs[k][:, 0:nh], nh, 1.0 / 64, [("ST8", k)], [("RT8", k)])

        def prepB(k, psv, nh, gainv, cs, outs, rkeys, wkeys, gkey):
            n = nh * 64
            qn3 = QN[:, 0:n].rearrange("p (a b) -> p a b", a=nh)
            rt3 = RTs[k][:, 0:nh].unsqueeze(2).broadcast_to([128, nh, 64])
            S.add("dve", lambda e: e.tensor_tensor(qn3, psv, gainv, ALU.mult), reads=rkeys + [gkey], writes=["QN"])
            if cs is not None:
                ccv, ssv = cs
                cc3 = ccv.unsqueeze(1).broadcast_to([128, nh, 16])
                sa3 = ssv[:, 0:8].unsqueeze(1).broadcast_to([128, nh, 8])
                sb3 = ssv[:, 8:16].unsqueeze(1).broadcast_to([128, nh, 8])
                t = qn3[:, :, 0:16]
                ra = RA[:, 0:nh * 16].rearrange("p (a b) -> p a b", a=nh)
                rb = RB[:, 0:nh * 16].rearrange("p (a b) -> p a b", a=nh)
                S.add("dve", lambda e: e.tensor_tensor(ra, t, cc3, ALU.mult), reads=["QN", "ROPE"], writes=[("ROPESCR", 0)])
                S.add("dve", lambda e: e.tensor_tensor(rb[:, :, 0:8], qn3[:, :, 8:16], sa3, ALU.mult), reads=["QN", "ROPE"], writes=[("ROPESCR", 1)])
                S.add("dve", lambda e: e.tensor_tensor(rb[:, :, 8:16], qn3[:, :, 0:8], sb3, ALU.mult), reads=["QN", "ROPE"], writes=[("ROPESCR", 2)])
                S.add("dve", lambda e: e.tensor_tensor(t, ra, rb, ALU.add), reads=[("ROPESCR", 0), ("ROPESCR", 1), ("ROPESCR", 2)], writes=["QN"])
            for (h0, h1, oap, vf) in outs:
                S.add("dve", lambda e, h0=h0, h1=h1, oap=oap, vf=vf: e.tensor_tensor(oap, vf(qn3[:, h0:h1, :]), vf(rt3[:, h0:h1, :]), ALU.mult),
                      reads=["QN", ("RT8", k)], writes=wkeys)

        def skew(n, stageA, stageB):
            for t in range(n + 1):
                if t < n:
                    stageA(t)
                if t >= 1:
                    stageB(t - 1)

        ident_v = (lambda v: v)

        def proj_tok(bank, bkey, tok_ap_fn, wv, ncols, wkey):
            for c in range(DC):
                S.add("pe", lambda e, c=c: e.matmul(bank[:, 0:ncols], tok_ap_fn(c), wv[:, c, 0:ncols],
                                                    start=(c == 0), stop=(c == DC - 1)),
                      reads=["HTall"] + wkey, writes=[bkey])

        HT_KEYS = [("HT", t) for t in range(TT)]

        def mark_ht_ready():
            pass

        for s in range(nseq):
            for q4 in range(4):
                S.dma(lambda e, s=s, q4=q4: e.dma_start(
                    out=X[:, q4 * 4:(q4 + 1) * 4, :],
                    in_=x_d[s, q4 * 512:(q4 + 1) * 512, :].rearrange("(t p) d -> p t d", p=128)),
                    writes=[("X", t) for t in range(q4 * 4, q4 * 4 + 4)])
            for li in layers:
                j = li // 2
                is_a = (li % 2 == 0)
                nmix = 8 if is_a else 4
                nch = nmix + 2
                S.dma(lambda e, li=li: e.dma_start(out=HG[:], in_=hg_d[li]), writes=["HG"])
                S.dma(lambda e, li=li: e.dma_start(out=CW[:].rearrange("p a b -> p (a b)"), in_=cw_d[li]), writes=["CW"])
                XQG = HG[:, 768:1024].rearrange("p (a b) -> p a b", a=4)
                XKG = HG[:, 1024:1280].rearrange("p (a b) -> p a b", a=4)
                S.dma(lambda e, li=li: e.dma_start(out=GAIN, in_=gains_d[li, 2]), writes=["GAIN"])
                S.dma(lambda e, s=s: e.dma_start(out=MEMX, in_=mem_d[s].rearrange("(t p) d -> p t d", p=128)),
                      writes=[("MEMX", 0), ("MEMX", 1)])
                load_w(WIN[:, :, 0:512], wkv_d[li].rearrange("(c p) n -> p c n", p=128), WIN_K)
                MEMT = QTB[:, 0, 0:DC * 256].rearrange("p (a b) -> p a b", a=DC)
                norm_to_HT(MEMX, 2, "GAIN", [HB0, HB1], JK, "JK", ["HB0", "HB1"], MEMT, "QTB", "MEMX", GAIN)
                pjb = [(PJ[:, :], "PJ"), (ACC[:, 0, :], ("ACC", 0))]
                sqb = [(SCB[:, 0, :], ("SCB", 0)), (SCB[:, 1, :], ("SCB", 1))]

                def memA(mt):
                    bank, bk = pjb[mt % 2]
                    for c in range(DC):
                        S.add("pe", lambda e, c=c, mt=mt, bank=bank: e.matmul(bank, MEMT[:, c, mt * 128:(mt + 1) * 128], WIN[:, c, 0:512],
                                                                          start=(c == 0), stop=(c == DC - 1)),
                              reads=[("QTB", 0), ("QTB", 1)] + WIN_K, writes=[bk])
                    prepA(mt % 2, sqb[mt % 2][0], sqb[mt % 2][1], bank[:, 0:256].rearrange("p (a b) -> p a b", a=4), 4, [bk])
                    S.add("act", lambda e, mt=mt, bank=bank: e.copy(MV1[:, mt, :, 0:64], bank[:, 256:512].rearrange("p (a b) -> p a b", a=4)),
                          reads=[bk], writes=["MV1"])

                def memB(mt):
                    bank, bk = pjb[mt % 2]
                    k = mt % 2
                    prepB(k, bank[:, 0:256].rearrange("p (a b) -> p a b", a=4), 4, XKG, None,
                          [(0, 4, QKBs[k][:, 0:256].rearrange("p (a b) -> p a b", a=4), ident_v)], [bk], [("QKB", k)], "HG")
                    for c2 in range(2):
                        S.add("pe", lambda e, c2=c2, k=k: e.transpose(TP[:, 4 * k + c2, :], QKBs[k][:, c2 * 128:(c2 + 1) * 128], IDENT[:]),
                              reads=[("QKB", k), "IDENT"], writes=[("TP", k), "TPB"])
                    S.add("act", lambda e, mt=mt, k=k: e.copy(MKT[:, :, mt * 128:(mt + 1) * 128], TP[:, 4 * k:4 * k + 2, :]),
                          reads=[("TP", k)], writes=["MKT", "TPB"])

                skew(2, memA, memB)
                S.dma(lambda e, li=li: e.dma_start(out=GAIN, in_=gains_d[li, 0]), writes=["GAIN"])
                norm_to_HT(X, TT, "GAIN", [HB0, HB1], JK, "JK", ["HB0", "HB1"], HT, "HT", "X", GAIN)
                mark_ht_ready()

                win_d = awin_d if is_a else bwin_d
                xq_col0 = 3072 if is_a else 4608
                wview = win_d[j].rearrange("(c p) n -> p c n", p=128)
                load_w(WIN[:, :, 0:256], wview[:, :, xq_col0:xq_col0 + 256], WIN_K)
                XQT = [QTB[:, 0, :], QTB[:, 1, :]]
                XQK = [("QTB", 0), ("QTB", 1)]
                def xqA(tt):
                    bank, bk = pjb[tt % 2]
                    proj_tok(bank, bk, lambda c, tt=tt: HT[:, c, tt * 128:(tt + 1) * 128], WIN, 256, WIN_K)
                    prepA(tt % 2, sqb[tt % 2][0], sqb[tt % 2][1], bank[:, 0:256].rearrange("p (a b) -> p a b", a=4), 4, [bk])

                def xqB(tt):
                    bank, bk = pjb[tt % 2]
                    k = tt % 2
                    prepB(k, bank[:, 0:256].rearrange("p (a b) -> p a b", a=4), 4, XQG, None,
                          [(0, 4, QKBs[k][:, 0:256].rearrange("p (a b) -> p a b", a=4), ident_v)], [bk], [("QKB", k)], "HG")
                    for c2 in range(2):
                        S.add("pe", lambda e, c2=c2, k=k: e.transpose(TP[:, 4 * k + c2, :], QKBs[k][:, c2 * 128:(c2 + 1) * 128], IDENT[:]),
                              reads=[("QKB", k), "IDENT"], writes=[("TP", k), "TPB"])
                    for c2 in range(2):
                        S.add("act", lambda e, tt=tt, c2=c2, k=k: e.copy(XQT[c2][:, tt * 128:(tt + 1) * 128], TP[:, 4 * k + c2, :]),
                              reads=[("TP", k)], writes=[XQK[c2], "TPB"])

                skew(TT, xqA, xqB)
                xits = [(c2, qc, hl) for c2 in range(2) for qc in range(4) for hl in range(2)]

                def x_score(i):
                    c2, qc, hl = xits[i]
                    r0 = hl * 64
                    sc, sk = (SCA, SCA_K) if i % 2 == 0 else (SCB, SCB_K)
                    et, ek = (ET0, "ET0") if i % 2 == 0 else (ET1, "ET1")
                    for mt in range(2):
                        S.add("pe", lambda e, mt=mt, sc=sc, c2=c2, r0=r0, qc=qc: e.matmul(
                            sc[:, mt, :], MKT[r0:r0 + 64, c2, mt * 128:(mt + 1) * 128],
                            XQT[c2][r0:r0 + 64, qc * 512:(qc + 1) * 512], start=True, stop=True),
                            reads=["MKT", XQK[c2]], writes=sk)
                    S.add("act", lambda e, sc=sc, et=et: e.activation(et[:, :, :], sc[:, :, :], AF.Exp, scale=0.125),
                          reads=sk, writes=[ek])

                def x_av(i):
                    c2, qc, hl = xits[i]
                    h = 2 * c2 + hl
                    et, ek = (ET0, "ET0") if i % 2 == 0 else (ET1, "ET1")
                    for qt in range(4):
                        for mt in range(2):
                            S.add("pe", lambda e, qt=qt, mt=mt, et=et, h=h, hl=hl: e.matmul(
                                ACC[:, hl, qt * 65:(qt + 1) * 65], et[:, mt, qt * 128:(qt + 1) * 128], MV1[:, mt, h, :],
                                start=(qt == 0 and mt == 0), stop=(mt == 1), skip_group_check=True),
                                reads=[ek, "MV1"], writes=[("ACC", hl)])
                    accv = ACC[:, hl, 0:260].rearrange("p (a b) -> p a b", a=4)
                    rr = R0 if hl == 0 else R1
                    S.add("dve", lambda e, accv=accv, rr=rr: e.reciprocal(rr[:, :], accv[:, :, 64]),
                          reads=[("ACC", hl)], writes=[("EPI", hl)])
                    S.add("dve", lambda e, accv=accv, hl=hl, rr=rr: e.tensor_tensor(
                        MB[:, :, hl * 64:(hl + 1) * 64], accv[:, :, 0:64],
                        rr[:, :].unsqueeze(2).broadcast_to([128, 4, 64]), ALU.mult),
                        reads=[("ACC", hl), ("EPI", hl)], writes=["MB"])
                    if hl == 1:
                        k = (i // 2) % 2
                        for qt in range(4):
                            S.add("pe", lambda e, qt=qt, k=k: e.transpose(TP[:, 4 * k + qt, :], MB[:, qt, :], IDENT[:]),
                                  reads=["MB", "IDENT"], writes=[("TP", k), "TPB"])
                        S.add("act", lambda e, qc=qc, c2=c2, nmix=nmix, k=k: e.copy(
                            MIXT[:, nmix + c2, qc * 512:(qc + 1) * 512], TP[:, 4 * k:4 * k + 4, :].rearrange("p a b -> p (a b)")),
                            reads=[("TP", k)], writes=[mixkey(nmix + c2), "TPB"])

                x_score(0)
                for i in range(len(xits)):
                    if i + 1 < len(xits):
                        x_score(i + 1)
                    x_av(i)

                if is_a:
                    QKG = HG[:, 0:512].rearrange("p (a b) -> p a b", a=8)
                    lam_init = 0.8 - 0.6 * float(np.exp(-0.3 * li))
                    lp = HG[:, 1408:1664].rearrange("p (a b) -> p a b", a=4)
                    S.add("dve", lambda e: e.tensor_tensor(QN[:, 0:128].rearrange("p (a b) -> p a b", a=2), lp[:, 0:4:2, :], lp[:, 1:4:2, :], ALU.mult),
                          reads=["HG"], writes=["QN"])
                    S.add("dve", lambda e: e.tensor_reduce(LAM[:, 0:2], QN[:, 0:128].rearrange("p (a b) -> p a b", a=2), AX.X, ALU.add),
                          reads=["QN"], writes=["LAM"])
                    S.add("act", lambda e: e.activation(LAM[:, 2:4], LAM[:, 0:2], AF.Exp), reads=["LAM"], writes=["LAM"])
                    S.add("dve", lambda e: e.scalar_tensor_tensor(LAM[:, 4:5], LAM[:, 2:3], -1.0, LAM[:, 3:4], ALU.mult, ALU.add),
                          reads=["LAM"], writes=["LAM"])
                    S.add("dve", lambda e, lam_init=lam_init: e.tensor_scalar(LAM[:, 5:6], LAM[:, 4:5], -lam_init, None, ALU.add),
                          reads=["LAM"], writes=["LAM"])
                    NEGLAM = LAM[:, 5:6]
                    S.add("dve", lambda e, lam_init=lam_init: e.tensor_scalar(SLGS[:], HG[:, 1280:1408], 1.0 - lam_init, None, ALU.mult),
                          reads=["HG"], writes=["SLGS"])
                    for hp in range(4):
                        for seg, (c0, w) in enumerate([(128 * hp, 128), (512 + 128 * hp, 128), (1024 + 128 * hp, 128),
                                                       (1536 + 128 * hp, 128)]):
                            load_w(WIN[:, :, seg * 128:(seg + 1) * 128], wview[:, :, c0:c0 + w], WIN_K)
                        load_w(WIN[:, :, 512:768], wview[:, :, 2048 + 256 * hp:2048 + 256 * hp + 256], WIN_K)
                        vbk = [(SCA[:, 0, :], ("SCA", 0)), (SCA[:, 1, :], ("SCA", 1))]

                        def aA(tt):
                            bank, bk = pjb[tt % 2]
                            proj_tok(bank, bk, lambda c, tt=tt: HT[:, c, tt * 128:(tt + 1) * 128], WIN, 512, WIN_K)
                            prepA(tt % 2, sqb[tt % 2][0], sqb[tt % 2][1], bank[:, 0:512].rearrange("p (a b) -> p a b", a=8), 8, [bk])
                            vb_, vk_ = vbk[tt % 2]
                            for c in range(DC):
                                S.add("pe", lambda e, c=c, tt=tt, vb_=vb_: e.matmul(vb_[:, 0:256], HT[:, c, tt * 128:(tt + 1) * 128],
                                                                                WIN[:, c, 512:768], start=(c == 0), stop=(c == DC - 1)),
                                      reads=["HTall"] + WIN_K, writes=[vk_])
                            S.add("act", lambda e, tt=tt, vb_=vb_: e.copy(V1A[:, tt, :, 0:128], vb_[:, 0:256].rearrange("p (a b) -> p a b", a=2)),
                                  reads=[vk_], writes=["V1A"])

                        def aB(tt):
                            bank, bk = pjb[tt % 2]
                            k = tt % 2
                            vfa = (lambda v: v.rearrange("p (c h) d -> p c h d", c=2))
                            outs_a = [(0, 4, QKBs[k][:, 0:256].rearrange("p (h c d) -> p c h d", h=2, c=2), vfa),
                                      (4, 8, QKBs[k][:, 256:512].rearrange("p (h c d) -> p c h d", h=2, c=2), vfa)]
                            prepB(k, bank[:, 0:512].rearrange("p (a b) -> p a b", a=8), 8, QKG,
                                  (ROPE[:, 0, 0, tt, :], ROPE[:, 0, 1, tt, :]), outs_a, [bk], [("QKB", k)], "HG")
                            for c4 in range(4):
                                S.add("pe", lambda e, c4=c4, k=k: e.transpose(TP[:, 4 * k + c4, :], QKBs[k][:, c4 * 128:(c4 + 1) * 128], IDENT[:]),
                                      reads=[("QKB", k), "IDENT"], writes=[("TP", k), "TPB"])
                            S.add("act", lambda e, tt=tt, k=k: e.copy(QTB[:, 0:2, tt * 128:(tt + 1) * 128], TP[:, 4 * k:4 * k + 2, :]),
                                  reads=[("TP", k)], writes=[("QTB", 0), ("QTB", 1), "TPB"])
                            S.add("act", lambda e, tt=tt, k=k: e.copy(KTB[:, 0:2, tt * 128:(tt + 1) * 128], TP[:, 4 * k + 2:4 * k + 4, :]),
                                  reads=[("TP", k)], writes=[("KTB", 0), ("KTB", 1), "TPB"])

                        skew(TT, aA, aB)
                        if hp == 0:
                            S.add("pool", lambda e: e.memset(V1A[:, :, :, 128:129], 1.0), writes=["V1A"])
                        aits = [(hl, qc, comp, kp) for hl in range(2) for qc in range(4) for comp in range(2) for kp in range(8)]

                        def a_score(i):
                            hl, qc, comp, kp = aits[i]
                            r0 = comp * 64
                            sc, sk = (SCA, SCA_K) if i % 2 == 0 else (SCB, SCB_K)
                            et, ek = (ET0, "ET0") if i % 2 == 0 else (ET1, "ET1")
                            for k2 in range(2):
                                kt = 2 * kp + k2
                                S.add("pe", lambda e, sc=sc, k2=k2, kt=kt, r0=r0, hl=hl, qc=qc: e.matmul(
                                    sc[:, k2, :], KTB[r0:r0 + 64, hl, kt * 128:(kt + 1) * 128],
                                    QTB[r0:r0 + 64, hl, qc * 512:(qc + 1) * 512], start=True, stop=True),
                                    reads=[("KTB", hl), ("QTB", hl)], writes=sk)
                            S.add("act", lambda e, sc=sc, et=et: e.activation(et[:, :, :], sc[:, :, :], AF.Exp, scale=0.125),
                                  reads=sk, writes=[ek])

                        def a_av(i):
                            hl, qc, comp, kp = aits[i]
                            h = 2 * hp + hl
                            et, ek = (ET0, "ET0") if i % 2 == 0 else (ET1, "ET1")
                            for k2 in range(2):
                                kt = 2 * kp + k2
                                for qt in range(4):
                                    bank, off = (0, qt * 129) if qt < 3 else (1, 0)
                                    S.add("pe", lambda e, et=et, k2=k2, kt=kt, qt=qt, bank=bank, off=off, hl=hl: e.matmul(
                                        ACC[:, bank, off:off + 129], et[:, k2, qt * 128:(qt + 1) * 128], V1A[:, kt, hl, :],
                                        start=(kt == 0 and qt in (0, 3)), stop=(kt == 15), skip_group_check=True),
                                        reads=[ek, "V1A"], writes=ACC_K)
                            if kp != 7:
                                return
                            a3 = ACC[:, 0, 0:387].rearrange("p (a b) -> p a b", a=3)
                            a1 = ACC[:, 1, 0:129]
                            rr = R0 if comp == 0 else R1
                            S.add("dve", lambda e, rr=rr, a3=a3: e.reciprocal(rr[:, 0:3], a3[:, :, 128]), reads=ACC_K, writes=["EPI"])
                            S.add("dve", lambda e, rr=rr, a1=a1: e.reciprocal(rr[:, 3:4], a1[:, 128:129]), reads=ACC_K, writes=["EPI"])
                            if comp == 0:
                                S.add("dve", lambda e, a3=a3: e.tensor_tensor(A0[:, 0:3, :], a3[:, :, 0:128],
                                                                          R0[:, 0:3].unsqueeze(2).broadcast_to([128, 3, 128]), ALU.mult),
                                      reads=ACC_K + ["EPI"], writes=["A0"])
                                S.add("dve", lambda e, a1=a1: e.tensor_scalar(A0[:, 3, :], a1[:, 0:128], R0[:, 3:4], None, ALU.mult),
                                      reads=ACC_K + ["EPI"], writes=["A0"])
                                return
                            S.add("dve", lambda e: e.tensor_scalar(R1[:, :], R1[:, :], NEGLAM, None, ALU.mult),
                                  reads=["EPI", "LAM"], writes=["EPI"])
                            for qt in range(4):
                                src_ = a3[:, qt, 0:128] if qt < 3 else a1[:, 0:128]
                                S.add("dve", lambda e, qt=qt, src_=src_: e.scalar_tensor_tensor(
                                    A0[:, qt, :], src_, R1[:, qt:qt + 1], A0[:, qt, :], ALU.mult, ALU.add),
                                    reads=ACC_K + ["EPI", "A0"], writes=["A0"])
                            for qt in range(4):
                                S.add("dve", lambda e, qt=qt: e.scalar_tensor_tensor(
                                    QN[:, 0:128], A0[:, qt, :], 1.0, A0[:, qt, :], ALU.mult, ALU.mult, accum_out=SSE[:, qt:qt + 1]),
                                    reads=["A0"], writes=["QN", ("SSE", qt)])
                            rsqrt_pool(RSE[:, :], SSE[:, :], 4, 1.0 / 128, [("SSE", q) for q in range(4)], ["RSE"])
                            for qt in range(4):
                                S.add("dve", lambda e, qt=qt: e.scalar_tensor_tensor(
                                    MB[:, qt, :], A0[:, qt, :], RSE[:, qt:qt + 1], SLGS[:], ALU.mult, ALU.mult),
                                    reads=["A0", "RSE", "SLGS"], writes=["MB"])
                            k = (i // 16) % 2
                            for qt in range(4):
                                S.add("pe", lambda e, qt=qt, k=k: e.transpose(TP[:, 4 * k + qt, :], MB[:, qt, :], IDENT[:]),
                                      reads=["MB", "IDENT"], writes=[("TP", k), "TPB"])
                            S.add("act", lambda e, qc=qc, h=h, k=k: e.copy(
                                MIXT[:, h, qc * 512:(qc + 1) * 512], TP[:, 4 * k:4 * k + 4, :].rearrange("p a b -> p (a b)")),
                                reads=[("TP", k)], writes=[mixkey(h), "TPB"])

                        a_score(0)
                        for i in range(len(aits)):
                            if i + 1 < len(aits):
                                a_score(i + 1)
                            a_av(i)
                else:
                    BQKG = HG[:, 0:768].rearrange("p (g a b) -> p g a b", g=3, a=4)
                    abanks = [(ACC, 0, ("ACC", 0)), (ACC, 1, ("ACC", 1)), (SCB, 0, ("SCB", 0))]
                    it = 0
                    for hp in range(4):
                        for g, (window, dil) in enumerate(B_GROUPS):
                            gb = g % 2
                            wo_ = gb * 384
                            for kind in range(3):
                                c0 = kind * 1536 + g * 512 + hp * 128
                                load_w(WIN[:, :, wo_ + kind * 128:wo_ + (kind + 1) * 128], wview[:, :, c0:c0 + 128], [("WIN", gb)])
                            L = SEQ // dil
                            nst = L // 128
                            pjb_b = [(PJ[:, :], "PJ"), (SCB[:, 1, :], ("SCB", 1))]
                            sqb_b = [(SCA[:, 0, :], ("SCA", 0)), (SCA[:, 1, :], ("SCA", 1))]

                            def bA(tj, g=g, gb=gb, dil=dil, nst=nst, wo_=wo_):
                                r, i0 = tj // nst, (tj % nst) * 128
                                lo = r + dil * i0
                                bank, bk = pjb_b[tj % 2]
                                for c in range(DC):
                                    S.add("pe", lambda e, c=c, lo=lo, bank=bank: e.matmul(
                                        bank[:, 0:384], HT[:, c, lo:lo + dil * 127 + 1:dil], WIN[:, c, wo_:wo_ + 384],
                                        start=(c == 0), stop=(c == DC - 1)),
                                        reads=["HTall", ("WIN", gb)], writes=[bk])
                                prepA(tj % 2, sqb_b[tj % 2][0], sqb_b[tj % 2][1], bank[:, 0:256].rearrange("p (a b) -> p a b", a=4), 4, [bk])
                                S.add("act", lambda e, tj=tj, bank=bank: e.copy(VB[:, gb, tj, :, 0:64],
                                                                             bank[:, 256:384].rearrange("p (a b) -> p a b", a=2)),
                                      reads=[bk], writes=[("V1A", gb)])

                            def bB(tj, g=g, gb=gb):
                                bank, bk = pjb_b[tj % 2]
                                k = tj % 2
                                prepB(k, bank[:, 0:256].rearrange("p (a b) -> p a b", a=4), 4, BQKG[:, g],
                                      (ROPE[:, g, 0, tj, :], ROPE[:, g, 1, tj, :]),
                                      [(0, 4, QKBs[k][:, 0:256].rearrange("p (a b) -> p a b", a=4), ident_v)], [bk], [("QKB", k)], "HG")
                                for c2 in range(2):
                                    S.add("pe", lambda e, c2=c2, k=k: e.transpose(TP[:, 4 * k + c2, :], QKBs[k][:, c2 * 128:(c2 + 1) * 128], IDENT[:]),
                                          reads=[("QKB", k), "IDENT"], writes=[("TP", k), "TPB"])
                                S.add("act", lambda e, tj=tj, k=k: e.copy(QTB[:, gb, tj * 128:(tj + 1) * 128], TP[:, 4 * k, :]),
                                      reads=[("TP", k)], writes=[("QTB", gb), "TPB"])
                                S.add("act", lambda e, tj=tj, k=k: e.copy(KTB[:, gb, tj * 128:(tj + 1) * 128], TP[:, 4 * k + 1, :]),
                                      reads=[("TP", k)], writes=[("KTB", gb), "TPB"])

                            skew(TT, bA, bB)
                            S.add("pool", lambda e, gb=gb: e.memset(VB[:, gb, :, :, 64:65], 1.0), writes=[("V1A", gb)])
                            bits = []
                            for hl in range(2):
                                started = set()
                                for tj in range(TT):
                                    seg, lj = tj // nst, tj % nst
                                    qlo, qhi = max(lj - 1, 0), min(lj + 1, nst - 1)
                                    firsts = []
                                    for qi in range(qlo, qhi + 1):
                                        slot = seg * nst + qi
                                        firsts.append((slot // 7) not in started)
                                        started.add(slot // 7)
                                    bits.append((hl, tj, seg, lj, qlo, qhi, firsts))

                            def b_score(i, gb=gb, nst=nst):
                                hl, tj, seg, lj, qlo, qhi, firsts = bits[i]
                                r0 = hl * 64
                                n = (qhi - qlo + 1) * 128
                                m0 = (qlo - lj + 1) * 128
                                q0 = (seg * nst + qlo) * 128
                                k2 = i % 2
                                et, ek = (ET0, "ET0") if i % 2 == 0 else (ET1, "ET1")
                                S.add("pe", lambda e: e.matmul(SCA[:, k2, 0:n], IDENT[:], MASK[:, m0:m0 + n], start=True, stop=False),
                                      reads=["IDENT", "MASK"], writes=[("SCA", k2)])
                                S.add("pe", lambda e: e.matmul(
                                    SCA[:, k2, 0:n], KTB[r0:r0 + 64, gb, tj * 128:(tj + 1) * 128], QTB[r0:r0 + 64, gb, q0:q0 + n],
                                    start=False, stop=True),
                                    reads=[("KTB", gb), ("QTB", gb)], writes=[("SCA", k2)])
                                S.add("act", lambda e: e.activation(et[:, 0, 0:n], SCA[:, k2, 0:n], AF.Exp, scale=0.125),
                                      reads=[("SCA", k2)], writes=[ek])

                            def b_av(i, g=g, gb=gb, nst=nst):
                                hl, tj, seg, lj, qlo, qhi, firsts = bits[i]
                                et, ek = (ET0, "ET0") if i % 2 == 0 else (ET1, "ET1")
                                for n_, qi in enumerate(range(qlo, qhi + 1)):
                                    slot = seg * nst + qi
                                    bt, bi, bk = abanks[slot // 7]
                                    off = (slot % 7) * 65
                                    first = firsts[n_]
                                    S.add("pe", lambda e, qi=qi, bt=bt, bi=bi, off=off, first=first: e.matmul(
                                        bt[:, bi, off:off + 65], et[:, 0, (qi - qlo) * 128:(qi - qlo + 1) * 128], VB[:, gb, tj, hl, :],
                                        start=first, stop=True, skip_group_check=True),
                                        reads=[ek, ("V1A", gb)], writes=[bk])
                                if tj != TT - 1:
                                    return
                                for b3 in range(3):
                                    bt, bi, bk = abanks[b3]
                                    ns = 7 if b3 < 2 else 2
                                    av = bt[:, bi, 0:ns * 65].rearrange("p (a b) -> p a b", a=ns)
                                    S.add("dve", lambda e, av=av, b3=b3, ns=ns: e.tensor_copy(
                                        NUMB[:, g, b3 * 7:b3 * 7 + ns, hl * 64:(hl + 1) * 64], av[:, :, 0:64]),
                                        reads=[bk], writes=["NUMB"])
                                    S.add("dve", lambda e, av=av, b3=b3, ns=ns: e.tensor_copy(
                                        DENF[:, g, b3 * 7:b3 * 7 + ns, hl], av[:, :, 64]),
                                        reads=[bk], writes=["DENF"])

                            b_score(0)
                            for i in range(len(bits)):
                                if i + 1 < len(bits):
                                    b_score(i + 1)
                                b_av(i)
                        for w in range(4):
                            mm = []
                            for jj in range(4):
                                mm.append((0, 4 * w + jj, slice(jj * 128, (jj + 1) * 128), slice(0, 128)))
                            for r in range(4):
                                mm.append((1, r * 4 + w, slice(r, 512, 4), slice(0, 128)))
                            for r in range(16):
                                mm.append((2, r, slice(r, 512, 16), slice(32 * w, 32 * w + 32)))
                            for n_, (g, tj, osl, isl) in enumerate(mm):
                                S.add("pe", lambda e, g=g, tj=tj, osl=osl, isl=isl, n_=n_: e.matmul(
                                    PJ[:, osl], NUMB[:, g, tj, :], IDENT[:, isl], start=(n_ == 0), stop=(n_ == len(mm) - 1),
                                    skip_group_check=True),
                                    reads=["NUMB", "IDENT"], writes=["PJ"])
                            for n_, (g, tj, osl, isl) in enumerate(mm):
                                S.add("pe", lambda e, g=g, tj=tj, osl=osl, isl=isl, n_=n_: e.matmul(
                                    SCB[0:2, 1, osl], DENF[:, g, tj, :], IDENTF[:, isl], start=(n_ == 0), stop=(n_ == len(mm) - 1),
                                    skip_group_check=True),
                                    reads=["DENF", "IDENTF"], writes=[("SCB", 1)])
                            S.add("dve", lambda e: e.reciprocal(RDEN[:, :], SCB[0:2, 1, :]), reads=[("SCB", 1)], writes=["RDEN"])
                            S.add("pe", lambda e: e.matmul(SCA[:, 0, :], SEL[:, :], RDEN[:, :], start=True, stop=True),
                                  reads=["SEL", "RDEN"], writes=[("SCA", 0)])
                            S.add("act", lambda e: e.copy(RDB[:, :], SCA[:, 0, :]), reads=[("SCA", 0)], writes=["RDB"])
                            S.add("dve", lambda e, w=w, hp=hp: e.tensor_tensor(MIXT[:, hp, w * 512:(w + 1) * 512], PJ[:, :], RDB[:, :], ALU.mult),
                                  reads=["PJ", "RDB"], writes=[("MIXT", hp)])

                wo_d = (awout_d if is_a else bwout_d)[j].rearrange("(c p) n -> p c n", p=128)
                load_w(WO[:, 0:nch, :], wo_d, "WO")
                for tt in range(TT):
                    for half in range(2):
                        for c in range(nch):
                            S.add("pe", lambda e, tt=tt, half=half, c=c, nch=nch: e.matmul(
                                ACC[:, half, :], MIXT[:, c, tt * 128:(tt + 1) * 128], WO[:, c, half * 512:(half + 1) * 512],
                                start=(c == 0), stop=(c == nch - 1)),
                                reads=[mixkey(c), "WO"], writes=[("ACC", half)])
                    S.add("dve", lambda e, tt=tt: e.tensor_tensor(X[:, tt, :], X[:, tt, :], ACC[:, :, :].rearrange("p a b -> p (a b)"), ALU.add),
                          reads=[("X", tt), ("ACC", 0), ("ACC", 1)], writes=[("X", tt)])

                S.dma(lambda e, li=li: e.dma_start(out=FGAIN, in_=gains_d[li, 1]), writes=["FGAIN"])
                norm_to_HT(X, TT, "FGAIN", [FHB0, FHB1], FJK, "FJK", ["FHB0", "FHB1"], HT, "HT", "X", FGAIN)
                mark_ht_ready()
                wu_d = wup_d[li].rearrange("(c p) n -> p c n", p=128)
                wd_d = wdown_d[li]
                WUs = [(WU0, "WU0"), (WU1, "WU1")]
                Gs = [(G0, "G0"), (G1, "G1")]
                S.add("pool", lambda e: e.memset(U[:, 0:1], 0.0), writes=["U"])
                S.add("pool", lambda e: e.memset(U[:, SEQ + 1:SEQ + 2], 0.0), writes=["U"])
                upbanks = [(PJ[:, :], "PJ"), (SCA[:, 0, :], ("SCA", 0)), (SCA[:, 1, :], ("SCA", 1))]
                dnbanks = [(ACC, ACC_K), (SCB, SCB_K)]
                ub = [0]
                db = [0]

                def load_up(gi):
                    fc0, n = FFN_GROUPS[gi]
                    wu, wk = WUs[gi % 2]
                    load_w(wu[:, :, 0, 0:n * 128], wu_d[:, :, fc0 * 128:(fc0 + n) * 128], wk)
                    load_w(wu[:, :, 1, 0:n * 128], wu_d[:, :, DFF + fc0 * 128:DFF + (fc0 + n) * 128], wk)

                def up(gi):
                    fc0, n = FFN_GROUPS[gi]
                    wu, wk = WUs[gi % 2]
                    gt, gk = Gs[gi % 2]
                    for l in range(n):
                        for ab in range(2):
                            ch = ab * NFC + fc0 + l
                            for tq in range(4):
                                bank, bkey = upbanks[ub[0] % 3]
                                ub[0] += 1
                                for c in range(DC):
                                    S.add("pe", lambda e, c=c, bank=bank, wu=wu, ab=ab, l=l, tq=tq: e.matmul(
                                        bank, wu[:, c, ab, l * 128:(l + 1) * 128], HT[:, c, tq * 512:(tq + 1) * 512],
                                        start=(c == 0), stop=(c == DC - 1)),
                                        reads=["HTall", wk], writes=[bkey])
                                S.add("act", lambda e, bank=bank, tq=tq: e.copy(U[:, 1 + tq * 512:1 + (tq + 1) * 512], bank),
                                      reads=[bkey], writes=["U"])
                            Cc, ck = (CA, "CA") if ab == 0 else (CB, "CB")
                            S.add("dve", lambda e, Cc=Cc, ch=ch: e.tensor_scalar(Cc[:, :], U[:, 1:SEQ + 1], CW[:, ch, 1:2], CW[:, ch, 3:4],
                                                                             ALU.mult, ALU.add),
                                  reads=["U", "CW"], writes=[ck])
                            S.add("dve", lambda e, Cc=Cc, ch=ch: e.scalar_tensor_tensor(Cc[:, :], U[:, 0:SEQ], CW[:, ch, 0:1], Cc[:, :],
                                                                                    ALU.mult, ALU.add),
                                  reads=["U", "CW", ck], writes=[ck])
                            S.add("dve", lambda e, Cc=Cc, ch=ch: e.scalar_tensor_tensor(Cc[:, :], U[:, 2:SEQ + 2], CW[:, ch, 2:3], Cc[:, :],
                                                                                    ALU.mult, ALU.add),
                                  reads=["U", "CW", ck], writes=[ck])
                            if ab == 0:
                                S.add("act", lambda e: e.activation(CA[:, :], CA[:, :], AF.Silu), reads=["CA"], writes=["CA"])
                            else:
                                S.add("pool", lambda e, gt=gt, l=l: e.tensor_tensor(gt[:, l, :], CA[:, :], CB[:, :], ALU.mult),
                                      reads=["CA", "CB"], writes=[gk])

                def load_down(gi):
                    fc0, n = FFN_GROUPS[gi]
                    load_w(WD[:, 0:n, :], wd_d[fc0 * 128:(fc0 + n) * 128, :].rearrange("(c p) n -> p c n", p=128), "WD")

                def down(gi):
                    fc0, n = FFN_GROUPS[gi]
                    gt, gk = Gs[gi % 2]
                    for tt in range(TT):
                        bt, bkey = dnbanks[db[0] % 2]
                        db[0] += 1
                        for half in range(2):
                            for l in range(n):
                                S.add("pe", lambda e, bt=bt, half=half, l=l, tt=tt, gt=gt: e.matmul(
                                    bt[:, half, :], gt[:, l, tt * 128:(tt + 1) * 128], WD[:, l, half * 512:(half + 1) * 512],
                                    start=(l == 0), stop=(l == n - 1)),
                                    reads=[gk, "WD"], writes=bkey)
                        S.add("dve", lambda e, tt=tt, bt=bt: e.tensor_tensor(X[:, tt, :], X[:, tt, :], bt[:, :, :].rearrange("p a b -> p (a b)"), ALU.add),
                              reads=[("X", tt)] + bkey, writes=[("X", tt)])

                ng = len(FFN_GROUPS)
                load_up(0)
                load_up(1)
                up(0)
                load_down(0)
                for gi in range(1, ng):
                    up(gi)
                    if gi + 1 < ng:
                        load_up(gi + 1)
                    down(gi - 1)
                    load_down(gi)
                down(ng - 1)
            for q4 in range(4):
                S.dma(lambda e, s=s, q4=q4: e.dma_start(
                    out=y_d[s, q4 * 512:(q4 + 1) * 512, :].rearrange("(t p) d -> p t d", p=128),
                    in_=X[:, q4 * 4:(q4 + 1) * 4, :]),
                    reads=[("X", t) for t in range(q4 * 4, q4 * 4 + 4)], is_output=True)
        S.emit()
    return nc


def _const_tables():
    rot = 16
    half = 8
    inv = (np.float32(500000.0) ** (-(np.arange(half, dtype=np.float32) * np.float32(2.0) / np.float32(rot)))).astype(np.float32)
    rope = np.zeros((3, 2, 128, TT, 16), np.float32)
    for g, (window, dil) in enumerate(B_GROUPS):
        L = SEQ // dil
        nst = L // 128
        for tj in range(TT):
            r, i0 = tj // nst, (tj % nst) * 128
            pos = (r + dil * (i0 + np.arange(128))).astype(np.float32)
            ang = (pos[:, None] * inv[None, :]).astype(np.float32)
            cs_, sn_ = np.cos(ang), np.sin(ang)
            rope[g, 0, :, tj, :] = np.concatenate([cs_, cs_], axis=1)
            rope[g, 1, :, tj, :] = np.concatenate([-sn_, sn_], axis=1)
    rope = rope.reshape(3, 2, 128, TT * 16)
    ident = np.eye(128, dtype=np.float32)
    k = np.arange(128)[:, None]
    c = np.arange(384)[None, :]
    rel = (c // 128 - 1) * 128 + (c % 128) - k
    mask = np.where(np.abs(rel) <= 64, 0.0, -30000.0).astype(np.float32)
    sel = np.zeros((2, 128), np.float32)
    sel[0, :64] = 1.0
    sel[1, 64:] = 1.0
    return rope, ident, mask, sel


def _prep_shared(inp):
    f = lambda a: np.ascontiguousarray(np.asarray(a, dtype=np.float32))
    gains = np.zeros((4, 3, 128, D), np.float32)
    hg = np.zeros((4, 128, 1664), np.float32)
    cw = np.zeros((4, 128, 44, 4), np.float32)
    for i in range(4):
        j = i // 2
        gains[i, 0] = np.broadcast_to(f(inp["norm_mix"])[i][None, :], (128, D))
        gains[i, 1] = np.broadcast_to(f(inp["norm_ffn"])[i][None, :], (128, D))
        gains[i, 2] = np.broadcast_to(f(inp["norm_mem"])[i][None, :], (128, D))
        row = np.zeros(1664, np.float32)
        if i % 2 == 0:
            qg = f(inp["a_q_norm"])[j]
            kg = f(inp["a_k_norm"])[j]
            row[0:512] = np.concatenate([qg] * 4 + [kg] * 4)
            row[1280:1408] = f(inp["a_subln"])[j]
            row[1408:1664] = f(inp["a_lambda"])[j].reshape(-1)
        else:
            for g in range(3):
                qg = f(inp["b_q_norm"])[j, g]
                kg = f(inp["b_k_norm"])[j, g]
                row[g * 256:(g + 1) * 256] = np.concatenate([qg, qg, kg, kg])
        row[768:1024] = np.tile(f(inp["xq_norm"])[i], 4)
        row[1024:1280] = np.tile(f(inp["xk_norm"])[i], 4)
        hg[i] = np.broadcast_to(row[None, :], (128, 1664))
        cwi = f(inp["conv_w"])[i].reshape(3, 44, 128)
        cbi = f(inp["conv_b"])[i].reshape(44, 128)
        cw[i, :, :, 0:3] = cwi.transpose(2, 1, 0)
        cw[i, :, :, 3] = cbi.T
    rope, ident, mask, sel = _const_tables()
    shared = {
        "w_mem_kv": f(inp["w_mem_kv"]), "a_w_in": f(inp["a_w_in"]), "a_w_out": f(inp["a_w_out"]),
        "b_w_in": f(inp["b_w_in"]), "b_w_out": f(inp["b_w_out"]), "w_up": f(inp["w_up"]), "w_down": f(inp["w_down"]),
        "gains": gains, "hg": hg, "cw": cw.reshape(4, 128, 176), "rope": rope, "ident": ident, "mask": mask, "sel": sel,
    }
    return shared


_PROGRAM_CACHE = {}


def _get_program(nseq, layers):
    key = (nseq, tuple(layers))
    if key not in _PROGRAM_CACHE:
        _PROGRAM_CACHE[key] = build_program(nseq, list(layers))
    return _PROGRAM_CACHE[key]


def kernel(**inp):
    xp = np.asarray(inp["x_prompt"], dtype=np.float32)
    xs = np.asarray(inp["x_sample"], dtype=np.float32)
    mp = np.asarray(inp["mem_prompt"], dtype=np.float32)
    ms = np.asarray(inp["mem_sample"], dtype=np.float32)
    x_all = np.concatenate([xp, xs], axis=0)
    m_all = np.concatenate([mp, ms], axis=0)
    nb = xp.shape[0]
    shared = _prep_shared(inp)
    nc = _get_program(SEQ_PER_CORE, (0, 1, 2, 3))
    in_maps = []
    for c in range(N_CORES):
        d = dict(shared)
        d["x"] = np.ascontiguousarray(x_all[c * SEQ_PER_CORE:(c + 1) * SEQ_PER_CORE])
        d["mem"] = np.ascontiguousarray(m_all[c * SEQ_PER_CORE:(c + 1) * SEQ_PER_CORE])
        in_maps.append(d)
    res = run_bass_kernel_spmd(nc, in_maps, core_ids=list(range(N_CORES)))
    y = np.concatenate([np.asarray(r["y"], dtype=np.float32) for r in res.results], axis=0)
    return (y[:nb], y[nb:])
```

```python
import contextlib
import numpy as np
import ml_dtypes
import concourse.bass as bass
import concourse.mybir as mybir
from concourse.bass_utils import run_bass_kernel_spmd

F32 = mybir.dt.float32
BF16 = mybir.dt.bfloat16
AF = mybir.ActivationFunctionType
ALU = mybir.AluOpType
AX = mybir.AxisListType

ENGS = ("pe", "act", "dve", "pool", "sp")
EPS = 1e-6


class Op:
    __slots__ = ("eng", "fn", "deps", "flag", "cnt", "idx", "dma", "dsem", "dval", "dprev")

    def __init__(self, eng, fn):
        self.eng = eng
        self.fn = fn
        self.deps = []
        self.flag = False
        self.cnt = 0
        self.idx = 0
        self.dma = False
        self.dsem = None
        self.dval = 0
        self.dprev = None


def _base(k):
    return k[0] if isinstance(k, tuple) else k


class Sched:
    def __init__(self, nc, n_dma_sems=32):
        self.nc = nc
        self.ops = {e: [] for e in ENGS}
        self.last_w = {}
        self.readers = {}
        self.by_base = {}
        self.alias = {}
        self.n_dma_sems = n_dma_sems
        self.dma_rr = 0
        self.dma_rr_sw = 0
        self.dma_cnt = [0] * n_dma_sems
        self.dma_last = [None] * n_dma_sems
        self.all_dma_out = []

    def set_alias(self, a, b):
        self.alias.setdefault(a, set()).add(b)
        self.alias.setdefault(b, set()).add(a)

    def _deps(self, op, reads, writes):
        deps = {}

        def add(d):
            if d is None or d is op:
                return
            if d.dma:
                deps[("dma", id(d))] = d
                return
            if d.eng == op.eng:
                if op.eng == "pe":
                    return
                if op.idx - d.idx > 2:
                    return
            k = d.eng
            if k not in deps or deps[k].idx < d.idx:
                deps[k] = d

        for r in reads:
            add(self.last_w.get(r))
            al = self.alias.get(_base(r))
            if al:
                for ab in al:
                    for k2 in self.by_base.get(ab, ()):
                        add(self.last_w.get(k2))
        for w in writes:
            add(self.last_w.get(w))
            for rd in self.readers.get(w, {}).values():
                add(rd)
            al = self.alias.get(_base(w))
            if al:
                for ab in al:
                    for k2 in self.by_base.get(ab, ()):
                        add(self.last_w.get(k2))
                        for rd in self.readers.get(k2, {}).values():
                            add(rd)
        rk = ("dma", id(op)) if op.dma else op.eng
        for r in reads:
            self.readers.setdefault(r, {})[rk] = op
            self.by_base.setdefault(_base(r), set()).add(r)
        for w in writes:
            self.last_w[w] = op
            self.readers[w] = {}
            self.by_base.setdefault(_base(w), set()).add(w)
        op.deps = list(deps.values())
        for d in op.deps:
            d.flag = True

    PSUM_BASES = ("SCA", "SCB", "ACC", "PJ", "TP")

    def add(self, eng, fn, reads=(), writes=()):
        op = Op(eng, fn)
        op.idx = len(self.ops[eng])
        rl = [("rl", r) for r in reads if _base(r) in self.PSUM_BASES]
        if rl:
            writes = list(writes) + rl
        self._deps(op, reads, writes)
        self.ops[eng].append(op)
        return op

    def dma(self, fn, reads=(), writes=(), queue="sp", is_output=False):
        op = Op(queue, fn)
        op.dma = True
        op.idx = len(self.ops[queue])
        half = self.n_dma_sems // 2
        if queue == "pool":
            s = half + self.dma_rr_sw
            self.dma_rr_sw = (self.dma_rr_sw + 1) % (self.n_dma_sems - half)
        else:
            s = self.dma_rr
            self.dma_rr = (self.dma_rr + 1) % half
        self.dma_cnt[s] += 1
        op.dsem = s
        op.dval = 16 * self.dma_cnt[s]
        op.dprev = self.dma_last[s]
        self.dma_last[s] = op
        self._deps(op, reads, writes)
        self.ops[queue].append(op)
        if is_output:
            self.all_dma_out.append(op)
        return op

    def emit(self):
        nc = self.nc
        for e in ENGS:
            c = 0
            for op in self.ops[e]:
                if op.dma:
                    continue
                if op.flag:
                    c += 1
                    op.cnt = c
        with contextlib.ExitStack() as st:
            esem = {e: st.enter_context(nc.semaphore("s_" + e)) for e in ENGS}
            dsem = [st.enter_context(nc.semaphore("d_%d" % i)) for i in range(self.n_dma_sems)]
            block = st.enter_context(nc.Block())

            def run(e, eng):
                seen = {}
                for op in self.ops[e]:
                    waits = []
                    for d in op.deps:
                        if d.dma:
                            key = ("d", d.dsem)
                            if seen.get(key, 0) < d.dval:
                                seen[key] = d.dval
                                waits.append((dsem[d.dsem], d.dval))
                        else:
                            key = ("e", d.eng)
                            if seen.get(key, 0) < d.cnt:
                                seen[key] = d.cnt
                                waits.append((esem[d.eng], d.cnt))
                    if op.dma and op.dprev is not None:
                        key = ("d", op.dsem)
                        if seen.get(key, 0) < op.dprev.dval:
                            seen[key] = op.dprev.dval
                            waits.append((dsem[op.dsem], op.dprev.dval))
                    for (s, v) in waits:
                        eng.wait_ge(s, v)
                    ins = op.fn(eng)
                    if op.dma:
                        ins.then_inc(dsem[op.dsem], 16)
                    elif op.flag:
                        ins.then_inc(esem[e], 1)
                if e == "sp":
                    fin = {}
                    for op in self.all_dma_out:
                        fin[op.dsem] = max(fin.get(op.dsem, 0), op.dval)
                    for s, v in fin.items():
                        eng.wait_ge(dsem[s], v)

            @block.tensor
            def _(t):
                run("pe", t)

            @block.scalar
            def _(a):
                run("act", a)

            @block.vector
            def _(v):
                run("dve", v)

            @block.gpsimd
            def _(g):
                run("pool", g)

            @block.sync
            def _(s):
                run("sp", s)


D = 1024
SEQ = 2048
TT = 16
DC = 8
NMEM = 256
DFF = 2816
NFC = 22
B_GROUPS = ((128, 1), (512, 4), (2048, 16))
N_CORES = 8
SEQ_PER_CORE = 5
FFN_GROUPS = [(0, 3), (3, 3), (6, 3), (9, 3), (12, 3), (15, 3), (18, 3), (21, 1)]


def build_program(nseq, layers):
    nc = bass.Bass("TRN2", target_bir_lowering=False)

    def din(name, shape, dt=F32):
        return nc.dram_tensor(name, list(shape), dt, kind="ExternalInput").ap()

    x_d = din("x", [nseq, SEQ, D])
    mem_d = din("mem", [nseq, NMEM, D])
    y_d = nc.dram_tensor("y", [nseq, SEQ, D], F32, kind="ExternalOutput").ap()
    wkv_d = din("w_mem_kv", [4, D, 512])
    awin_d = din("a_w_in", [2, D, 3328])
    awout_d = din("a_w_out", [2, 1280, D])
    bwin_d = din("b_w_in", [2, D, 4864])
    bwout_d = din("b_w_out", [2, 768, D])
    wup_d = din("w_up", [4, D, 2 * DFF])
    wdown_d = din("w_down", [4, DFF, D])
    gains_d = din("gains", [4, 3, 128, D])
    hg_d = din("hg", [4, 128, 1664])
    cw_d = din("cw", [4, 128, 44 * 4])
    rope_d = din("rope", [3, 2, 128, TT * 16])
    ident_d = din("ident", [128, 128])
    mask_d = din("mask", [128, 384])
    sel_d = din("sel", [2, 128])

    with contextlib.ExitStack() as st:
        def sb(name, shape, dt):
            return st.enter_context(nc.sbuf_tensor(name, list(shape), dt))

        def ps(name, shape, dt):
            return st.enter_context(nc.psum_tensor(name, list(shape), dt))

        S = Sched(nc)
        X = sb("X", [128, TT, D], F32)
        HT = sb("HT", [128, DC, SEQ], BF16)
        HG = sb("HG", [128, 1664], F32)
        CW = sb("CW", [128, 44, 4], F32)
        ROPE = sb("ROPE", [128, 3, 2, TT, 16], F32)
        IDENTF = sb("IDENTF", [128, 128], F32)
        IDENT = sb("IDENT", [128, 128], BF16)
        MASK = sb("MASK", [128, 384], BF16)
        SEL = sb("SEL", [2, 128], F32)
        MKT = sb("MKT", [128, 2, 256], BF16)
        MV1 = sb("MV1", [128, 2, 4, 65], BF16)
        SS = sb("SS", [128, 16], F32)
        RS = sb("RS", [128, 16], F32)
        NH = sb("NH", [128, 16], F32)
        ST8 = sb("ST8", [128, 8], F32)
        RT8 = sb("RT8", [128, 8], F32)
        ST8b = sb("ST8b", [128, 8], F32)
        RT8b = sb("RT8b", [128, 8], F32)
        LAM = sb("LAM", [128, 8], F32)
        SLGS = sb("SLGS", [128, 128], F32)
        ARENA_ELEMS = 46400
        AR = sb("AR", [128, ARENA_ELEMS], BF16)
        cursor = {"attn": 0, "ffn": 0}

        def carve(phase, nelem_bf16, shape, dt):
            off = cursor[phase]
            n = int(nelem_bf16)
            n = (n + 15) // 16 * 16
            cursor[phase] = off + n
            assert cursor[phase] <= ARENA_ELEMS, (phase, cursor[phase])
            v = AR[:, off:off + int(nelem_bf16)]
            if dt == F32:
                v = v.bitcast(F32)
            if len(shape) == 2:
                return v
            if len(shape) == 3:
                return v.rearrange("p (a b) -> p a b", a=shape[1])
            if len(shape) == 4:
                return v.rearrange("p (a b c) -> p a b c", a=shape[1], b=shape[2])
            if len(shape) == 5:
                return v.rearrange("p (a b c d) -> p a b c d", a=shape[1], b=shape[2], c=shape[3])
            raise ValueError

        def cb(phase, shape):
            return carve(phase, int(np.prod(shape[1:])), shape, BF16)

        def cf(phase, shape):
            return carve(phase, 2 * int(np.prod(shape[1:])), shape, F32)

        mixt_off = cursor["attn"]
        MIXT = cb("attn", [128, 10, SEQ])
        WIN = cb("attn", [128, DC, 768])
        QTB = cb("attn", [128, 2, SEQ])
        KTB = cb("attn", [128, 2, SEQ])
        v_off = cursor["attn"]
        _V = cb("attn", [128, 4160])
        V1A = AR[:, v_off:v_off + TT * 2 * 129].rearrange("p (a b c) -> p a b c", a=TT, b=2)
        VB = AR[:, v_off:v_off + 2 * TT * 2 * 65].rearrange("p (g a b c) -> p g a b c", g=2, a=TT, b=2)
        sq_off = cursor["attn"]
        SQ = cf("attn", [128, 512])
        QN = cf("attn", [128, 512])
        GAIN = AR[:, sq_off:sq_off + 2048].bitcast(F32)
        QKB = cb("attn", [128, 512])
        QKB2 = cb("attn", [128, 512])
        RA = cf("attn", [128, 128])
        RB = cf("attn", [128, 128])
        et_off = cursor["attn"]
        ET0 = cb("attn", [128, 2, 512])
        ET1 = cb("attn", [128, 2, 512])
        JK = AR[:, et_off:et_off + 1024]
        HB0 = AR[:, et_off + 1024:et_off + 2048]
        a0_off = cursor["attn"]
        A0 = cf("attn", [128, 4, 128])
        HB1 = AR[:, a0_off:a0_off + 1024]
        R0 = cf("attn", [128, 4])
        R1 = cf("attn", [128, 4])
        SSE = cf("attn", [128, 4])
        RSE = cf("attn", [128, 4])
        MB = cb("attn", [128, 4, 128])
        attn_end = cursor["attn"]
        nb_off = mixt_off + 6 * SEQ
        NUMB = AR[:, nb_off:nb_off + 3 * TT * 128].rearrange("p (g a b) -> p g a b", g=3, a=TT)
        df_off = nb_off + 3 * TT * 128
        DENF = AR[:, df_off:df_off + 2 * 3 * TT * 2].bitcast(F32).rearrange("p (g a b) -> p g a b", g=3, a=TT)
        RDEN = QN[0:2, :]
        RDB = SQ
        MEMX = AR[:, mixt_off:mixt_off + 2 * SEQ].bitcast(F32).rearrange("p (a b) -> p a b", a=2)
        WO_off = mixt_off + 10 * SEQ
        WO = AR[:, WO_off:WO_off + 10 * D].rearrange("p (a b) -> p a b", a=10)
        assert 10 * D <= DC * 768 + 2 * SEQ
        WU0 = cb("ffn", [128, DC, 2, 384])
        WU1 = cb("ffn", [128, DC, 2, 384])
        WD = cb("ffn", [128, 3, D])
        G0 = cb("ffn", [128, 3, SEQ])
        G1 = cb("ffn", [128, 3, SEQ])
        u_off = cursor["ffn"]
        U = cf("ffn", [128, SEQ + 2])
        FGAIN = AR[:, u_off:u_off + 2048].bitcast(F32)
        ca_off = cursor["ffn"]
        CA = cf("ffn", [128, SEQ])
        FJK = AR[:, ca_off:ca_off + 1024]
        cb_off = cursor["ffn"]
        CB = cf("ffn", [128, SEQ])
        FHB0 = AR[:, cb_off:cb_off + 1024]
        FHB1 = AR[:, cb_off + 1024:cb_off + 2048]
        ATTN_KEYS = ["MIXT", "MIXH", "WIN", "QTB", "KTB", "V1A", "SQ", "QN", "QKB", "ROPESCR", "ET0", "ET1", "EPI", "A0", "MB",
                     "JK", "HB0", "HB1", "NUMB", "DENF", "RDEN", "RDB", "MEMX", "WO", "GAIN", "SSE", "RSE"]
        FFN_KEYS = ["WU0", "WU1", "WD", "G0", "G1", "U", "CA", "CB", "FJK", "FHB0", "FHB1", "FGAIN"]
        for a_ in ATTN_KEYS:
            for b_ in FFN_KEYS:
                S.set_alias(a_, b_)
        for a_ in ("WIN", "QTB", "KTB"):
            S.set_alias("WO", a_)
        for a_, b_ in [("GAIN", "SQ"), ("GAIN", "QN"), ("JK", "ET0"), ("HB0", "ET1"), ("HB1", "A0"), ("NUMB", "MIXH"),
                       ("DENF", "MIXH"), ("MEMX", "MIXT"), ("RDEN", "QN"), ("RDB", "SQ"),
                       ("FGAIN", "U"), ("FJK", "CA"), ("FHB0", "CB"), ("FHB1", "CB")]:
            S.set_alias(a_, b_)

        def mixkey(c):
            return ("MIXT", c) if c < 6 else ("MIXH", c)

        SCA = ps("SCA", [128, 2, 512], F32)
        SCB = ps("SCB", [128, 2, 512], F32)
        ACC = ps("ACC", [128, 2, 512], F32)
        PJ = ps("PJ", [128, 512], F32)
        TP = ps("TP", [128, 8, 128], BF16)

        SCA_K = [("SCA", 0), ("SCA", 1)]
        SCB_K = [("SCB", 0), ("SCB", 1)]
        ACC_K = [("ACC", 0), ("ACC", 1)]
        S.dma(lambda e: e.dma_start(out=IDENTF[:], in_=ident_d), writes=["IDENTF"])
        S.dma(lambda e: nc.gpsimd.dma_start(out=MASK[:], in_=mask_d), writes=["MASK"], queue="pool")
        S.dma(lambda e: e.dma_start(out=SEL[:], in_=sel_d), writes=["SEL"])
        S.dma(lambda e: e.dma_start(out=ROPE[:].rearrange("p a b c d -> p a b (c d)"),
                                    in_=rope_d.rearrange("a b p n -> p a b n")), writes=["ROPE"])
        S.add("dve", lambda e: e.tensor_copy(IDENT[:], IDENTF[:]), reads=["IDENTF"], writes=["IDENT"])
        S.add("pool", lambda e: e.memset(NH[:], -0.5), writes=["NH"])
        S.add("pool", lambda e: e.memset(MV1[:], 1.0), writes=["MV1"])

        def rsqrt_pool(dst, src, n, scale, rkeys, wkeys):
            S.add("pool", lambda e: e.tensor_scalar(dst, src, scale, EPS, ALU.mult, ALU.add), reads=rkeys, writes=wkeys)
            S.add("pool", lambda e: e.tensor_tensor(dst, dst, NH[:, 0:n], ALU.pow), reads=wkeys + ["NH"], writes=wkeys)

        WIN_K = [("WIN", 0), ("WIN", 1)]

        def load_w(dst, src, wkeys):
            if not isinstance(wkeys, list):
                wkeys = [wkeys]
            S.dma(lambda e: nc.gpsimd.dma_start(out=dst, in_=src), writes=wkeys, queue="pool")

        def norm_to_HT(src3, ntiles, gain_key, hbs, jk, jkkey, hbkeys, dstT, dstkey, xkey, gain_ap):
            for tt in range(ntiles):
                S.add("act", lambda e, tt=tt: e.activation(jk, src3[:, tt, :], AF.Square, accum_out=SS[:, tt:tt + 1]),
                      reads=[(xkey, tt)], writes=[jkkey, ("SS", tt)])
            rsqrt_pool(RS[:, 0:ntiles], SS[:, 0:ntiles], ntiles, 1.0 / D, [("SS", t) for t in range(ntiles)], ["RS"])
            for tt in range(ntiles):
                hb = hbs[tt % 2]
                hk = hbkeys[tt % 2]
                S.add("dve", lambda e, tt=tt, hb=hb: e.scalar_tensor_tensor(hb, src3[:, tt, :], RS[:, tt:tt + 1], gain_ap,
                                                                        ALU.mult, ALU.mult),
                      reads=[(xkey, tt), "RS", gain_key], writes=[hk])
                for c in range(DC):
                    S.add("pe", lambda e, c=c, hb=hb: e.transpose(TP[:, c, :], hb[:, c * 128:(c + 1) * 128], IDENT[:]),
                          reads=[hk, "IDENT"], writes=[("TP", 0), ("TP", 1), "TPB"])
                S.add("act", lambda e, tt=tt: e.copy(dstT[:, :, tt * 128:(tt + 1) * 128], TP[:, :, :]),
                      reads=[("TP", 0), ("TP", 1)], writes=[(dstkey, tt), "TPB"] + (["HTall"] if dstkey == "HT" else []))

        STs = [ST8, ST8b]
        RTs = [RT8, RT8b]
        QKBs = [QKB, QKB2]
        TP_K = [("TP", 0), ("TP", 1)]

        def prepA(k, sqb, sqkey, psv, nh, rkeys):
            sq3 = sqb[:, 0:nh * 64].rearrange("p (a b) -> p a b", a=nh)
            S.add("act", lambda e: e.activation(sq3, psv, AF.Square), reads=rkeys, writes=[sqkey])
            S.add("dve", lambda e: e.tensor_reduce(STs[k][:, 0:nh], sq3, AX.X, ALU.add), reads=[sqkey], writes=[("ST8", k)])
            rsqrt_pool(RTs[k][:, 0:nh], STs[k][:, 0:nh], nh, 1.0 / 64, [("ST8", k)], [("RT8", k)])

        def prepB(k, psv, nh, gainv, cs, outs, rkeys, wkeys, gkey):
            n = nh * 64
            qn3 = QN[:, 0:n].rearrange("p (a b) -> p a b", a=nh)
            rt3 = RTs[k][:, 0:nh].unsqueeze(2).broadcast_to([128, nh, 64])
            S.add("dve", lambda e: e.tensor_tensor(qn3, psv, gainv, ALU.mult), reads=rkeys + [gkey], writes=["QN"])
            if cs is not None:
                ccv, ssv = cs
                cc3 = ccv.unsqueeze(1).broadcast_to([128, nh, 16])
                sa3 = ssv[:, 0:8].unsqueeze(1).broadcast_to([128, nh, 8])
                sb3 = ssv[:, 8:16].unsqueeze(1).broadcast_to([128, nh, 8])
                t = qn3[:, :, 0:16]
                ra = RA[:, 0:nh * 16].rearrange("p (a b) -> p a b", a=nh)
                rb = RB[:, 0:nh * 16].rearrange("p (a b) -> p a b", a=nh)
                S.add("dve", lambda e: e.tensor_tensor(ra, t, cc3, ALU.mult), reads=["QN", "ROPE"], writes=[("ROPESCR", 0)])
                S.add("dve", lambda e: e.tensor_tensor(rb[:, :, 0:8], qn3[:, :, 8:16], sa3, ALU.mult), reads=["QN", "ROPE"], writes=[("ROPESCR", 1)])
                S.add("dve", lambda e: e.tensor_tensor(rb[:, :, 8:16], qn3[:, :, 0:8], sb3, ALU.mult), reads=["QN", "ROPE"], writes=[("ROPESCR", 2)])
                S.add("dve", lambda e: e.tensor_tensor(t, ra, rb, ALU.add), reads=[("ROPESCR", 0), ("ROPESCR", 1), ("ROPESCR", 2)], writes=["QN"])
            for (h0, h1, oap, vf) in outs:
                S.add("dve", lambda e, h0=h0, h1=h1, oap=oap, vf=vf: e.tensor_tensor(oap, vf(qn3[:, h0:h1, :]), vf(rt3[:, h0:h1, :]), ALU.mult),
                      reads=["QN", ("RT8", k)], writes=wkeys)

        def skew(n, stageA, stageB):
            for t in range(n + 1):
                if t < n:
                    stageA(t)
                if t >= 1:
                    stageB(t - 1)

        ident_v = (lambda v: v)

        def proj_tok(bank, bkey, tok_ap_fn, wv, ncols, wkey):
            for c in range(DC):
                S.add("pe", lambda e, c=c: e.matmul(bank[:, 0:ncols], tok_ap_fn(c), wv[:, c, 0:ncols],
                                                    start=(c == 0), stop=(c == DC - 1)),
                      reads=["HTall"] + wkey, writes=[bkey])

        HT_KEYS = [("HT", t) for t in range(TT)]

        def mark_ht_ready():
            pass

        for s in range(nseq):
            for q4 in range(4):
                S.dma(lambda e, s=s, q4=q4: e.dma_start(
                    out=X[:, q4 * 4:(q4 + 1) * 4, :],
                    in_=x_d[s, q4 * 512:(q4 + 1) * 512, :].rearrange("(t p) d -> p t d", p=128)),
                    writes=[("X", t) for t in range(q4 * 4, q4 * 4 + 4)])
            for li in layers:
                j = li // 2
                is_a = (li % 2 == 0)
                nmix = 8 if is_a else 4
                nch = nmix + 2
                S.dma(lambda e, li=li: e.dma_start(out=HG[:], in_=hg_d[li]), writes=["HG"])
                S.dma(lambda e, li=li: e.dma_start(out=CW[:].rearrange("p a b -> p (a b)"), in_=cw_d[li]), writes=["CW"])
                XQG = HG[:, 768:1024].rearrange("p (a b) -> p a b", a=4)
                XKG = HG[:, 1024:1280].rearrange("p (a b) -> p a b", a=4)
                S.dma(lambda e, li=li: e.dma_start(out=GAIN, in_=gains_d[li, 2]), writes=["GAIN"])
                S.dma(lambda e, s=s: e.dma_start(out=MEMX, in_=mem_d[s].rearrange("(t p) d -> p t d", p=128)),
                      writes=[("MEMX", 0), ("MEMX", 1)])
                load_w(WIN[:, :, 0:512], wkv_d[li].rearrange("(c p) n -> p c n", p=128), WIN_K)
                MEMT = QTB[:, 0, 0:DC * 256].rearrange("p (a b) -> p a b", a=DC)
                norm_to_HT(MEMX, 2, "GAIN", [HB0, HB1], JK, "JK", ["HB0", "HB1"], MEMT, "QTB", "MEMX", GAIN)
                pjb = [(PJ[:, :], "PJ"), (ACC[:, 0, :], ("ACC", 0))]
                sqb = [(SCB[:, 0, :], ("SCB", 0)), (SCB[:, 1, :], ("SCB", 1))]

                def memA(mt):
                    bank, bk = pjb[mt % 2]
                    for c in range(DC):
                        S.add("pe", lambda e, c=c, mt=mt, bank=bank: e.matmul(bank, MEMT[:, c, mt * 128:(mt + 1) * 128], WIN[:, c, 0:512],
                                                                          start=(c == 0), stop=(c == DC - 1)),
                              reads=[("QTB", 0), ("QTB", 1)] + WIN_K, writes=[bk])
                    prepA(mt % 2, sqb[mt % 2][0], sqb[mt % 2][1], bank[:, 0:256].rearrange("p (a b) -> p a b", a=4), 4, [bk])
                    S.add("act", lambda e, mt=mt, bank=bank: e.copy(MV1[:, mt, :, 0:64], bank[:, 256:512].rearrange("p (a b) -> p a b", a=4)),
                          reads=[bk], writes=["MV1"])

                def memB(mt):
                    bank, bk = pjb[mt % 2]
                    k = mt % 2
                    prepB(k, bank[:, 0:256].rearrange("p (a b) -> p a b", a=4), 4, XKG, None,
                          [(0, 4, QKBs[k][:, 0:256].rearrange("p (a b) -> p a b", a=4), ident_v)], [bk], [("QKB", k)], "HG")
                    for c2 in range(2):
                        S.add("pe", lambda e, c2=c2, k=k: e.transpose(TP[:, 4 * k + c2, :], QKBs[k][:, c2 * 128:(c2 + 1) * 128], IDENT[:]),
                              reads=[("QKB", k), "IDENT"], writes=[("TP", k), "TPB"])
                    S.add("act", lambda e, mt=mt, k=k: e.copy(MKT[:, :, mt * 128:(mt + 1) * 128], TP[:, 4 * k:4 * k + 2, :]),
                          reads=[("TP", k)], writes=["MKT", "TPB"])

                skew(2, memA, memB)
                S.dma(lambda e, li=li: e.dma_start(out=GAIN, in_=gains_d[li, 0]), writes=["GAIN"])
                norm_to_HT(X, TT, "GAIN", [HB0, HB1], JK, "JK", ["HB0", "HB1"], HT, "HT", "X", GAIN)
                mark_ht_ready()

                win_d = awin_d if is_a else bwin_d
                xq_col0 = 3072 if is_a else 4608
                wview = win_d[j].rearrange("(c p) n -> p c n", p=128)
                load_w(WIN[:, :, 0:256], wview[:, :, xq_col0:xq_col0 + 256], WIN_K)
                XQT = [QTB[:, 0, :], QTB[:, 1, :]]
                XQK = [("QTB", 0), ("QTB", 1)]
                def xqA(tt):
                    bank, bk = pjb[tt % 2]
                    proj_tok(bank, bk, lambda c, tt=tt: HT[:, c, tt * 128:(tt + 1) * 128], WIN, 256, WIN_K)
                    prepA(tt % 2, sqb[tt % 2][0], sqb[tt % 2][1], bank[:, 0:256].rearrange("p (a b) -> p a b", a=4), 4, [bk])

                def xqB(tt):
                    bank, bk = pjb[tt % 2]
                    k = tt % 2
                    prepB(k, bank[:, 0:256].rearrange("p (a b) -> p a b", a=4), 4, XQG, None,
                          [(0, 4, QKBs[k][:, 0:256].rearrange("p (a b) -> p a b", a=4), ident_v)], [bk], [("QKB", k)], "HG")
                    for c2 in range(2):
                        S.add("pe", lambda e, c2=c2, k=k: e.transpose(TP[:, 4 * k + c2, :], QKBs[k][:, c2 * 128:(c2 + 1) * 128], IDENT[:]),
                              reads=[("QKB", k), "IDENT"], writes=[("TP", k), "TPB"])
                    for c2 in range(2):
                        S.add("act", lambda e, tt=tt, c2=c2, k=k: e.copy(XQT[c2][:, tt * 128:(tt + 1) * 128], TP[:, 4 * k + c2, :]),
                              reads=[("TP", k)], writes=[XQK[c2], "TPB"])

                skew(TT, xqA, xqB)
                xits = [(c2, qc, hl) for c2 in range(2) for qc in range(4) for hl in range(2)]

                def x_score(i):
                    c2, qc, hl = xits[i]
                    r0 = hl * 64
                    sc, sk = (SCA, SCA_K) if i % 2 == 0 else (SCB, SCB_K)
                    et, ek = (ET0, "ET0") if i % 2 == 0 else (ET1, "ET1")
                    for mt in range(2):
                        S.add("pe", lambda e, mt=mt, sc=sc, c2=c2, r0=r0, qc=qc: e.matmul(
                            sc[:, mt, :], MKT[r0:r0 + 64, c2, mt * 128:(mt + 1) * 128],
                            XQT[c2][r0:r0 + 64, qc * 512:(qc + 1) * 512], start=True, stop=True),
                            reads=["MKT", XQK[c2]], writes=sk)
                    S.add("act", lambda e, sc=sc, et=et: e.activation(et[:, :, :], sc[:, :, :], AF.Exp, scale=0.125),
                          reads=sk, writes=[ek])

                def x_av(i):
                    c2, qc, hl = xits[i]
                    h = 2 * c2 + hl
                    et, ek = (ET0, "ET0") if i % 2 == 0 else (ET1, "ET1")
                    for qt in range(4):
                        for mt in range(2):
                            S.add("pe", lambda e, qt=qt, mt=mt, et=et, h=h, hl=hl: e.matmul(
                                ACC[:, hl, qt * 65:(qt + 1) * 65], et[:, mt, qt * 128:(qt + 1) * 128], MV1[:, mt, h, :],
                                start=(qt == 0 and mt == 0), stop=(mt == 1), skip_group_check=True),
                                reads=[ek, "MV1"], writes=[("ACC", hl)])
                    accv = ACC[:, hl, 0:260].rearrange("p (a b) -> p a b", a=4)
                    rr = R0 if hl == 0 else R1
                    S.add("dve", lambda e, accv=accv, rr=rr: e.reciprocal(rr[:, :], accv[:, :, 64]),
                          reads=[("ACC", hl)], writes=[("EPI", hl)])
                    S.add("dve", lambda e, accv=accv, hl=hl, rr=rr: e.tensor_tensor(
                        MB[:, :, hl * 64:(hl + 1) * 64], accv[:, :, 0:64],
                        rr[:, :].unsqueeze(2).broadcast_to([128, 4, 64]), ALU.mult),
                        reads=[("ACC", hl), ("EPI", hl)], writes=["MB"])
                    if hl == 1:
                        k = (i // 2) % 2
                        for qt in range(4):
                            S.add("pe", lambda e, qt=qt, k=k: e.transpose(TP[:, 4 * k + qt, :], MB[:, qt, :], IDENT[:]),
                                  reads=["MB", "IDENT"], writes=[("TP", k), "TPB"])
                        S.add("act", lambda e, qc=qc, c2=c2, nmix=nmix, k=k: e.copy(
                            MIXT[:, nmix + c2, qc * 512:(qc + 1) * 512], TP[:, 4 * k:4 * k + 4, :].rearrange("p a b -> p (a b)")),
                            reads=[("TP", k)], writes=[mixkey(nmix + c2), "TPB"])

                x_score(0)
                for i in range(len(xits)):
                    if i + 1 < len(xits):
                        x_score(i + 1)
                    x_av(i)

                if is_a:
                    QKG = HG[:, 0:512].rearrange("p (a b) -> p a b", a=8)
                    lam_init = 0.8 - 0.6 * float(np.exp(-0.3 * li))
                    lp = HG[:, 1408:1664].rearrange("p (a b) -> p a b", a=4)
                    S.add("dve", lambda e: e.tensor_tensor(QN[:, 0:128].rearrange("p (a b) -> p a b", a=2), lp[:, 0:4:2, :], lp[:, 1:4:2, :], ALU.mult),
                          reads=["HG"], writes=["QN"])
                    S.add("dve", lambda e: e.tensor_reduce(LAM[:, 0:2], QN[:, 0:128].rearrange("p (a b) -> p a b", a=2), AX.X, ALU.add),
                          reads=["QN"], writes=["LAM"])
                    S.add("act", lambda e: e.activation(LAM[:, 2:4], LAM[:, 0:2], AF.Exp), reads=["LAM"], writes=["LAM"])
                    S.add("dve", lambda e: e.scalar_tensor_tensor(LAM[:, 4:5], LAM[:, 2:3], -1.0, LAM[:, 3:4], ALU.mult, ALU.add),
                          reads=["LAM"], writes=["LAM"])
                    S.add("dve", lambda e, lam_init=lam_init: e.tensor_scalar(LAM[:, 5:6], LAM[:, 4:5], -lam_init, None, ALU.add),
                          reads=["LAM"], writes=["LAM"])
                    NEGLAM = LAM[:, 5:6]
                    S.add("dve", lambda e, lam_init=lam_init: e.tensor_scalar(SLGS[:], HG[:, 1280:1408], 1.0 - lam_init, None, ALU.mult),
                          reads=["HG"], writes=["SLGS"])
                    def load_pair(hp_):
                        for seg, (c0, w) in enumerate([(128 * hp_, 128), (512 + 128 * hp_, 128), (1024 + 128 * hp_, 128),
                                                       (1536 + 128 * hp_, 128)]):
                            load_w(WIN[:, :, seg * 128:(seg + 1) * 128], wview[:, :, c0:c0 + w], WIN_K)
                        load_w(WIN[:, :, 512:768], wview[:, :, 2048 + 256 * hp_:2048 + 256 * hp_ + 256], WIN_K)

                    load_pair(0)
                    for hp in range(4):
                        vbk = [(SCA[:, 0, :], ("SCA", 0)), (SCA[:, 1, :], ("SCA", 1))]

                        def aA(tt):
                            bank, bk = pjb[tt % 2]
                            proj_tok(bank, bk, lambda c, tt=tt: HT[:, c, tt * 128:(tt + 1) * 128], WIN, 512, WIN_K)
                            prepA(tt % 2, sqb[tt % 2][0], sqb[tt % 2][1], bank[:, 0:512].rearrange("p (a b) -> p a b", a=8), 8, [bk])
                            vb_, vk_ = vbk[tt % 2]
                            for c in range(DC):
                                S.add("pe", lambda e, c=c, tt=tt, vb_=vb_: e.matmul(vb_[:, 0:256], HT[:, c, tt * 128:(tt + 1) * 128],
                                                                                WIN[:, c, 512:768], start=(c == 0), stop=(c == DC - 1)),
                                      reads=["HTall"] + WIN_K, writes=[vk_])
                            S.add("act", lambda e, tt=tt, vb_=vb_: e.copy(V1A[:, tt, :, 0:128], vb_[:, 0:256].rearrange("p (a b) -> p a b", a=2)),
                                  reads=[vk_], writes=["V1A"])

                        def aB(tt):
                            bank, bk = pjb[tt % 2]
                            k = tt % 2
                            vfa = (lambda v: v.rearrange("p (c h) d -> p c h d", c=2))
                            outs_a = [(0, 4, QKBs[k][:, 0:256].rearrange("p (h c d) -> p c h d", h=2, c=2), vfa),
                                      (4, 8, QKBs[k][:, 256:512].rearrange("p (h c d) -> p c h d", h=2, c=2), vfa)]
                            prepB(k, bank[:, 0:512].rearrange("p (a b) -> p a b", a=8), 8, QKG,
                                  (ROPE[:, 0, 0, tt, :], ROPE[:, 0, 1, tt, :]), outs_a, [bk], [("QKB", k)], "HG")
                            for c4 in range(4):
                                S.add("pe", lambda e, c4=c4, k=k: e.transpose(TP[:, 4 * k + c4, :], QKBs[k][:, c4 * 128:(c4 + 1) * 128], IDENT[:]),
                                      reads=[("QKB", k), "IDENT"], writes=[("TP", k), "TPB"])
                            S.add("act", lambda e, tt=tt, k=k: e.copy(QTB[:, 0:2, tt * 128:(tt + 1) * 128], TP[:, 4 * k:4 * k + 2, :]),
                                  reads=[("TP", k)], writes=[("QTB", 0), ("QTB", 1), "TPB"])
                            S.add("act", lambda e, tt=tt, k=k: e.copy(KTB[:, 0:2, tt * 128:(tt + 1) * 128], TP[:, 4 * k + 2:4 * k + 4, :]),
                                  reads=[("TP", k)], writes=[("KTB", 0), ("KTB", 1), "TPB"])

                        skew(TT, aA, aB)
                        if hp + 1 < 4:
                            load_pair(hp + 1)
                        if hp == 0:
                            S.add("pool", lambda e: e.memset(V1A[:, :, :, 128:129], 1.0), writes=["V1A"])
                        aits = [(hl, qc, comp, kp) for hl in range(2) for qc in range(4) for comp in range(2) for kp in range(8)]

                        def a_score(i):
                            hl, qc, comp, kp = aits[i]
                            r0 = comp * 64
                            sc, sk = (SCA, SCA_K) if i % 2 == 0 else (SCB, SCB_K)
                            et, ek = (ET0, "ET0") if i % 2 == 0 else (ET1, "ET1")
                            for k2 in range(2):
                                kt = 2 * kp + k2
                                S.add("pe", lambda e, sc=sc, k2=k2, kt=kt, r0=r0, hl=hl, qc=qc: e.matmul(
                                    sc[:, k2, :], KTB[r0:r0 + 64, hl, kt * 128:(kt + 1) * 128],
                                    QTB[r0:r0 + 64, hl, qc * 512:(qc + 1) * 512], start=True, stop=True),
                                    reads=[("KTB", hl), ("QTB", hl)], writes=sk)
                            S.add("act", lambda e, sc=sc, et=et: e.activation(et[:, :, :], sc[:, :, :], AF.Exp, scale=0.125),
                                  reads=sk, writes=[ek])

                        def a_av(i):
                            hl, qc, comp, kp = aits[i]
                            h = 2 * hp + hl
                            et, ek = (ET0, "ET0") if i % 2 == 0 else (ET1, "ET1")
                            for k2 in range(2):
                                kt = 2 * kp + k2
                                for qt in range(4):
                                    bank, off = (0, qt * 129) if qt < 3 else (1, 0)
                                    S.add("pe", lambda e, et=et, k2=k2, kt=kt, qt=qt, bank=bank, off=off, hl=hl: e.matmul(
                                        ACC[:, bank, off:off + 129], et[:, k2, qt * 128:(qt + 1) * 128], V1A[:, kt, hl, :],
                                        start=(kt == 0 and qt in (0, 3)), stop=(kt == 15), skip_group_check=True),
                                        reads=[ek, "V1A"], writes=ACC_K)
                            if kp != 7:
                                return
                            a3 = ACC[:, 0, 0:387].rearrange("p (a b) -> p a b", a=3)
                            a1 = ACC[:, 1, 0:129]
                            rr = R0 if comp == 0 else R1
                            S.add("dve", lambda e, rr=rr, a3=a3: e.reciprocal(rr[:, 0:3], a3[:, :, 128]), reads=ACC_K, writes=["EPI"])
                            S.add("dve", lambda e, rr=rr, a1=a1: e.reciprocal(rr[:, 3:4], a1[:, 128:129]), reads=ACC_K, writes=["EPI"])
                            if comp == 0:
                                S.add("dve", lambda e, a3=a3: e.tensor_tensor(A0[:, 0:3, :], a3[:, :, 0:128],
                                                                          R0[:, 0:3].unsqueeze(2).broadcast_to([128, 3, 128]), ALU.mult),
                                      reads=ACC_K + ["EPI"], writes=["A0"])
                                S.add("dve", lambda e, a1=a1: e.tensor_scalar(A0[:, 3, :], a1[:, 0:128], R0[:, 3:4], None, ALU.mult),
                                      reads=ACC_K + ["EPI"], writes=["A0"])
                                return
                            S.add("dve", lambda e: e.tensor_scalar(R1[:, :], R1[:, :], NEGLAM, None, ALU.mult),
                                  reads=["EPI", "LAM"], writes=["EPI"])
                            for qt in range(4):
                                src_ = a3[:, qt, 0:128] if qt < 3 else a1[:, 0:128]
                                S.add("dve", lambda e, qt=qt, src_=src_: e.scalar_tensor_tensor(
                                    A0[:, qt, :], src_, R1[:, qt:qt + 1], A0[:, qt, :], ALU.mult, ALU.add),
                                    reads=ACC_K + ["EPI", "A0"], writes=["A0"])
                            for qt in range(4):
                                S.add("dve", lambda e, qt=qt: e.scalar_tensor_tensor(
                                    QN[:, 0:128], A0[:, qt, :], 1.0, A0[:, qt, :], ALU.mult, ALU.mult, accum_out=SSE[:, qt:qt + 1]),
                                    reads=["A0"], writes=["QN", ("SSE", qt)])
                            rsqrt_pool(RSE[:, :], SSE[:, :], 4, 1.0 / 128, [("SSE", q) for q in range(4)], ["RSE"])
                            for qt in range(4):
                                S.add("dve", lambda e, qt=qt: e.scalar_tensor_tensor(
                                    MB[:, qt, :], A0[:, qt, :], RSE[:, qt:qt + 1], SLGS[:], ALU.mult, ALU.mult),
                                    reads=["A0", "RSE", "SLGS"], writes=["MB"])
                            k = (i // 16) % 2
                            for qt in range(4):
                                S.add("pe", lambda e, qt=qt, k=k: e.transpose(TP[:, 4 * k + qt, :], MB[:, qt, :], IDENT[:]),
                                      reads=["MB", "IDENT"], writes=[("TP", k), "TPB"])
                            S.add("act", lambda e, qc=qc, h=h, k=k: e.copy(
                                MIXT[:, h, qc * 512:(qc + 1) * 512], TP[:, 4 * k:4 * k + 4, :].rearrange("p a b -> p (a b)")),
                                reads=[("TP", k)], writes=[mixkey(h), "TPB"])

                        a_score(0)
                        for i in range(len(aits)):
                            if i + 1 < len(aits):
                                a_score(i + 1)
                            a_av(i)
                else:
                    BQKG = HG[:, 0:768].rearrange("p (g a b) -> p g a b", g=3, a=4)
                    abanks = [(ACC, 0, ("ACC", 0)), (ACC, 1, ("ACC", 1)), (SCB, 0, ("SCB", 0))]
                    it = 0
                    for hp in range(4):
                        for g, (window, dil) in enumerate(B_GROUPS):
                            gb = g % 2
                            wo_ = gb * 384
                            for kind in range(3):
                                c0 = kind * 1536 + g * 512 + hp * 128
                                load_w(WIN[:, :, wo_ + kind * 128:wo_ + (kind + 1) * 128], wview[:, :, c0:c0 + 128], [("WIN", gb)])
                            L = SEQ // dil
                            nst = L // 128
                            pjb_b = [(PJ[:, :], "PJ"), (SCB[:, 1, :], ("SCB", 1))]
                            sqb_b = [(SCA[:, 0, :], ("SCA", 0)), (SCA[:, 1, :], ("SCA", 1))]

                            def bA(tj, g=g, gb=gb, dil=dil, nst=nst, wo_=wo_):
                                r, i0 = tj // nst, (tj % nst) * 128
                                lo = r + dil * i0
                                bank, bk = pjb_b[tj % 2]
                                for c in range(DC):
                                    S.add("pe", lambda e, c=c, lo=lo, bank=bank: e.matmul(
                                        bank[:, 0:384], HT[:, c, lo:lo + dil * 127 + 1:dil], WIN[:, c, wo_:wo_ + 384],
                                        start=(c == 0), stop=(c == DC - 1)),
                                        reads=["HTall", ("WIN", gb)], writes=[bk])
                                prepA(tj % 2, sqb_b[tj % 2][0], sqb_b[tj % 2][1], bank[:, 0:256].rearrange("p (a b) -> p a b", a=4), 4, [bk])
                                S.add("act", lambda e, tj=tj, bank=bank: e.copy(VB[:, gb, tj, :, 0:64],
                                                                             bank[:, 256:384].rearrange("p (a b) -> p a b", a=2)),
                                      reads=[bk], writes=[("V1A", gb)])

                            def bB(tj, g=g, gb=gb):
                                bank, bk = pjb_b[tj % 2]
                                k = tj % 2
                                prepB(k, bank[:, 0:256].rearrange("p (a b) -> p a b", a=4), 4, BQKG[:, g],
                                      (ROPE[:, g, 0, tj, :], ROPE[:, g, 1, tj, :]),
                                      [(0, 4, QKBs[k][:, 0:256].rearrange("p (a b) -> p a b", a=4), ident_v)], [bk], [("QKB", k)], "HG")
                                for c2 in range(2):
                                    S.add("pe", lambda e, c2=c2, k=k: e.transpose(TP[:, 4 * k + c2, :], QKBs[k][:, c2 * 128:(c2 + 1) * 128], IDENT[:]),
                                          reads=[("QKB", k), "IDENT"], writes=[("TP", k), "TPB"])
                                S.add("act", lambda e, tj=tj, k=k: e.copy(QTB[:, gb, tj * 128:(tj + 1) * 128], TP[:, 4 * k, :]),
                                      reads=[("TP", k)], writes=[("QTB", gb), "TPB"])
                                S.add("act", lambda e, tj=tj, k=k: e.copy(KTB[:, gb, tj * 128:(tj + 1) * 128], TP[:, 4 * k + 1, :]),
                                      reads=[("TP", k)], writes=[("KTB", gb), "TPB"])

                            skew(TT, bA, bB)
                            S.add("pool", lambda e, gb=gb: e.memset(VB[:, gb, :, :, 64:65], 1.0), writes=[("V1A", gb)])
                            bits = []
                            for hl in range(2):
                                started = set()
                                for tj in range(TT):
                                    seg, lj = tj // nst, tj % nst
                                    qlo, qhi = max(lj - 1, 0), min(lj + 1, nst - 1)
                                    firsts = []
                                    for qi in range(qlo, qhi + 1):
                                        slot = seg * nst + qi
                                        firsts.append((slot // 7) not in started)
                                        started.add(slot // 7)
                                    bits.append((hl, tj, seg, lj, qlo, qhi, firsts))

                            def b_score(i, gb=gb, nst=nst):
                                hl, tj, seg, lj, qlo, qhi, firsts = bits[i]
                                r0 = hl * 64
                                n = (qhi - qlo + 1) * 128
                                m0 = (qlo - lj + 1) * 128
                                q0 = (seg * nst + qlo) * 128
                                k2 = i % 2
                                et, ek = (ET0, "ET0") if i % 2 == 0 else (ET1, "ET1")
                                S.add("pe", lambda e: e.matmul(SCA[:, k2, 0:n], IDENT[:], MASK[:, m0:m0 + n], start=True, stop=False),
                                      reads=["IDENT", "MASK"], writes=[("SCA", k2)])
                                S.add("pe", lambda e: e.matmul(
                                    SCA[:, k2, 0:n], KTB[r0:r0 + 64, gb, tj * 128:(tj + 1) * 128], QTB[r0:r0 + 64, gb, q0:q0 + n],
                                    start=False, stop=True),
                                    reads=[("KTB", gb), ("QTB", gb)], writes=[("SCA", k2)])
                                S.add("act", lambda e: e.activation(et[:, 0, 0:n], SCA[:, k2, 0:n], AF.Exp, scale=0.125),
                                      reads=[("SCA", k2)], writes=[ek])

                            def b_av(i, g=g, gb=gb, nst=nst):
                                hl, tj, seg, lj, qlo, qhi, firsts = bits[i]
                                et, ek = (ET0, "ET0") if i % 2 == 0 else (ET1, "ET1")
                                for n_, qi in enumerate(range(qlo, qhi + 1)):
                                    slot = seg * nst + qi
                                    bt, bi, bk = abanks[slot // 7]
                                    off = (slot % 7) * 65
                                    first = firsts[n_]
                                    S.add("pe", lambda e, qi=qi, bt=bt, bi=bi, off=off, first=first: e.matmul(
                                        bt[:, bi, off:off + 65], et[:, 0, (qi - qlo) * 128:(qi - qlo + 1) * 128], VB[:, gb, tj, hl, :],
                                        start=first, stop=True, skip_group_check=True),
                                        reads=[ek, ("V1A", gb)], writes=[bk])
                                if tj != TT - 1:
                                    return
                                for b3 in range(3):
                                    bt, bi, bk = abanks[b3]
                                    ns = 7 if b3 < 2 else 2
                                    av = bt[:, bi, 0:ns * 65].rearrange("p (a b) -> p a b", a=ns)
                                    S.add("dve", lambda e, av=av, b3=b3, ns=ns: e.tensor_copy(
                                        NUMB[:, g, b3 * 7:b3 * 7 + ns, hl * 64:(hl + 1) * 64], av[:, :, 0:64]),
                                        reads=[bk], writes=["NUMB"])
                                    S.add("dve", lambda e, av=av, b3=b3, ns=ns: e.tensor_copy(
                                        DENF[:, g, b3 * 7:b3 * 7 + ns, hl], av[:, :, 64]),
                                        reads=[bk], writes=["DENF"])

                            b_score(0)
                            for i in range(len(bits)):
                                if i + 1 < len(bits):
                                    b_score(i + 1)
                                b_av(i)
                        for w in range(4):
                            mm = []
                            for jj in range(4):
                                mm.append((0, 4 * w + jj, slice(jj * 128, (jj + 1) * 128), slice(0, 128)))
                            for r in range(4):
                                mm.append((1, r * 4 + w, slice(r, 512, 4), slice(0, 128)))
                            for r in range(16):
                                mm.append((2, r, slice(r, 512, 16), slice(32 * w, 32 * w + 32)))
                            for n_, (g, tj, osl, isl) in enumerate(mm):
                                S.add("pe", lambda e, g=g, tj=tj, osl=osl, isl=isl, n_=n_: e.matmul(
                                    PJ[:, osl], NUMB[:, g, tj, :], IDENT[:, isl], start=(n_ == 0), stop=(n_ == len(mm) - 1),
                                    skip_group_check=True),
                                    reads=["NUMB", "IDENT"], writes=["PJ"])
                            for n_, (g, tj, osl, isl) in enumerate(mm):
                                S.add("pe", lambda e, g=g, tj=tj, osl=osl, isl=isl, n_=n_: e.matmul(
                                    SCB[0:2, 1, osl], DENF[:, g, tj, :], IDENTF[:, isl], start=(n_ == 0), stop=(n_ == len(mm) - 1),
                                    skip_group_check=True),
                                    reads=["DENF", "IDENTF"], writes=[("SCB", 1)])
                            S.add("dve", lambda e: e.reciprocal(RDEN[:, :], SCB[0:2, 1, :]), reads=[("SCB", 1)], writes=["RDEN"])
                            S.add("pe", lambda e: e.matmul(SCA[:, 0, :], SEL[:, :], RDEN[:, :], start=True, stop=True),
                                  reads=["SEL", "RDEN"], writes=[("SCA", 0)])
                            S.add("act", lambda e: e.copy(RDB[:, :], SCA[:, 0, :]), reads=[("SCA", 0)], writes=["RDB"])
                            S.add("dve", lambda e, w=w, hp=hp: e.tensor_tensor(MIXT[:, hp, w * 512:(w + 1) * 512], PJ[:, :], RDB[:, :], ALU.mult),
                                  reads=["PJ", "RDB"], writes=[("MIXT", hp)])

                wo_d = (awout_d if is_a else bwout_d)[j].rearrange("(c p) n -> p c n", p=128)
                load_w(WO[:, 0:nch, :], wo_d, "WO")
                for tt in range(TT):
                    for half in range(2):
                        for c in range(nch):
                            S.add("pe", lambda e, tt=tt, half=half, c=c, nch=nch: e.matmul(
                                ACC[:, half, :], MIXT[:, c, tt * 128:(tt + 1) * 128], WO[:, c, half * 512:(half + 1) * 512],
                                start=(c == 0), stop=(c == nch - 1)),
                                reads=[mixkey(c), "WO"], writes=[("ACC", half)])
                    S.add("dve", lambda e, tt=tt: e.tensor_tensor(X[:, tt, :], X[:, tt, :], ACC[:, :, :].rearrange("p a b -> p (a b)"), ALU.add),
                          reads=[("X", tt), ("ACC", 0), ("ACC", 1)], writes=[("X", tt)])

                S.dma(lambda e, li=li: e.dma_start(out=FGAIN, in_=gains_d[li, 1]), writes=["FGAIN"])
                norm_to_HT(X, TT, "FGAIN", [FHB0, FHB1], FJK, "FJK", ["FHB0", "FHB1"], HT, "HT", "X", FGAIN)
                mark_ht_ready()
                wu_d = wup_d[li].rearrange("(c p) n -> p c n", p=128)
                wd_d = wdown_d[li]
                WUs = [(WU0, "WU0"), (WU1, "WU1")]
                Gs = [(G0, "G0"), (G1, "G1")]
                S.add("pool", lambda e: e.memset(U[:, 0:1], 0.0), writes=["U"])
                S.add("pool", lambda e: e.memset(U[:, SEQ + 1:SEQ + 2], 0.0), writes=["U"])
                upbanks = [(PJ[:, :], "PJ"), (SCA[:, 0, :], ("SCA", 0)), (SCA[:, 1, :], ("SCA", 1))]
                dnbanks = [(ACC, ACC_K), (SCB, SCB_K)]
                ub = [0]
                db = [0]

                def load_up(gi):
                    fc0, n = FFN_GROUPS[gi]
                    wu, wk = WUs[gi % 2]
                    load_w(wu[:, :, 0, 0:n * 128], wu_d[:, :, fc0 * 128:(fc0 + n) * 128], wk)
                    load_w(wu[:, :, 1, 0:n * 128], wu_d[:, :, DFF + fc0 * 128:DFF + (fc0 + n) * 128], wk)

                def up(gi):
                    fc0, n = FFN_GROUPS[gi]
                    wu, wk = WUs[gi % 2]
                    gt, gk = Gs[gi % 2]
                    for l in range(n):
                        for ab in range(2):
                            ch = ab * NFC + fc0 + l
                            for tq in range(4):
                                bank, bkey = upbanks[ub[0] % 3]
                                ub[0] += 1
                                for c in range(DC):
                                    S.add("pe", lambda e, c=c, bank=bank, wu=wu, ab=ab, l=l, tq=tq: e.matmul(
                                        bank, wu[:, c, ab, l * 128:(l + 1) * 128], HT[:, c, tq * 512:(tq + 1) * 512],
                                        start=(c == 0), stop=(c == DC - 1)),
                                        reads=["HTall", wk], writes=[bkey])
                                S.add("act", lambda e, bank=bank, tq=tq: e.copy(U[:, 1 + tq * 512:1 + (tq + 1) * 512], bank),
                                      reads=[bkey], writes=["U"])
                            Cc, ck = (CA, "CA") if ab == 0 else (CB, "CB")
                            S.add("dve", lambda e, Cc=Cc, ch=ch: e.tensor_scalar(Cc[:, :], U[:, 1:SEQ + 1], CW[:, ch, 1:2], CW[:, ch, 3:4],
                                                                             ALU.mult, ALU.add),
                                  reads=["U", "CW"], writes=[ck])
                            S.add("dve", lambda e, Cc=Cc, ch=ch: e.scalar_tensor_tensor(Cc[:, :], U[:, 0:SEQ], CW[:, ch, 0:1], Cc[:, :],
                                                                                    ALU.mult, ALU.add),
                                  reads=["U", "CW", ck], writes=[ck])
                            S.add("dve", lambda e, Cc=Cc, ch=ch: e.scalar_tensor_tensor(Cc[:, :], U[:, 2:SEQ + 2], CW[:, ch, 2:3], Cc[:, :],
                                                                                    ALU.mult, ALU.add),
                                  reads=["U", "CW", ck], writes=[ck])
                            if ab == 0:
                                S.add("act", lambda e: e.activation(CA[:, :], CA[:, :], AF.Silu), reads=["CA"], writes=["CA"])
                            else:
                                S.add("pool", lambda e, gt=gt, l=l: e.tensor_tensor(gt[:, l, :], CA[:, :], CB[:, :], ALU.mult),
                                      reads=["CA", "CB"], writes=[gk])

                def load_down(gi):
                    fc0, n = FFN_GROUPS[gi]
                    load_w(WD[:, 0:n, :], wd_d[fc0 * 128:(fc0 + n) * 128, :].rearrange("(c p) n -> p c n", p=128), "WD")

                def down(gi):
                    fc0, n = FFN_GROUPS[gi]
                    gt, gk = Gs[gi % 2]
                    for tt in range(TT):
                        bt, bkey = dnbanks[db[0] % 2]
                        db[0] += 1
                        for half in range(2):
                            for l in range(n):
                                S.add("pe", lambda e, bt=bt, half=half, l=l, tt=tt, gt=gt: e.matmul(
                                    bt[:, half, :], gt[:, l, tt * 128:(tt + 1) * 128], WD[:, l, half * 512:(half + 1) * 512],
                                    start=(l == 0), stop=(l == n - 1)),
                                    reads=[gk, "WD"], writes=bkey)
                        S.add("dve", lambda e, tt=tt, bt=bt: e.tensor_tensor(X[:, tt, :], X[:, tt, :], bt[:, :, :].rearrange("p a b -> p (a b)"), ALU.add),
                              reads=[("X", tt)] + bkey, writes=[("X", tt)])

                ng = len(FFN_GROUPS)
                load_up(0)
                load_up(1)
                up(0)
                load_down(0)
                for gi in range(1, ng):
                    up(gi)
                    if gi + 1 < ng:
                        load_up(gi + 1)
                    down(gi - 1)
                    load_down(gi)
                down(ng - 1)
            for q4 in range(4):
                S.dma(lambda e, s=s, q4=q4: e.dma_start(
                    out=y_d[s, q4 * 512:(q4 + 1) * 512, :].rearrange("(t p) d -> p t d", p=128),
                    in_=X[:, q4 * 4:(q4 + 1) * 4, :]),
                    reads=[("X", t) for t in range(q4 * 4, q4 * 4 + 4)], is_output=True)
        S.emit()
    return nc


def _const_tables():
    rot = 16
    half = 8
    inv = (np.float32(500000.0) ** (-(np.arange(half, dtype=np.float32) * np.float32(2.0) / np.float32(rot)))).astype(np.float32)
    rope = np.zeros((3, 2, 128, TT, 16), np.float32)
    for g, (window, dil) in enumerate(B_GROUPS):
        L = SEQ // dil
        nst = L // 128
        for tj in range(TT):
            r, i0 = tj // nst, (tj % nst) * 128
            pos = (r + dil * (i0 + np.arange(128))).astype(np.float32)
            ang = (pos[:, None] * inv[None, :]).astype(np.float32)
            cs_, sn_ = np.cos(ang), np.sin(ang)
            rope[g, 0, :, tj, :] = np.concatenate([cs_, cs_], axis=1)
            rope[g, 1, :, tj, :] = np.concatenate([-sn_, sn_], axis=1)
    rope = rope.reshape(3, 2, 128, TT * 16)
    ident = np.eye(128, dtype=np.float32)
    k = np.arange(128)[:, None]
    c = np.arange(384)[None, :]
    rel = (c // 128 - 1) * 128 + (c % 128) - k
    mask = np.where(np.abs(rel) <= 64, 0.0, -30000.0).astype(np.float32)
    sel = np.zeros((2, 128), np.float32)
    sel[0, :64] = 1.0
    sel[1, 64:] = 1.0
    return rope, ident, mask, sel


def _prep_shared(inp):
    f = lambda a: np.ascontiguousarray(np.asarray(a, dtype=np.float32))
    gains = np.zeros((4, 3, 128, D), np.float32)
    hg = np.zeros((4, 128, 1664), np.float32)
    cw = np.zeros((4, 128, 44, 4), np.float32)
    for i in range(4):
        j = i // 2
        gains[i, 0] = np.broadcast_to(f(inp["norm_mix"])[i][None, :], (128, D))
        gains[i, 1] = np.broadcast_to(f(inp["norm_ffn"])[i][None, :], (128, D))
        gains[i, 2] = np.broadcast_to(f(inp["norm_mem"])[i][None, :], (128, D))
        row = np.zeros(1664, np.float32)
        if i % 2 == 0:
            qg = f(inp["a_q_norm"])[j]
            kg = f(inp["a_k_norm"])[j]
            row[0:512] = np.concatenate([qg] * 4 + [kg] * 4)
            row[1280:1408] = f(inp["a_subln"])[j]
            row[1408:1664] = f(inp["a_lambda"])[j].reshape(-1)
        else:
            for g in range(3):
                qg = f(inp["b_q_norm"])[j, g]
                kg = f(inp["b_k_norm"])[j, g]
                row[g * 256:(g + 1) * 256] = np.concatenate([qg, qg, kg, kg])
        row[768:1024] = np.tile(f(inp["xq_norm"])[i], 4)
        row[1024:1280] = np.tile(f(inp["xk_norm"])[i], 4)
        hg[i] = np.broadcast_to(row[None, :], (128, 1664))
        cwi = f(inp["conv_w"])[i].reshape(3, 44, 128)
        cbi = f(inp["conv_b"])[i].reshape(44, 128)
        cw[i, :, :, 0:3] = cwi.transpose(2, 1, 0)
        cw[i, :, :, 3] = cbi.T
    rope, ident, mask, sel = _const_tables()
    shared = {
        "w_mem_kv": f(inp["w_mem_kv"]), "a_w_in": f(inp["a_w_in"]), "a_w_out": f(inp["a_w_out"]),
        "b_w_in": f(inp["b_w_in"]), "b_w_out": f(inp["b_w_out"]), "w_up": f(inp["w_up"]), "w_down": f(inp["w_down"]),
        "gains": gains, "hg": hg, "cw": cw.reshape(4, 128, 176), "rope": rope, "ident": ident, "mask": mask, "sel": sel,
    }
    return shared


_PROGRAM_CACHE = {}


def _get_program(nseq, layers):
    key = (nseq, tuple(layers))
    if key not in _PROGRAM_CACHE:
        _PROGRAM_CACHE[key] = build_program(nseq, list(layers))
    return _PROGRAM_CACHE[key]


def kernel(**inp):
    xp = np.asarray(inp["x_prompt"], dtype=np.float32)
    xs = np.asarray(inp["x_sample"], dtype=np.float32)
    mp = np.asarray(inp["mem_prompt"], dtype=np.float32)
    ms = np.asarray(inp["mem_sample"], dtype=np.float32)
    x_all = np.concatenate([xp, xs], axis=0)
    m_all = np.concatenate([mp, ms], axis=0)
    nb = xp.shape[0]
    shared = _prep_shared(inp)
    nc = _get_program(SEQ_PER_CORE, (0, 1, 2, 3))
    in_maps = []
    for c in range(N_CORES):
        d = dict(shared)
        d["x"] = np.ascontiguousarray(x_all[c * SEQ_PER_CORE:(c + 1) * SEQ_PER_CORE])
        d["mem"] = np.ascontiguousarray(m_all[c * SEQ_PER_CORE:(c + 1) * SEQ_PER_CORE])
        in_maps.append(d)
    res = run_bass_kernel_spmd(nc, in_maps, core_ids=list(range(N_CORES)))
    y = np.concatenate([np.asarray(r["y"], dtype=np.float32) for r in res.results], axis=0)
    return (y[:nb], y[nb:])
```

```python
import contextlib
import numpy as np
import ml_dtypes
import concourse.bass as bass
import concourse.mybir as mybir
from concourse.bass_utils import run_bass_kernel_spmd

F32 = mybir.dt.float32
BF16 = mybir.dt.bfloat16
AF = mybir.ActivationFunctionType
ALU = mybir.AluOpType
AX = mybir.AxisListType

ENGS = ("pe", "act", "dve", "pool", "sp")
EPS = 1e-6


class Op:
    __slots__ = ("eng", "fn", "deps", "flag", "cnt", "idx", "dma", "dsem", "dval", "dprev")

    def __init__(self, eng, fn):
        self.eng = eng
        self.fn = fn
        self.deps = []
        self.flag = False
        self.cnt = 0
        self.idx = 0
        self.dma = False
        self.dsem = None
        self.dval = 0
        self.dprev = None


def _base(k):
    return k[0] if isinstance(k, tuple) else k


class Sched:
    def __init__(self, nc, n_dma_sems=32):
        self.nc = nc
        self.ops = {e: [] for e in ENGS}
        self.last_w = {}
        self.readers = {}
        self.by_base = {}
        self.alias = {}
        self.n_dma_sems = n_dma_sems
        self.dma_rr = 0
        self.dma_rr_sw = 0
        self.dma_cnt = [0] * n_dma_sems
        self.dma_last = [None] * n_dma_sems
        self.all_dma_out = []

    def set_alias(self, a, b):
        self.alias.setdefault(a, set()).add(b)
        self.alias.setdefault(b, set()).add(a)

    def _deps(self, op, reads, writes):
        deps = {}

        def add(d):
            if d is None or d is op:
                return
            if d.dma:
                deps[("dma", id(d))] = d
                return
            if d.eng == op.eng:
                if op.eng == "pe":
                    return
                if op.idx - d.idx > 2:
                    return
            k = d.eng
            if k not in deps or deps[k].idx < d.idx:
                deps[k] = d

        for r in reads:
            add(self.last_w.get(r))
            al = self.alias.get(_base(r))
            if al:
                for ab in al:
                    for k2 in self.by_base.get(ab, ()):
                        add(self.last_w.get(k2))
        for w in writes:
            add(self.last_w.get(w))
            for rd in self.readers.get(w, {}).values():
                add(rd)
            al = self.alias.get(_base(w))
            if al:
                for ab in al:
                    for k2 in self.by_base.get(ab, ()):
                        add(self.last_w.get(k2))
                        for rd in self.readers.get(k2, {}).values():
                            add(rd)
        rk = ("dma", id(op)) if op.dma else op.eng
        for r in reads:
            self.readers.setdefault(r, {})[rk] = op
            self.by_base.setdefault(_base(r), set()).add(r)
        for w in writes:
            self.last_w[w] = op
            self.readers[w] = {}
            self.by_base.setdefault(_base(w), set()).add(w)
        op.deps = list(deps.values())
        for d in op.deps:
            d.flag = True

    PSUM_BASES = ("SCA", "SCB", "ACC", "PJ", "TP")

    def add(self, eng, fn, reads=(), writes=()):
        op = Op(eng, fn)
        op.idx = len(self.ops[eng])
        rl = [("rl", r) for r in reads if _base(r) in self.PSUM_BASES]
        if rl:
            writes = list(writes) + rl
        self._deps(op, reads, writes)
        self.ops[eng].append(op)
        return op

    def dma(self, fn, reads=(), writes=(), queue="sp", is_output=False):
        op = Op(queue, fn)
        op.dma = True
        op.idx = len(self.ops[queue])
        half = self.n_dma_sems // 2
        if queue == "pool":
            s = half + self.dma_rr_sw
            self.dma_rr_sw = (self.dma_rr_sw + 1) % (self.n_dma_sems - half)
        else:
            s = self.dma_rr
            self.dma_rr = (self.dma_rr + 1) % half
        self.dma_cnt[s] += 1
        op.dsem = s
        op.dval = 16 * self.dma_cnt[s]
        op.dprev = self.dma_last[s]
        self.dma_last[s] = op
        self._deps(op, reads, writes)
        self.ops[queue].append(op)
        if is_output:
            self.all_dma_out.append(op)
        return op

    def emit(self):
        nc = self.nc
        for e in ENGS:
            c = 0
            for op in self.ops[e]:
                if op.dma:
                    continue
                if op.flag:
                    c += 1
                    op.cnt = c
        with contextlib.ExitStack() as st:
            esem = {e: st.enter_context(nc.semaphore("s_" + e)) for e in ENGS}
            dsem = [st.enter_context(nc.semaphore("d_%d" % i)) for i in range(self.n_dma_sems)]
            block = st.enter_context(nc.Block())

            def run(e, eng):
                seen = {}
                for op in self.ops[e]:
                    waits = []
                    for d in op.deps:
                        if d.dma:
                            key = ("d", d.dsem)
                            if seen.get(key, 0) < d.dval:
                                seen[key] = d.dval
                                waits.append((dsem[d.dsem], d.dval))
                        else:
                            key = ("e", d.eng)
                            if seen.get(key, 0) < d.cnt:
                                seen[key] = d.cnt
                                waits.append((esem[d.eng], d.cnt))
                    if op.dma and op.dprev is not None:
                        key = ("d", op.dsem)
                        if seen.get(key, 0) < op.dprev.dval:
                            seen[key] = op.dprev.dval
                            waits.append((dsem[op.dsem], op.dprev.dval))
                    for (s, v) in waits:
                        eng.wait_ge(s, v)
                    ins = op.fn(eng)
                    if op.dma:
                        ins.then_inc(dsem[op.dsem], 16)
                    elif op.flag:
                        ins.then_inc(esem[e], 1)
                if e == "sp":
                    fin = {}
                    for op in self.all_dma_out:
                        fin[op.dsem] = max(fin.get(op.dsem, 0), op.dval)
                    for s, v in fin.items():
                        eng.wait_ge(dsem[s], v)

            @block.tensor
            def _(t):
                run("pe", t)

            @block.scalar
            def _(a):
                run("act", a)

            @block.vector
            def _(v):
                run("dve", v)

            @block.gpsimd
            def _(g):
                run("pool", g)

            @block.sync
            def _(s):
                run("sp", s)


D = 1024
SEQ = 2048
TT = 16
DC = 8
NMEM = 256
DFF = 2816
NFC = 22
B_GROUPS = ((128, 1), (512, 4), (2048, 16))
N_CORES = 8
SEQ_PER_CORE = 5
FFN_GROUPS = [(0, 3), (3, 3), (6, 3), (9, 3), (12, 3), (15, 3), (18, 3), (21, 1)]


def build_program(nseq, layers):
    nc = bass.Bass("TRN2", target_bir_lowering=False)

    def din(name, shape, dt=F32):
        return nc.dram_tensor(name, list(shape), dt, kind="ExternalInput").ap()

    x_d = din("x", [nseq, SEQ, D])
    mem_d = din("mem", [nseq, NMEM, D])
    y_d = nc.dram_tensor("y", [nseq, SEQ, D], F32, kind="ExternalOutput").ap()
    wkv_d = din("w_mem_kv", [4, D, 512])
    awin_d = din("a_w_in", [2, D, 3328])
    awout_d = din("a_w_out", [2, 1280, D])
    bwin_d = din("b_w_in", [2, D, 4864])
    bwout_d = din("b_w_out", [2, 768, D])
    wup_d = din("w_up", [4, D, 2 * DFF])
    wdown_d = din("w_down", [4, DFF, D])
    gains_d = din("gains", [4, 3, 128, D])
    hg_d = din("hg", [4, 128, 1664])
    cw_d = din("cw", [4, 128, 44 * 4])
    rope_d = din("rope", [3, 2, 128, TT * 16])
    ident_d = din("ident", [128, 128])
    mask_d = din("mask", [128, 384])
    sel_d = din("sel", [2, 128])

    with contextlib.ExitStack() as st:
        def sb(name, shape, dt):
            return st.enter_context(nc.sbuf_tensor(name, list(shape), dt))

        def ps(name, shape, dt):
            return st.enter_context(nc.psum_tensor(name, list(shape), dt))

        S = Sched(nc)
        X = sb("X", [128, TT, D], F32)
        HT = sb("HT", [128, DC, SEQ], BF16)
        HG = sb("HG", [128, 1664], F32)
        CW = sb("CW", [128, 44, 4], F32)
        ROPE = sb("ROPE", [128, 3, 2, TT, 16], F32)
        IDENTF = sb("IDENTF", [128, 128], F32)
        IDENT = sb("IDENT", [128, 128], BF16)
        MASK = sb("MASK", [128, 384], BF16)
        SEL = sb("SEL", [2, 128], F32)
        MKT = sb("MKT", [128, 2, 256], BF16)
        MV1 = sb("MV1", [128, 2, 4, 65], BF16)
        SS = sb("SS", [128, 16], F32)
        RS = sb("RS", [128, 16], F32)
        NH = sb("NH", [128, 16], F32)
        ST8 = sb("ST8", [128, 8], F32)
        RT8 = sb("RT8", [128, 8], F32)
        ST8b = sb("ST8b", [128, 8], F32)
        RT8b = sb("RT8b", [128, 8], F32)
        LAM = sb("LAM", [128, 8], F32)
        SLGS = sb("SLGS", [128, 128], F32)
        ARENA_ELEMS = 46400
        AR = sb("AR", [128, ARENA_ELEMS], BF16)
        cursor = {"attn": 0, "ffn": 0}

        def carve(phase, nelem_bf16, shape, dt):
            off = cursor[phase]
            n = int(nelem_bf16)
            n = (n + 15) // 16 * 16
            cursor[phase] = off + n
            assert cursor[phase] <= ARENA_ELEMS, (phase, cursor[phase])
            v = AR[:, off:off + int(nelem_bf16)]
            if dt == F32:
                v = v.bitcast(F32)
            if len(shape) == 2:
                return v
            if len(shape) == 3:
                return v.rearrange("p (a b) -> p a b", a=shape[1])
            if len(shape) == 4:
                return v.rearrange("p (a b c) -> p a b c", a=shape[1], b=shape[2])
            if len(shape) == 5:
                return v.rearrange("p (a b c d) -> p a b c d", a=shape[1], b=shape[2], c=shape[3])
            raise ValueError

        def cb(phase, shape):
            return carve(phase, int(np.prod(shape[1:])), shape, BF16)

        def cf(phase, shape):
            return carve(phase, 2 * int(np.prod(shape[1:])), shape, F32)

        mixt_off = cursor["attn"]
        MIXT = cb("attn", [128, 10, SEQ])
        WIN = cb("attn", [128, DC, 768])
        QTB = cb("attn", [128, 2, SEQ])
        KTB = cb("attn", [128, 2, SEQ])
        v_off = cursor["attn"]
        _V = cb("attn", [128, 4160])
        V1A = AR[:, v_off:v_off + TT * 2 * 129].rearrange("p (a b c) -> p a b c", a=TT, b=2)
        VB = AR[:, v_off:v_off + 2 * TT * 2 * 65].rearrange("p (g a b c) -> p g a b c", g=2, a=TT, b=2)
        sq_off = cursor["attn"]
        SQ = cf("attn", [128, 512])
        QN = cf("attn", [128, 512])
        GAIN = AR[:, sq_off:sq_off + 2048].bitcast(F32)
        QKB = cb("attn", [128, 512])
        QKB2 = cb("attn", [128, 512])
        RA = cf("attn", [128, 128])
        RB = cf("attn", [128, 128])
        et_off = cursor["attn"]
        ET0 = cb("attn", [128, 2, 512])
        ET1 = cb("attn", [128, 2, 512])
        JK = AR[:, et_off:et_off + 1024]
        HB0 = AR[:, et_off + 1024:et_off + 2048]
        a0_off = cursor["attn"]
        A0 = cf("attn", [128, 4, 128])
        HB1 = AR[:, a0_off:a0_off + 1024]
        R0 = cf("attn", [128, 4])
        R1 = cf("attn", [128, 4])
        SSE = cf("attn", [128, 4])
        RSE = cf("attn", [128, 4])
        MB = cb("attn", [128, 4, 128])
        attn_end = cursor["attn"]
        nb_off = mixt_off + 6 * SEQ
        NUMB = AR[:, nb_off:nb_off + 3 * TT * 128].rearrange("p (g a b) -> p g a b", g=3, a=TT)
        df_off = nb_off + 3 * TT * 128
        DENF = AR[:, df_off:df_off + 2 * 3 * TT * 2].bitcast(F32).rearrange("p (g a b) -> p g a b", g=3, a=TT)
        RDEN = QN[0:2, :]
        RDB = SQ
        MEMX = AR[:, mixt_off:mixt_off + 2 * SEQ].bitcast(F32).rearrange("p (a b) -> p a b", a=2)
        WO_off = mixt_off + 10 * SEQ
        WO = AR[:, WO_off:WO_off + 10 * D].rearrange("p (a b) -> p a b", a=10)
        assert 10 * D <= DC * 768 + 2 * SEQ
        WU0 = cb("ffn", [128, DC, 2, 384])
        WU1 = cb("ffn", [128, DC, 2, 384])
        WD = cb("ffn", [128, 3, D])
        G0 = cb("ffn", [128, 3, SEQ])
        G1 = cb("ffn", [128, 3, SEQ])
        u_off = cursor["ffn"]
        U = cf("ffn", [128, SEQ + 2])
        FGAIN = AR[:, u_off:u_off + 2048].bitcast(F32)
        ca_off = cursor["ffn"]
        CA = cf("ffn", [128, SEQ])
        FJK = AR[:, ca_off:ca_off + 1024]
        cb_off = cursor["ffn"]
        CB = cf("ffn", [128, SEQ])
        FHB0 = AR[:, cb_off:cb_off + 1024]
        FHB1 = AR[:, cb_off + 1024:cb_off + 2048]
        ATTN_KEYS = ["MIXT", "MIXH", "WIN", "QTB", "KTB", "V1A", "SQ", "QN", "QKB", "ROPESCR", "ET0", "ET1", "ETB", "EPI", "A0", "MB",
                     "JK", "HB0", "HB1", "NUMB", "DENF", "RDEN", "RDB", "MEMX", "WO", "GAIN", "SSE", "RSE"]
        FFN_KEYS = ["WU0", "WU1", "WD", "G0", "G1", "U", "CA", "CB", "FJK", "FHB0", "FHB1", "FGAIN"]
        for a_ in ATTN_KEYS:
            for b_ in FFN_KEYS:
                S.set_alias(a_, b_)
        for a_ in ("WIN", "QTB", "KTB"):
            S.set_alias("WO", a_)
        for a_, b_ in [("ETB", "ET0"), ("ETB", "ET1"), ("JK", "ETB"), ("HB0", "ETB"), ("GAIN", "SQ"), ("GAIN", "QN"), ("JK", "ET0"), ("HB0", "ET1"), ("HB1", "A0"), ("NUMB", "MIXH"),
                       ("DENF", "MIXH"), ("MEMX", "MIXT"), ("RDEN", "QN"), ("RDB", "SQ"),
                       ("FGAIN", "U"), ("FJK", "CA"), ("FHB0", "CB"), ("FHB1", "CB")]:
            S.set_alias(a_, b_)

        def mixkey(c):
            return ("MIXT", c) if c < 6 else ("MIXH", c)

        SCA = ps("SCA", [128, 2, 512], F32)
        SCB = ps("SCB", [128, 2, 512], F32)
        ACC = ps("ACC", [128, 2, 512], F32)
        PJ = ps("PJ", [128, 512], F32)
        TP = ps("TP", [128, 8, 128], BF16)

        SCA_K = [("SCA", 0), ("SCA", 1)]
        SCB_K = [("SCB", 0), ("SCB", 1)]
        ACC_K = [("ACC", 0), ("ACC", 1)]
        S.dma(lambda e: e.dma_start(out=IDENTF[:], in_=ident_d), writes=["IDENTF"])
        S.dma(lambda e: nc.gpsimd.dma_start(out=MASK[:], in_=mask_d), writes=["MASK"], queue="pool")
        S.dma(lambda e: e.dma_start(out=SEL[:], in_=sel_d), writes=["SEL"])
        S.dma(lambda e: e.dma_start(out=ROPE[:].rearrange("p a b c d -> p a b (c d)"),
                                    in_=rope_d.rearrange("a b p n -> p a b n")), writes=["ROPE"])
        S.add("dve", lambda e: e.tensor_copy(IDENT[:], IDENTF[:]), reads=["IDENTF"], writes=["IDENT"])
        S.add("pool", lambda e: e.memset(NH[:], -0.5), writes=["NH"])
        S.add("pool", lambda e: e.memset(MV1[:], 1.0), writes=["MV1"])

        def rsqrt_pool(dst, src, n, scale, rkeys, wkeys):
            S.add("pool", lambda e: e.tensor_scalar(dst, src, scale, EPS, ALU.mult, ALU.add), reads=rkeys, writes=wkeys)
            S.add("pool", lambda e: e.tensor_tensor(dst, dst, NH[:, 0:n], ALU.pow), reads=wkeys + ["NH"], writes=wkeys)

        WIN_K = [("WIN", 0), ("WIN", 1)]

        def load_w(dst, src, wkeys):
            if not isinstance(wkeys, list):
                wkeys = [wkeys]
            S.dma(lambda e: nc.gpsimd.dma_start(out=dst, in_=src), writes=wkeys, queue="pool")

        def norm_to_HT(src3, ntiles, gain_key, hbs, jk, jkkey, hbkeys, dstT, dstkey, xkey, gain_ap):
            for tt in range(ntiles):
                S.add("act", lambda e, tt=tt: e.activation(jk, src3[:, tt, :], AF.Square, accum_out=SS[:, tt:tt + 1]),
                      reads=[(xkey, tt)], writes=[jkkey, ("SS", tt)])
            rsqrt_pool(RS[:, 0:ntiles], SS[:, 0:ntiles], ntiles, 1.0 / D, [("SS", t) for t in range(ntiles)], ["RS"])
            for tt in range(ntiles):
                hb = hbs[tt % 2]
                hk = hbkeys[tt % 2]
                S.add("dve", lambda e, tt=tt, hb=hb: e.scalar_tensor_tensor(hb, src3[:, tt, :], RS[:, tt:tt + 1], gain_ap,
                                                                        ALU.mult, ALU.mult),
                      reads=[(xkey, tt), "RS", gain_key], writes=[hk])
                for c in range(DC):
                    S.add("pe", lambda e, c=c, hb=hb: e.transpose(TP[:, c, :], hb[:, c * 128:(c + 1) * 128], IDENT[:]),
                          reads=[hk, "IDENT"], writes=[("TP", 0), ("TP", 1), "TPB"])
                S.add("act", lambda e, tt=tt: e.copy(dstT[:, :, tt * 128:(tt + 1) * 128], TP[:, :, :]),
                      reads=[("TP", 0), ("TP", 1)], writes=[(dstkey, tt), "TPB"] + (["HTall"] if dstkey == "HT" else []))

        STs = [ST8, ST8b]
        RTs = [RT8, RT8b]
        QKBs = [QKB, QKB2]
        TP_K = [("TP", 0), ("TP", 1)]

        def prepA(k, sqb, sqkey, psv, nh, rkeys):
            sq3 = sqb[:, 0:nh * 64].rearrange("p (a b) -> p a b", a=nh)
            S.add("act", lambda e: e.activation(sq3, psv, AF.Square), reads=rkeys, writes=[sqkey])
            S.add("dve", lambda e: e.tensor_reduce(STs[k][:, 0:nh], sq3, AX.X, ALU.add), reads=[sqkey], writes=[("ST8", k)])
            rsqrt_pool(RTs[k][:, 0:nh], STs[k][:, 0:nh], nh, 1.0 / 64, [("ST8", k)], [("RT8", k)])

        def prepB(k, psv, nh, gainv, cs, outs, rkeys, wkeys, gkey):
            n = nh * 64
            qn3 = QN[:, 0:n].rearrange("p (a b) -> p a b", a=nh)
            rt3 = RTs[k][:, 0:nh].unsqueeze(2).broadcast_to([128, nh, 64])
            S.add("dve", lambda e: e.tensor_tensor(qn3, psv, gainv, ALU.mult), reads=rkeys + [gkey], writes=["QN"])
            if cs is not None:
                ccv, ssv = cs
                cc3 = ccv.unsqueeze(1).broadcast_to([128, nh, 16])
                sa3 = ssv[:, 0:8].unsqueeze(1).broadcast_to([128, nh, 8])
                sb3 = ssv[:, 8:16].unsqueeze(1).broadcast_to([128, nh, 8])
                t = qn3[:, :, 0:16]
                ra = RA[:, 0:nh * 16].rearrange("p (a b) -> p a b", a=nh)
                rb = RB[:, 0:nh * 16].rearrange("p (a b) -> p a b", a=nh)
                S.add("dve", lambda e: e.tensor_tensor(ra, t, cc3, ALU.mult), reads=["QN", "ROPE"], writes=[("ROPESCR", 0)])
                S.add("dve", lambda e: e.tensor_tensor(rb[:, :, 0:8], qn3[:, :, 8:16], sa3, ALU.mult), reads=["QN", "ROPE"], writes=[("ROPESCR", 1)])
                S.add("dve", lambda e: e.tensor_tensor(rb[:, :, 8:16], qn3[:, :, 0:8], sb3, ALU.mult), reads=["QN", "ROPE"], writes=[("ROPESCR", 2)])
                S.add("dve", lambda e: e.tensor_tensor(t, ra, rb, ALU.add), reads=[("ROPESCR", 0), ("ROPESCR", 1), ("ROPESCR", 2)], writes=["QN"])
            for (h0, h1, oap, vf) in outs:
                S.add("dve", lambda e, h0=h0, h1=h1, oap=oap, vf=vf: e.tensor_tensor(oap, vf(qn3[:, h0:h1, :]), vf(rt3[:, h0:h1, :]), ALU.mult),
                      reads=["QN", ("RT8", k)], writes=wkeys)

        def skew(n, stageA, stageB):
            for t in range(n + 1):
                if t < n:
                    stageA(t)
                if t >= 1:
                    stageB(t - 1)

        ident_v = (lambda v: v)

        def proj_tok(bank, bkey, tok_ap_fn, wv, ncols, wkey):
            for c in range(DC):
                S.add("pe", lambda e, c=c: e.matmul(bank[:, 0:ncols], tok_ap_fn(c), wv[:, c, 0:ncols],
                                                    start=(c == 0), stop=(c == DC - 1)),
                      reads=["HTall"] + wkey, writes=[bkey])

        HT_KEYS = [("HT", t) for t in range(TT)]

        def mark_ht_ready():
            pass

        for s in range(nseq):
            for q4 in range(4):
                S.dma(lambda e, s=s, q4=q4: e.dma_start(
                    out=X[:, q4 * 4:(q4 + 1) * 4, :],
                    in_=x_d[s, q4 * 512:(q4 + 1) * 512, :].rearrange("(t p) d -> p t d", p=128)),
                    writes=[("X", t) for t in range(q4 * 4, q4 * 4 + 4)])
            for li in layers:
                j = li // 2
                is_a = (li % 2 == 0)
                nmix = 8 if is_a else 4
                nch = nmix + 2
                S.dma(lambda e, li=li: e.dma_start(out=HG[:], in_=hg_d[li]), writes=["HG"])
                S.dma(lambda e, li=li: e.dma_start(out=CW[:].rearrange("p a b -> p (a b)"), in_=cw_d[li]), writes=["CW"])
                XQG = HG[:, 768:1024].rearrange("p (a b) -> p a b", a=4)
                XKG = HG[:, 1024:1280].rearrange("p (a b) -> p a b", a=4)
                S.dma(lambda e, li=li: e.dma_start(out=GAIN, in_=gains_d[li, 2]), writes=["GAIN"])
                S.dma(lambda e, s=s: e.dma_start(out=MEMX, in_=mem_d[s].rearrange("(t p) d -> p t d", p=128)),
                      writes=[("MEMX", 0), ("MEMX", 1)])
                load_w(WIN[:, :, 0:512], wkv_d[li].rearrange("(c p) n -> p c n", p=128), WIN_K)
                MEMT = QTB[:, 0, 0:DC * 256].rearrange("p (a b) -> p a b", a=DC)
                norm_to_HT(MEMX, 2, "GAIN", [HB0, HB1], JK, "JK", ["HB0", "HB1"], MEMT, "QTB", "MEMX", GAIN)
                pjb = [(PJ[:, :], "PJ"), (ACC[:, 0, :], ("ACC", 0))]
                sqb = [(SCB[:, 0, :], ("SCB", 0)), (SCB[:, 1, :], ("SCB", 1))]

                def memA(mt):
                    bank, bk = pjb[mt % 2]
                    for c in range(DC):
                        S.add("pe", lambda e, c=c, mt=mt, bank=bank: e.matmul(bank, MEMT[:, c, mt * 128:(mt + 1) * 128], WIN[:, c, 0:512],
                                                                          start=(c == 0), stop=(c == DC - 1)),
                              reads=[("QTB", 0), ("QTB", 1)] + WIN_K, writes=[bk])
                    prepA(mt % 2, sqb[mt % 2][0], sqb[mt % 2][1], bank[:, 0:256].rearrange("p (a b) -> p a b", a=4), 4, [bk])
                    S.add("act", lambda e, mt=mt, bank=bank: e.copy(MV1[:, mt, :, 0:64], bank[:, 256:512].rearrange("p (a b) -> p a b", a=4)),
                          reads=[bk], writes=["MV1"])

                def memB(mt):
                    bank, bk = pjb[mt % 2]
                    k = mt % 2
                    prepB(k, bank[:, 0:256].rearrange("p (a b) -> p a b", a=4), 4, XKG, None,
                          [(0, 4, QKBs[k][:, 0:256].rearrange("p (a b) -> p a b", a=4), ident_v)], [bk], [("QKB", k)], "HG")
                    for c2 in range(2):
                        S.add("pe", lambda e, c2=c2, k=k: e.transpose(TP[:, 4 * k + c2, :], QKBs[k][:, c2 * 128:(c2 + 1) * 128], IDENT[:]),
                              reads=[("QKB", k), "IDENT"], writes=[("TP", k), "TPB"])
                    S.add("act", lambda e, mt=mt, k=k: e.copy(MKT[:, :, mt * 128:(mt + 1) * 128], TP[:, 4 * k:4 * k + 2, :]),
                          reads=[("TP", k)], writes=["MKT", "TPB"])

                skew(2, memA, memB)
                S.dma(lambda e, li=li: e.dma_start(out=GAIN, in_=gains_d[li, 0]), writes=["GAIN"])
                norm_to_HT(X, TT, "GAIN", [HB0, HB1], JK, "JK", ["HB0", "HB1"], HT, "HT", "X", GAIN)
                mark_ht_ready()

                win_d = awin_d if is_a else bwin_d
                xq_col0 = 3072 if is_a else 4608
                wview = win_d[j].rearrange("(c p) n -> p c n", p=128)
                load_w(WIN[:, :, 0:256], wview[:, :, xq_col0:xq_col0 + 256], WIN_K)
                XQT = [QTB[:, 0, :], QTB[:, 1, :]]
                XQK = [("QTB", 0), ("QTB", 1)]
                def xqA(tt):
                    bank, bk = pjb[tt % 2]
                    proj_tok(bank, bk, lambda c, tt=tt: HT[:, c, tt * 128:(tt + 1) * 128], WIN, 256, WIN_K)
                    prepA(tt % 2, sqb[tt % 2][0], sqb[tt % 2][1], bank[:, 0:256].rearrange("p (a b) -> p a b", a=4), 4, [bk])

                def xqB(tt):
                    bank, bk = pjb[tt % 2]
                    k = tt % 2
                    prepB(k, bank[:, 0:256].rearrange("p (a b) -> p a b", a=4), 4, XQG, None,
                          [(0, 4, QKBs[k][:, 0:256].rearrange("p (a b) -> p a b", a=4), ident_v)], [bk], [("QKB", k)], "HG")
                    for c2 in range(2):
                        S.add("pe", lambda e, c2=c2, k=k: e.transpose(TP[:, 4 * k + c2, :], QKBs[k][:, c2 * 128:(c2 + 1) * 128], IDENT[:]),
                              reads=[("QKB", k), "IDENT"], writes=[("TP", k), "TPB"])
                    for c2 in range(2):
                        S.add("act", lambda e, tt=tt, c2=c2, k=k: e.copy(XQT[c2][:, tt * 128:(tt + 1) * 128], TP[:, 4 * k + c2, :]),
                              reads=[("TP", k)], writes=[XQK[c2], "TPB"])

                skew(TT, xqA, xqB)
                xits = [(c2, qc, hl) for c2 in range(2) for qc in range(4) for hl in range(2)]

                def x_score(i):
                    c2, qc, hl = xits[i]
                    r0 = hl * 64
                    sc, sk = (SCA, SCA_K) if i % 2 == 0 else (SCB, SCB_K)
                    et, ek = (ET0, "ET0") if i % 2 == 0 else (ET1, "ET1")
                    for mt in range(2):
                        S.add("pe", lambda e, mt=mt, sc=sc, c2=c2, r0=r0, qc=qc: e.matmul(
                            sc[:, mt, :], MKT[r0:r0 + 64, c2, mt * 128:(mt + 1) * 128],
                            XQT[c2][r0:r0 + 64, qc * 512:(qc + 1) * 512], start=True, stop=True),
                            reads=["MKT", XQK[c2]], writes=sk)
                    S.add("act", lambda e, sc=sc, et=et: e.activation(et[:, :, :], sc[:, :, :], AF.Exp, scale=0.125),
                          reads=sk, writes=[ek])

                def x_av(i):
                    c2, qc, hl = xits[i]
                    h = 2 * c2 + hl
                    et, ek = (ET0, "ET0") if i % 2 == 0 else (ET1, "ET1")
                    for qt in range(4):
                        for mt in range(2):
                            S.add("pe", lambda e, qt=qt, mt=mt, et=et, h=h, hl=hl: e.matmul(
                                ACC[:, hl, qt * 65:(qt + 1) * 65], et[:, mt, qt * 128:(qt + 1) * 128], MV1[:, mt, h, :],
                                start=(qt == 0 and mt == 0), stop=(mt == 1), skip_group_check=True),
                                reads=[ek, "MV1"], writes=[("ACC", hl)])
                    accv = ACC[:, hl, 0:260].rearrange("p (a b) -> p a b", a=4)
                    rr = R0 if hl == 0 else R1
                    S.add("dve", lambda e, accv=accv, rr=rr: e.reciprocal(rr[:, :], accv[:, :, 64]),
                          reads=[("ACC", hl)], writes=[("EPI", hl)])
                    S.add("dve", lambda e, accv=accv, hl=hl, rr=rr: e.tensor_tensor(
                        MB[:, :, hl * 64:(hl + 1) * 64], accv[:, :, 0:64],
                        rr[:, :].unsqueeze(2).broadcast_to([128, 4, 64]), ALU.mult),
                        reads=[("ACC", hl), ("EPI", hl)], writes=["MB"])
                    if hl == 1:
                        k = (i // 2) % 2
                        for qt in range(4):
                            S.add("pe", lambda e, qt=qt, k=k: e.transpose(TP[:, 4 * k + qt, :], MB[:, qt, :], IDENT[:]),
                                  reads=["MB", "IDENT"], writes=[("TP", k), "TPB"])
                        S.add("act", lambda e, qc=qc, c2=c2, nmix=nmix, k=k: e.copy(
                            MIXT[:, nmix + c2, qc * 512:(qc + 1) * 512], TP[:, 4 * k:4 * k + 4, :].rearrange("p a b -> p (a b)")),
                            reads=[("TP", k)], writes=[mixkey(nmix + c2), "TPB"])

                x_score(0)
                for i in range(len(xits)):
                    if i + 1 < len(xits):
                        x_score(i + 1)
                    x_av(i)

                if is_a:
                    QKG = HG[:, 0:512].rearrange("p (a b) -> p a b", a=8)
                    lam_init = 0.8 - 0.6 * float(np.exp(-0.3 * li))
                    lp = HG[:, 1408:1664].rearrange("p (a b) -> p a b", a=4)
                    S.add("dve", lambda e: e.tensor_tensor(QN[:, 0:128].rearrange("p (a b) -> p a b", a=2), lp[:, 0:4:2, :], lp[:, 1:4:2, :], ALU.mult),
                          reads=["HG"], writes=["QN"])
                    S.add("dve", lambda e: e.tensor_reduce(LAM[:, 0:2], QN[:, 0:128].rearrange("p (a b) -> p a b", a=2), AX.X, ALU.add),
                          reads=["QN"], writes=["LAM"])
                    S.add("act", lambda e: e.activation(LAM[:, 2:4], LAM[:, 0:2], AF.Exp), reads=["LAM"], writes=["LAM"])
                    S.add("dve", lambda e: e.scalar_tensor_tensor(LAM[:, 4:5], LAM[:, 2:3], -1.0, LAM[:, 3:4], ALU.mult, ALU.add),
                          reads=["LAM"], writes=["LAM"])
                    S.add("dve", lambda e, lam_init=lam_init: e.tensor_scalar(LAM[:, 5:6], LAM[:, 4:5], -lam_init, None, ALU.add),
                          reads=["LAM"], writes=["LAM"])
                    NEGLAM = LAM[:, 5:6]
                    S.add("dve", lambda e, lam_init=lam_init: e.tensor_scalar(SLGS[:], HG[:, 1280:1408], 1.0 - lam_init, None, ALU.mult),
                          reads=["HG"], writes=["SLGS"])
                    def load_pair(hp_):
                        for seg, (c0, w) in enumerate([(128 * hp_, 128), (512 + 128 * hp_, 128), (1024 + 128 * hp_, 128),
                                                       (1536 + 128 * hp_, 128)]):
                            load_w(WIN[:, :, seg * 128:(seg + 1) * 128], wview[:, :, c0:c0 + w], WIN_K)
                        load_w(WIN[:, :, 512:768], wview[:, :, 2048 + 256 * hp_:2048 + 256 * hp_ + 256], WIN_K)

                    load_pair(0)
                    for hp in range(4):
                        vbk = [(SCA[:, 0, :], ("SCA", 0)), (SCA[:, 1, :], ("SCA", 1))]

                        def aA(tt):
                            bank, bk = pjb[tt % 2]
                            proj_tok(bank, bk, lambda c, tt=tt: HT[:, c, tt * 128:(tt + 1) * 128], WIN, 512, WIN_K)
                            prepA(tt % 2, sqb[tt % 2][0], sqb[tt % 2][1], bank[:, 0:512].rearrange("p (a b) -> p a b", a=8), 8, [bk])
                            vb_, vk_ = vbk[tt % 2]
                            for c in range(DC):
                                S.add("pe", lambda e, c=c, tt=tt, vb_=vb_: e.matmul(vb_[:, 0:256], HT[:, c, tt * 128:(tt + 1) * 128],
                                                                                WIN[:, c, 512:768], start=(c == 0), stop=(c == DC - 1)),
                                      reads=["HTall"] + WIN_K, writes=[vk_])
                            S.add("act", lambda e, tt=tt, vb_=vb_: e.copy(V1A[:, tt, :, 0:128], vb_[:, 0:256].rearrange("p (a b) -> p a b", a=2)),
                                  reads=[vk_], writes=["V1A"])

                        def aB(tt):
                            bank, bk = pjb[tt % 2]
                            k = tt % 2
                            vfa = (lambda v: v.rearrange("p (c h) d -> p c h d", c=2))
                            outs_a = [(0, 4, QKBs[k][:, 0:256].rearrange("p (h c d) -> p c h d", h=2, c=2), vfa),
                                      (4, 8, QKBs[k][:, 256:512].rearrange("p (h c d) -> p c h d", h=2, c=2), vfa)]
                            prepB(k, bank[:, 0:512].rearrange("p (a b) -> p a b", a=8), 8, QKG,
                                  (ROPE[:, 0, 0, tt, :], ROPE[:, 0, 1, tt, :]), outs_a, [bk], [("QKB", k)], "HG")
                            for c4 in range(4):
                                S.add("pe", lambda e, c4=c4, k=k: e.transpose(TP[:, 4 * k + c4, :], QKBs[k][:, c4 * 128:(c4 + 1) * 128], IDENT[:]),
                                      reads=[("QKB", k), "IDENT"], writes=[("TP", k), "TPB"])
                            S.add("act", lambda e, tt=tt, k=k: e.copy(QTB[:, 0:2, tt * 128:(tt + 1) * 128], TP[:, 4 * k:4 * k + 2, :]),
                                  reads=[("TP", k)], writes=[("QTB", 0), ("QTB", 1), "TPB"])
                            S.add("act", lambda e, tt=tt, k=k: e.copy(KTB[:, 0:2, tt * 128:(tt + 1) * 128], TP[:, 4 * k + 2:4 * k + 4, :]),
                                  reads=[("TP", k)], writes=[("KTB", 0), ("KTB", 1), "TPB"])

                        skew(TT, aA, aB)
                        if hp + 1 < 4:
                            load_pair(hp + 1)
                        if hp == 0:
                            S.add("pool", lambda e: e.memset(V1A[:, :, :, 128:129], 1.0), writes=["V1A"])
                        aits = [(hl, qc, comp, kp) for hl in range(2) for qc in range(4) for comp in range(2) for kp in range(8)]

                        def a_score(i):
                            hl, qc, comp, kp = aits[i]
                            r0 = comp * 64
                            sc, sk = (SCA, SCA_K) if i % 2 == 0 else (SCB, SCB_K)
                            et, ek = (ET0, "ET0") if i % 2 == 0 else (ET1, "ET1")
                            for k2 in range(2):
                                kt = 2 * kp + k2
                                S.add("pe", lambda e, sc=sc, k2=k2, kt=kt, r0=r0, hl=hl, qc=qc: e.matmul(
                                    sc[:, k2, :], KTB[r0:r0 + 64, hl, kt * 128:(kt + 1) * 128],
                                    QTB[r0:r0 + 64, hl, qc * 512:(qc + 1) * 512], start=True, stop=True),
                                    reads=[("KTB", hl), ("QTB", hl)], writes=sk)
                            S.add("act", lambda e, sc=sc, et=et: e.activation(et[:, :, :], sc[:, :, :], AF.Exp, scale=0.125),
                                  reads=sk, writes=[ek])

                        def a_av(i):
                            hl, qc, comp, kp = aits[i]
                            h = 2 * hp + hl
                            et, ek = (ET0, "ET0") if i % 2 == 0 else (ET1, "ET1")
                            for k2 in range(2):
                                kt = 2 * kp + k2
                                for qt in range(4):
                                    bank, off = (0, qt * 129) if qt < 3 else (1, 0)
                                    S.add("pe", lambda e, et=et, k2=k2, kt=kt, qt=qt, bank=bank, off=off, hl=hl: e.matmul(
                                        ACC[:, bank, off:off + 129], et[:, k2, qt * 128:(qt + 1) * 128], V1A[:, kt, hl, :],
                                        start=(kt == 0 and qt in (0, 3)), stop=(kt == 15), skip_group_check=True),
                                        reads=[ek, "V1A"], writes=ACC_K)
                            if kp != 7:
                                return
                            a3 = ACC[:, 0, 0:387].rearrange("p (a b) -> p a b", a=3)
                            a1 = ACC[:, 1, 0:129]
                            rr = R0 if comp == 0 else R1
                            S.add("dve", lambda e, rr=rr, a3=a3: e.reciprocal(rr[:, 0:3], a3[:, :, 128]), reads=ACC_K, writes=["EPI"])
                            S.add("dve", lambda e, rr=rr, a1=a1: e.reciprocal(rr[:, 3:4], a1[:, 128:129]), reads=ACC_K, writes=["EPI"])
                            if comp == 0:
                                S.add("dve", lambda e, a3=a3: e.tensor_tensor(A0[:, 0:3, :], a3[:, :, 0:128],
                                                                          R0[:, 0:3].unsqueeze(2).broadcast_to([128, 3, 128]), ALU.mult),
                                      reads=ACC_K + ["EPI"], writes=["A0"])
                                S.add("dve", lambda e, a1=a1: e.tensor_scalar(A0[:, 3, :], a1[:, 0:128], R0[:, 3:4], None, ALU.mult),
                                      reads=ACC_K + ["EPI"], writes=["A0"])
                                return
                            S.add("dve", lambda e: e.tensor_scalar(R1[:, :], R1[:, :], NEGLAM, None, ALU.mult),
                                  reads=["EPI", "LAM"], writes=["EPI"])
                            for qt in range(4):
                                src_ = a3[:, qt, 0:128] if qt < 3 else a1[:, 0:128]
                                S.add("dve", lambda e, qt=qt, src_=src_: e.scalar_tensor_tensor(
                                    A0[:, qt, :], src_, R1[:, qt:qt + 1], A0[:, qt, :], ALU.mult, ALU.add),
                                    reads=ACC_K + ["EPI", "A0"], writes=["A0"])
                            for qt in range(4):
                                S.add("dve", lambda e, qt=qt: e.scalar_tensor_tensor(
                                    QN[:, 0:128], A0[:, qt, :], 1.0, A0[:, qt, :], ALU.mult, ALU.mult, accum_out=SSE[:, qt:qt + 1]),
                                    reads=["A0"], writes=["QN", ("SSE", qt)])
                            rsqrt_pool(RSE[:, :], SSE[:, :], 4, 1.0 / 128, [("SSE", q) for q in range(4)], ["RSE"])
                            for qt in range(4):
                                S.add("dve", lambda e, qt=qt: e.scalar_tensor_tensor(
                                    MB[:, qt, :], A0[:, qt, :], RSE[:, qt:qt + 1], SLGS[:], ALU.mult, ALU.mult),
                                    reads=["A0", "RSE", "SLGS"], writes=["MB"])
                            k = (i // 16) % 2
                            for qt in range(4):
                                S.add("pe", lambda e, qt=qt, k=k: e.transpose(TP[:, 4 * k + qt, :], MB[:, qt, :], IDENT[:]),
                                      reads=["MB", "IDENT"], writes=[("TP", k), "TPB"])
                            S.add("act", lambda e, qc=qc, h=h, k=k: e.copy(
                                MIXT[:, h, qc * 512:(qc + 1) * 512], TP[:, 4 * k:4 * k + 4, :].rearrange("p a b -> p (a b)")),
                                reads=[("TP", k)], writes=[mixkey(h), "TPB"])

                        a_score(0)
                        for i in range(len(aits)):
                            if i + 1 < len(aits):
                                a_score(i + 1)
                            a_av(i)
                else:
                    BQKG = HG[:, 0:768].rearrange("p (g a b) -> p g a b", g=3, a=4)
                    abanks = [(ACC, 0, ("ACC", 0)), (ACC, 1, ("ACC", 1)), (SCB, 0, ("SCB", 0))]
                    it = 0
                    for hp in range(4):
                        for g, (window, dil) in enumerate(B_GROUPS):
                            gb = g % 2
                            wo_ = gb * 384
                            for kind in range(3):
                                c0 = kind * 1536 + g * 512 + hp * 128
                                load_w(WIN[:, :, wo_ + kind * 128:wo_ + (kind + 1) * 128], wview[:, :, c0:c0 + 128], [("WIN", gb)])
                            L = SEQ // dil
                            nst = L // 128
                            pjb_b = [(PJ[:, :], "PJ"), (SCB[:, 1, :], ("SCB", 1))]
                            sqb_b = [(SCA[:, 0, :], ("SCA", 0)), (SCA[:, 1, :], ("SCA", 1))]

                            def bA(tj, g=g, gb=gb, dil=dil, nst=nst, wo_=wo_):
                                r, i0 = tj // nst, (tj % nst) * 128
                                lo = r + dil * i0
                                bank, bk = pjb_b[tj % 2]
                                for c in range(DC):
                                    S.add("pe", lambda e, c=c, lo=lo, bank=bank: e.matmul(
                                        bank[:, 0:384], HT[:, c, lo:lo + dil * 127 + 1:dil], WIN[:, c, wo_:wo_ + 384],
                                        start=(c == 0), stop=(c == DC - 1)),
                                        reads=["HTall", ("WIN", gb)], writes=[bk])
                                prepA(tj % 2, sqb_b[tj % 2][0], sqb_b[tj % 2][1], bank[:, 0:256].rearrange("p (a b) -> p a b", a=4), 4, [bk])
                                S.add("act", lambda e, tj=tj, bank=bank: e.copy(VB[:, gb, tj, :, 0:64],
                                                                             bank[:, 256:384].rearrange("p (a b) -> p a b", a=2)),
                                      reads=[bk], writes=[("V1A", gb)])

                            def bB(tj, g=g, gb=gb):
                                bank, bk = pjb_b[tj % 2]
                                k = tj % 2
                                prepB(k, bank[:, 0:256].rearrange("p (a b) -> p a b", a=4), 4, BQKG[:, g],
                                      (ROPE[:, g, 0, tj, :], ROPE[:, g, 1, tj, :]),
                                      [(0, 4, QKBs[k][:, 0:256].rearrange("p (a b) -> p a b", a=4), ident_v)], [bk], [("QKB", k)], "HG")
                                for c2 in range(2):
                                    S.add("pe", lambda e, c2=c2, k=k: e.transpose(TP[:, 4 * k + c2, :], QKBs[k][:, c2 * 128:(c2 + 1) * 128], IDENT[:]),
                                          reads=[("QKB", k), "IDENT"], writes=[("TP", k), "TPB"])
                                S.add("act", lambda e, tj=tj, k=k: e.copy(QTB[:, gb, tj * 128:(tj + 1) * 128], TP[:, 4 * k, :]),
                                      reads=[("TP", k)], writes=[("QTB", gb), "TPB"])
                                S.add("act", lambda e, tj=tj, k=k: e.copy(KTB[:, gb, tj * 128:(tj + 1) * 128], TP[:, 4 * k + 1, :]),
                                      reads=[("TP", k)], writes=[("KTB", gb), "TPB"])

                            skew(TT, bA, bB)
                            S.add("pool", lambda e, gb=gb: e.memset(VB[:, gb, :, :, 64:65], 1.0), writes=[("V1A", gb)])
                            bits = []
                            for hl in range(2):
                                started = set()
                                for tj in range(TT):
                                    seg, lj = tj // nst, tj % nst
                                    qlo, qhi = max(lj - 1, 0), min(lj + 1, nst - 1)
                                    firsts = []
                                    for qi in range(qlo, qhi + 1):
                                        slot = seg * nst + qi
                                        firsts.append((slot // 7) not in started)
                                        started.add(slot // 7)
                                    bits.append((hl, tj, seg, lj, qlo, qhi, firsts))

                            etb = [(ET0[:, 0, :], ("ETB", 0)), (ET0[:, 1, :], ("ETB", 1)), (ET1[:, 0, :], ("ETB", 2)), (ET1[:, 1, :], ("ETB", 3))]

                            def b_score(i, gb=gb, nst=nst):
                                hl, tj, seg, lj, qlo, qhi, firsts = bits[i]
                                r0 = hl * 64
                                n = (qhi - qlo + 1) * 128
                                m0 = (qlo - lj + 1) * 128
                                q0 = (seg * nst + qlo) * 128
                                k2 = i % 2
                                et, ek = etb[i % 4]
                                S.add("pe", lambda e: e.matmul(
                                    SCA[:, k2, 0:n], KTB[r0:r0 + 64, gb, tj * 128:(tj + 1) * 128], QTB[r0:r0 + 64, gb, q0:q0 + n],
                                    start=True, stop=True),
                                    reads=[("KTB", gb), ("QTB", gb)], writes=[("SCA", k2)])
                                S.add("act", lambda e: e.activation(et[:, 0:n], SCA[:, k2, 0:n], AF.Exp, scale=0.125),
                                      reads=[("SCA", k2)], writes=[ek])
                                S.add("dve", lambda e: e.tensor_tensor(et[:, 0:n], et[:, 0:n], MASK[:, m0:m0 + n], ALU.mult),
                                      reads=[ek, "MASK"], writes=[ek])

                            def b_av(i, g=g, gb=gb, nst=nst):
                                hl, tj, seg, lj, qlo, qhi, firsts = bits[i]
                                et, ek = etb[i % 4]
                                for n_, qi in enumerate(range(qlo, qhi + 1)):
                                    slot = seg * nst + qi
                                    bt, bi, bk = abanks[slot // 7]
                                    off = (slot % 7) * 65
                                    first = firsts[n_]
                                    S.add("pe", lambda e, qi=qi, bt=bt, bi=bi, off=off, first=first: e.matmul(
                                        bt[:, bi, off:off + 65], et[:, (qi - qlo) * 128:(qi - qlo + 1) * 128], VB[:, gb, tj, hl, :],
                                        start=first, stop=True, skip_group_check=True),
                                        reads=[ek, ("V1A", gb)], writes=[bk])
                                if tj != TT - 1:
                                    return
                                for b3 in range(3):
                                    bt, bi, bk = abanks[b3]
                                    ns = 7 if b3 < 2 else 2
                                    av = bt[:, bi, 0:ns * 65].rearrange("p (a b) -> p a b", a=ns)
                                    S.add("dve", lambda e, av=av, b3=b3, ns=ns: e.tensor_copy(
                                        NUMB[:, g, b3 * 7:b3 * 7 + ns, hl * 64:(hl + 1) * 64], av[:, :, 0:64]),
                                        reads=[bk], writes=["NUMB"])
                                    S.add("dve", lambda e, av=av, b3=b3, ns=ns: e.tensor_copy(
                                        DENF[:, g, b3 * 7:b3 * 7 + ns, hl], av[:, :, 64]),
                                        reads=[bk], writes=["DENF"])

                            b_score(0)
                            b_score(1)
                            for i in range(len(bits)):
                                if i + 2 < len(bits):
                                    b_score(i + 2)
                                b_av(i)
                        for w in range(4):
                            mm = []
                            for jj in range(4):
                                mm.append((0, 4 * w + jj, slice(jj * 128, (jj + 1) * 128), slice(0, 128)))
                            for r in range(4):
                                mm.append((1, r * 4 + w, slice(r, 512, 4), slice(0, 128)))
                            for r in range(16):
                                mm.append((2, r, slice(r, 512, 16), slice(32 * w, 32 * w + 32)))
                            for n_, (g, tj, osl, isl) in enumerate(mm):
                                S.add("pe", lambda e, g=g, tj=tj, osl=osl, isl=isl, n_=n_: e.matmul(
                                    PJ[:, osl], NUMB[:, g, tj, :], IDENT[:, isl], start=(n_ == 0), stop=(n_ == len(mm) - 1),
                                    skip_group_check=True),
                                    reads=["NUMB", "IDENT"], writes=["PJ"])
                            for n_, (g, tj, osl, isl) in enumerate(mm):
                                S.add("pe", lambda e, g=g, tj=tj, osl=osl, isl=isl, n_=n_: e.matmul(
                                    SCB[0:2, 1, osl], DENF[:, g, tj, :], IDENTF[:, isl], start=(n_ == 0), stop=(n_ == len(mm) - 1),
                                    skip_group_check=True),
                                    reads=["DENF", "IDENTF"], writes=[("SCB", 1)])
                            S.add("dve", lambda e: e.reciprocal(RDEN[:, :], SCB[0:2, 1, :]), reads=[("SCB", 1)], writes=["RDEN"])
                            S.add("pe", lambda e: e.matmul(SCA[:, 0, :], SEL[:, :], RDEN[:, :], start=True, stop=True),
                                  reads=["SEL", "RDEN"], writes=[("SCA", 0)])
                            S.add("act", lambda e: e.copy(RDB[:, :], SCA[:, 0, :]), reads=[("SCA", 0)], writes=["RDB"])
                            S.add("dve", lambda e, w=w, hp=hp: e.tensor_tensor(MIXT[:, hp, w * 512:(w + 1) * 512], PJ[:, :], RDB[:, :], ALU.mult),
                                  reads=["PJ", "RDB"], writes=[("MIXT", hp)])

                wo_d = (awout_d if is_a else bwout_d)[j].rearrange("(c p) n -> p c n", p=128)
                load_w(WO[:, 0:nch, :], wo_d, "WO")
                for tt in range(TT):
                    for half in range(2):
                        for c in range(nch):
                            S.add("pe", lambda e, tt=tt, half=half, c=c, nch=nch: e.matmul(
                                ACC[:, half, :], MIXT[:, c, tt * 128:(tt + 1) * 128], WO[:, c, half * 512:(half + 1) * 512],
                                start=(c == 0), stop=(c == nch - 1)),
                                reads=[mixkey(c), "WO"], writes=[("ACC", half)])
                    S.add("dve", lambda e, tt=tt: e.tensor_tensor(X[:, tt, :], X[:, tt, :], ACC[:, :, :].rearrange("p a b -> p (a b)"), ALU.add),
                          reads=[("X", tt), ("ACC", 0), ("ACC", 1)], writes=[("X", tt)])

                S.dma(lambda e, li=li: e.dma_start(out=FGAIN, in_=gains_d[li, 1]), writes=["FGAIN"])
                norm_to_HT(X, TT, "FGAIN", [FHB0, FHB1], FJK, "FJK", ["FHB0", "FHB1"], HT, "HT", "X", FGAIN)
                mark_ht_ready()
                wu_d = wup_d[li].rearrange("(c p) n -> p c n", p=128)
                wd_d = wdown_d[li]
                WUs = [(WU0, "WU0"), (WU1, "WU1")]
                Gs = [(G0, "G0"), (G1, "G1")]
                S.add("pool", lambda e: e.memset(U[:, 0:1], 0.0), writes=["U"])
                S.add("pool", lambda e: e.memset(U[:, SEQ + 1:SEQ + 2], 0.0), writes=["U"])
                upbanks = [(PJ[:, :], "PJ"), (SCA[:, 0, :], ("SCA", 0)), (SCA[:, 1, :], ("SCA", 1))]
                dnbanks = [(ACC, ACC_K), (SCB, SCB_K)]
                ub = [0]
                db = [0]

                def load_up(gi):
                    fc0, n = FFN_GROUPS[gi]
                    wu, wk = WUs[gi % 2]
                    load_w(wu[:, :, 0, 0:n * 128], wu_d[:, :, fc0 * 128:(fc0 + n) * 128], wk)
                    load_w(wu[:, :, 1, 0:n * 128], wu_d[:, :, DFF + fc0 * 128:DFF + (fc0 + n) * 128], wk)

                def up(gi):
                    fc0, n = FFN_GROUPS[gi]
                    wu, wk = WUs[gi % 2]
                    gt, gk = Gs[gi % 2]
                    for l in range(n):
                        for ab in range(2):
                            ch = ab * NFC + fc0 + l
                            for tq in range(4):
                                bank, bkey = upbanks[ub[0] % 3]
                                ub[0] += 1
                                for c in range(DC):
                                    S.add("pe", lambda e, c=c, bank=bank, wu=wu, ab=ab, l=l, tq=tq: e.matmul(
                                        bank, wu[:, c, ab, l * 128:(l + 1) * 128], HT[:, c, tq * 512:(tq + 1) * 512],
                                        start=(c == 0), stop=(c == DC - 1)),
                                        reads=["HTall", wk], writes=[bkey])
                                S.add("act", lambda e, bank=bank, tq=tq: e.copy(U[:, 1 + tq * 512:1 + (tq + 1) * 512], bank),
                                      reads=[bkey], writes=["U"])
                            Cc, ck = (CA, "CA") if ab == 0 else (CB, "CB")
                            S.add("dve", lambda e, Cc=Cc, ch=ch: e.tensor_scalar(Cc[:, :], U[:, 1:SEQ + 1], CW[:, ch, 1:2], CW[:, ch, 3:4],
                                                                             ALU.mult, ALU.add),
                                  reads=["U", "CW"], writes=[ck])
                            S.add("dve", lambda e, Cc=Cc, ch=ch: e.scalar_tensor_tensor(Cc[:, :], U[:, 0:SEQ], CW[:, ch, 0:1], Cc[:, :],
                                                                                    ALU.mult, ALU.add),
                                  reads=["U", "CW", ck], writes=[ck])
                            S.add("dve", lambda e, Cc=Cc, ch=ch: e.scalar_tensor_tensor(Cc[:, :], U[:, 2:SEQ + 2], CW[:, ch, 2:3], Cc[:, :],
                                                                                    ALU.mult, ALU.add),
                                  reads=["U", "CW", ck], writes=[ck])
                            if ab == 0:
                                S.add("act", lambda e: e.activation(CA[:, :], CA[:, :], AF.Silu), reads=["CA"], writes=["CA"])
                            else:
                                S.add("pool", lambda e, gt=gt, l=l: e.tensor_tensor(gt[:, l, :], CA[:, :], CB[:, :], ALU.mult),
                                      reads=["CA", "CB"], writes=[gk])

                def load_down(gi):
                    fc0, n = FFN_GROUPS[gi]
                    load_w(WD[:, 0:n, :], wd_d[fc0 * 128:(fc0 + n) * 128, :].rearrange("(c p) n -> p c n", p=128), "WD")

                def down(gi):
                    fc0, n = FFN_GROUPS[gi]
                    gt, gk = Gs[gi % 2]
                    for tt in range(TT):
                        bt, bkey = dnbanks[db[0] % 2]
                        db[0] += 1
                        for half in range(2):
                            for l in range(n):
                                S.add("pe", lambda e, bt=bt, half=half, l=l, tt=tt, gt=gt: e.matmul(
                                    bt[:, half, :], gt[:, l, tt * 128:(tt + 1) * 128], WD[:, l, half * 512:(half + 1) * 512],
                                    start=(l == 0), stop=(l == n - 1)),
                                    reads=[gk, "WD"], writes=bkey)
                        S.add("dve", lambda e, tt=tt, bt=bt: e.tensor_tensor(X[:, tt, :], X[:, tt, :], bt[:, :, :].rearrange("p a b -> p (a b)"), ALU.add),
                              reads=[("X", tt)] + bkey, writes=[("X", tt)])

                ng = len(FFN_GROUPS)
                load_up(0)
                load_up(1)
                up(0)
                load_down(0)
                for gi in range(1, ng):
                    up(gi)
                    if gi + 1 < ng:
                        load_up(gi + 1)
                    down(gi - 1)
                    load_down(gi)
                down(ng - 1)
            for q4 in range(4):
                S.dma(lambda e, s=s, q4=q4: e.dma_start(
                    out=y_d[s, q4 * 512:(q4 + 1) * 512, :].rearrange("(t p) d -> p t d", p=128),
                    in_=X[:, q4 * 4:(q4 + 1) * 4, :]),
                    reads=[("X", t) for t in range(q4 * 4, q4 * 4 + 4)], is_output=True)
        S.emit()
    return nc


def _const_tables():
    rot = 16
    half = 8
    inv = (np.float32(500000.0) ** (-(np.arange(half, dtype=np.float32) * np.float32(2.0) / np.float32(rot)))).astype(np.float32)
    rope = np.zeros((3, 2, 128, TT, 16), np.float32)
    for g, (window, dil) in enumerate(B_GROUPS):
        L = SEQ // dil
        nst = L // 128
        for tj in range(TT):
            r, i0 = tj // nst, (tj % nst) * 128
            pos = (r + dil * (i0 + np.arange(128))).astype(np.float32)
            ang = (pos[:, None] * inv[None, :]).astype(np.float32)
            cs_, sn_ = np.cos(ang), np.sin(ang)
            rope[g, 0, :, tj, :] = np.concatenate([cs_, cs_], axis=1)
            rope[g, 1, :, tj, :] = np.concatenate([-sn_, sn_], axis=1)
    rope = rope.reshape(3, 2, 128, TT * 16)
    ident = np.eye(128, dtype=np.float32)
    k = np.arange(128)[:, None]
    c = np.arange(384)[None, :]
    rel = (c // 128 - 1) * 128 + (c % 128) - k
    mask = np.where(np.abs(rel) <= 64, 1.0, 0.0).astype(np.float32)
    sel = np.zeros((2, 128), np.float32)
    sel[0, :64] = 1.0
    sel[1, 64:] = 1.0
    return rope, ident, mask, sel


def _prep_shared(inp):
    f = lambda a: np.ascontiguousarray(np.asarray(a, dtype=np.float32))
    gains = np.zeros((4, 3, 128, D), np.float32)
    hg = np.zeros((4, 128, 1664), np.float32)
    cw = np.zeros((4, 128, 44, 4), np.float32)
    for i in range(4):
        j = i // 2
        gains[i, 0] = np.broadcast_to(f(inp["norm_mix"])[i][None, :], (128, D))
        gains[i, 1] = np.broadcast_to(f(inp["norm_ffn"])[i][None, :], (128, D))
        gains[i, 2] = np.broadcast_to(f(inp["norm_mem"])[i][None, :], (128, D))
        row = np.zeros(1664, np.float32)
        if i % 2 == 0:
            qg = f(inp["a_q_norm"])[j]
            kg = f(inp["a_k_norm"])[j]
            row[0:512] = np.concatenate([qg] * 4 + [kg] * 4)
            row[1280:1408] = f(inp["a_subln"])[j]
            row[1408:1664] = f(inp["a_lambda"])[j].reshape(-1)
        else:
            for g in range(3):
                qg = f(inp["b_q_norm"])[j, g]
                kg = f(inp["b_k_norm"])[j, g]
                row[g * 256:(g + 1) * 256] = np.concatenate([qg, qg, kg, kg])
        row[768:1024] = np.tile(f(inp["xq_norm"])[i], 4)
        row[1024:1280] = np.tile(f(inp["xk_norm"])[i], 4)
        hg[i] = np.broadcast_to(row[None, :], (128, 1664))
        cwi = f(inp["conv_w"])[i].reshape(3, 44, 128)
        cbi = f(inp["conv_b"])[i].reshape(44, 128)
        cw[i, :, :, 0:3] = cwi.transpose(2, 1, 0)
        cw[i, :, :, 3] = cbi.T
    rope, ident, mask, sel = _const_tables()
    shared = {
        "w_mem_kv": f(inp["w_mem_kv"]), "a_w_in": f(inp["a_w_in"]), "a_w_out": f(inp["a_w_out"]),
        "b_w_in": f(inp["b_w_in"]), "b_w_out": f(inp["b_w_out"]), "w_up": f(inp["w_up"]), "w_down": f(inp["w_down"]),
        "gains": gains, "hg": hg, "cw": cw.reshape(4, 128, 176), "rope": rope, "ident": ident, "mask": mask, "sel": sel,
    }
    return shared


_PROGRAM_CACHE = {}


def _get_program(nseq, layers):
    key = (nseq, tuple(layers))
    if key not in _PROGRAM_CACHE:
        _PROGRAM_CACHE[key] = build_program(nseq, list(layers))
    return _PROGRAM_CACHE[key]


def kernel(**inp):
    xp = np.asarray(inp["x_prompt"], dtype=np.float32)
    xs = np.asarray(inp["x_sample"], dtype=np.float32)
    mp = np.asarray(inp["mem_prompt"], dtype=np.float32)
    ms = np.asarray(inp["mem_sample"], dtype=np.float32)
    x_all = np.concatenate([xp, xs], axis=0)
    m_all = np.concatenate([mp, ms], axis=0)
    nb = xp.shape[0]
    shared = _prep_shared(inp)
    nc = _get_program(SEQ_PER_CORE, (0, 1, 2, 3))
    in_maps = []
    for c in range(N_CORES):
        d = dict(shared)
        d["x"] = np.ascontiguousarray(x_all[c * SEQ_PER_CORE:(c + 1) * SEQ_PER_CORE])
        d["mem"] = np.ascontiguousarray(m_all[c * SEQ_PER_CORE:(c + 1) * SEQ_PER_CORE])
        in_maps.append(d)
    res = run_bass_kernel_spmd(nc, in_maps, core_ids=list(range(N_CORES)))
    y = np.concatenate([np.asarray(r["y"], dtype=np.float32) for r in res.results], axis=0)
    return (y[:nb], y[nb:])
```

```python
import contextlib
import numpy as np
import ml_dtypes
import concourse.bass as bass
import concourse.mybir as mybir
from concourse.bass_utils import run_bass_kernel_spmd

F32 = mybir.dt.float32
BF16 = mybir.dt.bfloat16
AF = mybir.ActivationFunctionType
ALU = mybir.AluOpType
AX = mybir.AxisListType

ENGS = ("pe", "act", "dve", "pool", "sp")
EPS = 1e-6


class Op:
    __slots__ = ("eng", "fn", "deps", "flag", "cnt", "idx", "dma", "dsem", "dval", "dprev")

    def __init__(self, eng, fn):
        self.eng = eng
        self.fn = fn
        self.deps = []
        self.flag = False
        self.cnt = 0
        self.idx = 0
        self.dma = False
        self.dsem = None
        self.dval = 0
        self.dprev = None


def _base(k):
    return k[0] if isinstance(k, tuple) else k


class Sched:
    def __init__(self, nc, n_dma_sems=32):
        self.nc = nc
        self.ops = {e: [] for e in ENGS}
        self.last_w = {}
        self.readers = {}
        self.by_base = {}
        self.alias = {}
        self.n_dma_sems = n_dma_sems
        self.dma_rr = 0
        self.dma_rr_sw = 0
        self.dma_cnt = [0] * n_dma_sems
        self.dma_last = [None] * n_dma_sems
        self.all_dma_out = []

    def set_alias(self, a, b):
        self.alias.setdefault(a, set()).add(b)
        self.alias.setdefault(b, set()).add(a)

    def _deps(self, op, reads, writes):
        deps = {}

        def add(d):
            if d is None or d is op:
                return
            if d.dma:
                deps[("dma", id(d))] = d
                return
            if d.eng == op.eng:
                if op.eng == "pe":
                    return
                if op.idx - d.idx > 2:
                    return
            k = d.eng
            if k not in deps or deps[k].idx < d.idx:
                deps[k] = d

        for r in reads:
            add(self.last_w.get(r))
            al = self.alias.get(_base(r))
            if al:
                for ab in al:
                    for k2 in self.by_base.get(ab, ()):
                        add(self.last_w.get(k2))
        for w in writes:
            add(self.last_w.get(w))
            for rd in self.readers.get(w, {}).values():
                add(rd)
            al = self.alias.get(_base(w))
            if al:
                for ab in al:
                    for k2 in self.by_base.get(ab, ()):
                        add(self.last_w.get(k2))
                        for rd in self.readers.get(k2, {}).values():
                            add(rd)
        rk = ("dma", id(op)) if op.dma else op.eng
        for r in reads:
            self.readers.setdefault(r, {})[rk] = op
            self.by_base.setdefault(_base(r), set()).add(r)
        for w in writes:
            self.last_w[w] = op
            self.readers[w] = {}
            self.by_base.setdefault(_base(w), set()).add(w)
        op.deps = list(deps.values())
        for d in op.deps:
            d.flag = True

    PSUM_BASES = ("SCA", "SCB", "ACC", "PJ", "TP")

    def add(self, eng, fn, reads=(), writes=()):
        op = Op(eng, fn)
        op.idx = len(self.ops[eng])
        rl = [("rl", r) for r in reads if _base(r) in self.PSUM_BASES]
        if rl:
            writes = list(writes) + rl
        self._deps(op, reads, writes)
        self.ops[eng].append(op)
        return op

    def dma(self, fn, reads=(), writes=(), queue="sp", is_output=False):
        op = Op(queue, fn)
        op.dma = True
        op.idx = len(self.ops[queue])
        half = self.n_dma_sems // 2
        if queue == "pool":
            s = half + self.dma_rr_sw
            self.dma_rr_sw = (self.dma_rr_sw + 1) % (self.n_dma_sems - half)
        else:
            s = self.dma_rr
            self.dma_rr = (self.dma_rr + 1) % half
        self.dma_cnt[s] += 1
        op.dsem = s
        op.dval = 16 * self.dma_cnt[s]
        op.dprev = self.dma_last[s]
        self.dma_last[s] = op
        self._deps(op, reads, writes)
        self.ops[queue].append(op)
        if is_output:
            self.all_dma_out.append(op)
        return op

    def emit(self):
        nc = self.nc
        for e in ENGS:
            c = 0
            for op in self.ops[e]:
                if op.dma:
                    continue
                if op.flag:
                    c += 1
                    op.cnt = c
        with contextlib.ExitStack() as st:
            esem = {e: st.enter_context(nc.semaphore("s_" + e)) for e in ENGS}
            dsem = [st.enter_context(nc.semaphore("d_%d" % i)) for i in range(self.n_dma_sems)]
            block = st.enter_context(nc.Block())

            def run(e, eng):
                seen = {}
                for op in self.ops[e]:
                    waits = []
                    for d in op.deps:
                        if d.dma:
                            key = ("d", d.dsem)
                            if seen.get(key, 0) < d.dval:
                                seen[key] = d.dval
                                waits.append((dsem[d.dsem], d.dval))
                        else:
                            key = ("e", d.eng)
                            if seen.get(key, 0) < d.cnt:
                                seen[key] = d.cnt
                                waits.append((esem[d.eng], d.cnt))
                    if op.dma and op.dprev is not None:
                        key = ("d", op.dsem)
                        if seen.get(key, 0) < op.dprev.dval:
                            seen[key] = op.dprev.dval
                            waits.append((dsem[op.dsem], op.dprev.dval))
                    for (s, v) in waits:
                        eng.wait_ge(s, v)
                    ins = op.fn(eng)
                    if op.dma:
                        ins.then_inc(dsem[op.dsem], 16)
                    elif op.flag:
                        ins.then_inc(esem[e], 1)
                if e == "sp":
                    fin = {}
                    for op in self.all_dma_out:
                        fin[op.dsem] = max(fin.get(op.dsem, 0), op.dval)
                    for s, v in fin.items():
                        eng.wait_ge(dsem[s], v)

            @block.tensor
            def _(t):
                run("pe", t)

            @block.scalar
            def _(a):
                run("act", a)

            @block.vector
            def _(v):
                run("dve", v)

            @block.gpsimd
            def _(g):
                run("pool", g)

            @block.sync
            def _(s):
                run("sp", s)


D = 1024
SEQ = 2048
TT = 16
DC = 8
NMEM = 256
DFF = 2816
NFC = 22
B_GROUPS = ((128, 1), (512, 4), (2048, 16))
N_CORES = 8
SEQ_PER_CORE = 5
FFN_GROUPS = [(0, 3), (3, 3), (6, 3), (9, 3), (12, 3), (15, 3), (18, 3), (21, 1)]


def build_program(nseq, layers):
    nc = bass.Bass("TRN2", target_bir_lowering=False)

    def din(name, shape, dt=F32):
        return nc.dram_tensor(name, list(shape), dt, kind="ExternalInput").ap()

    x_d = din("x", [nseq, SEQ, D])
    mem_d = din("mem", [nseq, NMEM, D])
    y_d = nc.dram_tensor("y", [nseq, SEQ, D], F32, kind="ExternalOutput").ap()
    wkv_d = din("w_mem_kv", [4, D, 512])
    awin_d = din("a_w_in", [2, D, 3328])
    awout_d = din("a_w_out", [2, 1280, D])
    bwin_d = din("b_w_in", [2, D, 4864])
    bwout_d = din("b_w_out", [2, 768, D])
    wup_d = din("w_up", [4, D, 2 * DFF])
    wdown_d = din("w_down", [4, DFF, D])
    gains_d = din("gains", [4, 3, 128, D])
    hg_d = din("hg", [4, 128, 1664])
    cw_d = din("cw", [4, 128, 44 * 4])
    rope_d = din("rope", [3, 2, 128, TT * 16])
    ident_d = din("ident", [128, 128])
    mask_d = din("mask", [128, 384])
    sel_d = din("sel", [2, 128])

    with contextlib.ExitStack() as st:
        def sb(name, shape, dt):
            return st.enter_context(nc.sbuf_tensor(name, list(shape), dt))

        def ps(name, shape, dt):
            return st.enter_context(nc.psum_tensor(name, list(shape), dt))

        S = Sched(nc)
        X = sb("X", [128, TT, D], F32)
        HT = sb("HT", [128, DC, SEQ], BF16)
        HG = sb("HG", [128, 1664], F32)
        CW = sb("CW", [128, 44, 4], F32)
        ROPE = sb("ROPE", [128, 3, 2, TT, 16], F32)
        IDENTF = sb("IDENTF", [128, 128], F32)
        IDENT = sb("IDENT", [128, 128], BF16)
        MASK = sb("MASK", [128, 384], BF16)
        SEL = sb("SEL", [2, 128], F32)
        MKT = sb("MKT", [128, 2, 256], BF16)
        MV1 = sb("MV1", [128, 2, 4, 65], BF16)
        SS = sb("SS", [128, 16], F32)
        RS = sb("RS", [128, 16], F32)
        NH = sb("NH", [128, 16], F32)
        ST8 = sb("ST8", [128, 8], F32)
        RT8 = sb("RT8", [128, 8], F32)
        ST8b = sb("ST8b", [128, 8], F32)
        RT8b = sb("RT8b", [128, 8], F32)
        LAM = sb("LAM", [128, 8], F32)
        SLGS = sb("SLGS", [128, 128], F32)
        ARENA_ELEMS = 46400
        AR = sb("AR", [128, ARENA_ELEMS], BF16)
        cursor = {"attn": 0, "ffn": 0}

        def carve(phase, nelem_bf16, shape, dt):
            off = cursor[phase]
            n = int(nelem_bf16)
            n = (n + 15) // 16 * 16
            cursor[phase] = off + n
            assert cursor[phase] <= ARENA_ELEMS, (phase, cursor[phase])
            v = AR[:, off:off + int(nelem_bf16)]
            if dt == F32:
                v = v.bitcast(F32)
            if len(shape) == 2:
                return v
            if len(shape) == 3:
                return v.rearrange("p (a b) -> p a b", a=shape[1])
            if len(shape) == 4:
                return v.rearrange("p (a b c) -> p a b c", a=shape[1], b=shape[2])
            if len(shape) == 5:
                return v.rearrange("p (a b c d) -> p a b c d", a=shape[1], b=shape[2], c=shape[3])
            raise ValueError

        def cb(phase, shape):
            return carve(phase, int(np.prod(shape[1:])), shape, BF16)

        def cf(phase, shape):
            return carve(phase, 2 * int(np.prod(shape[1:])), shape, F32)

        mixt_off = cursor["attn"]
        MIXT = cb("attn", [128, 10, SEQ])
        WIN = cb("attn", [128, DC, 768])
        QTB = cb("attn", [128, 2, SEQ])
        KTB = cb("attn", [128, 2, SEQ])
        v_off = cursor["attn"]
        _V = cb("attn", [128, 4160])
        V1A = AR[:, v_off:v_off + TT * 2 * 129].rearrange("p (a b c) -> p a b c", a=TT, b=2)
        VB = AR[:, v_off:v_off + 2 * TT * 2 * 65].rearrange("p (g a b c) -> p g a b c", g=2, a=TT, b=2)
        sq_off = cursor["attn"]
        SQ = cf("attn", [128, 512])
        QN = cf("attn", [128, 512])
        GAIN = AR[:, sq_off:sq_off + 2048].bitcast(F32)
        QKB = cb("attn", [128, 512])
        QKB2 = cb("attn", [128, 512])
        RA = cf("attn", [128, 128])
        RB = cf("attn", [128, 128])
        et_off = cursor["attn"]
        ET0 = cb("attn", [128, 2, 512])
        ET1 = cb("attn", [128, 2, 512])
        JK = AR[:, et_off:et_off + 1024]
        HB0 = AR[:, et_off + 1024:et_off + 2048]
        a0_off = cursor["attn"]
        A0 = cf("attn", [128, 4, 128])
        HB1 = AR[:, a0_off:a0_off + 1024]
        R0 = cf("attn", [128, 4])
        R1 = cf("attn", [128, 4])
        SSE = cf("attn", [128, 4])
        RSE = cf("attn", [128, 4])
        MB = cb("attn", [128, 4, 128])
        attn_end = cursor["attn"]
        nb_off = mixt_off + 6 * SEQ
        NUMB = AR[:, nb_off:nb_off + 3 * TT * 128].rearrange("p (g a b) -> p g a b", g=3, a=TT)
        df_off = nb_off + 3 * TT * 128
        DENF = AR[:, df_off:df_off + 2 * 3 * TT * 2].bitcast(F32).rearrange("p (g a b) -> p g a b", g=3, a=TT)
        RDEN = QN[0:2, :]
        RDB = SQ
        MEMX = AR[:, mixt_off:mixt_off + 2 * SEQ].bitcast(F32).rearrange("p (a b) -> p a b", a=2)
        WO_off = mixt_off + 10 * SEQ
        WO = AR[:, WO_off:WO_off + 10 * D].rearrange("p (a b) -> p a b", a=10)
        assert 10 * D <= DC * 768 + 2 * SEQ
        WU0 = cb("ffn", [128, DC, 2, 384])
        WU1 = cb("ffn", [128, DC, 2, 384])
        WD = cb("ffn", [128, 3, D])
        G0 = cb("ffn", [128, 3, SEQ])
        G1 = cb("ffn", [128, 3, SEQ])
        u_off = cursor["ffn"]
        U = cf("ffn", [128, SEQ + 2])
        FGAIN = AR[:, u_off:u_off + 2048].bitcast(F32)
        ca_off = cursor["ffn"]
        CA = cf("ffn", [128, SEQ])
        FJK = AR[:, ca_off:ca_off + 1024]
        cb_off = cursor["ffn"]
        CB = cf("ffn", [128, SEQ])
        FHB0 = AR[:, cb_off:cb_off + 1024]
        FHB1 = AR[:, cb_off + 1024:cb_off + 2048]
        ATTN_KEYS = ["MIXT", "MIXH", "WIN", "QTB", "KTB", "V1A", "SQ", "QN", "QKB", "ROPESCR", "ET0", "ET1", "ETB", "EPI", "A0", "MB",
                     "JK", "HB0", "HB1", "NUMB", "DENF", "RDEN", "RDB", "MEMX", "WO", "GAIN", "SSE", "RSE"]
        FFN_KEYS = ["WU0", "WU1", "WD", "G0", "G1", "U", "CA", "CB", "FJK", "FHB0", "FHB1", "FGAIN"]
        for a_ in ATTN_KEYS:
            for b_ in FFN_KEYS:
                S.set_alias(a_, b_)
        for a_ in ("WIN", "QTB", "KTB"):
            S.set_alias("WO", a_)
        for a_, b_ in [("ETB", "ET0"), ("ETB", "ET1"), ("JK", "ETB"), ("HB0", "ETB"), ("GAIN", "SQ"), ("GAIN", "QN"), ("JK", "ET0"), ("HB0", "ET1"), ("HB1", "A0"), ("NUMB", "MIXH"),
                       ("DENF", "MIXH"), ("MEMX", "MIXT"), ("RDEN", "QN"), ("RDB", "SQ"),
                       ("FGAIN", "U"), ("FJK", "CA"), ("FHB0", "CB"), ("FHB1", "CB")]:
            S.set_alias(a_, b_)

        def mixkey(c):
            return ("MIXT", c) if c < 6 else ("MIXH", c)

        SCA = ps("SCA", [128, 2, 512], F32)
        SCB = ps("SCB", [128, 2, 512], F32)
        ACC = ps("ACC", [128, 2, 512], F32)
        PJ = ps("PJ", [128, 512], F32)
        TP = ps("TP", [128, 8, 128], BF16)

        SCA_K = [("SCA", 0), ("SCA", 1)]
        SCB_K = [("SCB", 0), ("SCB", 1)]
        ACC_K = [("ACC", 0), ("ACC", 1)]
        S.dma(lambda e: e.dma_start(out=IDENTF[:], in_=ident_d), writes=["IDENTF"])
        S.dma(lambda e: nc.gpsimd.dma_start(out=MASK[:], in_=mask_d), writes=["MASK"], queue="pool")
        S.dma(lambda e: e.dma_start(out=SEL[:], in_=sel_d), writes=["SEL"])
        S.dma(lambda e: e.dma_start(out=ROPE[:].rearrange("p a b c d -> p a b (c d)"),
                                    in_=rope_d.rearrange("a b p n -> p a b n")), writes=["ROPE"])
        S.add("dve", lambda e: e.tensor_copy(IDENT[:], IDENTF[:]), reads=["IDENTF"], writes=["IDENT"])
        S.add("pool", lambda e: e.memset(NH[:], -0.5), writes=["NH"])
        S.add("pool", lambda e: e.memset(MV1[:], 1.0), writes=["MV1"])

        def rsqrt_pool(dst, src, n, scale, rkeys, wkeys):
            S.add("pool", lambda e: e.tensor_scalar(dst, src, scale, EPS, ALU.mult, ALU.add), reads=rkeys, writes=wkeys)
            S.add("pool", lambda e: e.tensor_tensor(dst, dst, NH[:, 0:n], ALU.pow), reads=wkeys + ["NH"], writes=wkeys)

        WIN_K = [("WIN", 0), ("WIN", 1)]

        def load_w(dst, src, wkeys):
            if not isinstance(wkeys, list):
                wkeys = [wkeys]
            S.dma(lambda e: nc.gpsimd.dma_start(out=dst, in_=src), writes=wkeys, queue="pool")

        def norm_to_HT(src3, ntiles, gain_key, hbs, jk, jkkey, hbkeys, dstT, dstkey, xkey, gain_ap):
            for tt in range(ntiles):
                S.add("act", lambda e, tt=tt: e.activation(jk, src3[:, tt, :], AF.Square, accum_out=SS[:, tt:tt + 1]),
                      reads=[(xkey, tt)], writes=[jkkey, ("SS", tt)])
            rsqrt_pool(RS[:, 0:ntiles], SS[:, 0:ntiles], ntiles, 1.0 / D, [("SS", t) for t in range(ntiles)], ["RS"])
            for tt in range(ntiles):
                hb = hbs[tt % 2]
                hk = hbkeys[tt % 2]
                S.add("dve", lambda e, tt=tt, hb=hb: e.scalar_tensor_tensor(hb, src3[:, tt, :], RS[:, tt:tt + 1], gain_ap,
                                                                        ALU.mult, ALU.mult),
                      reads=[(xkey, tt), "RS", gain_key], writes=[hk])
                for c in range(DC):
                    S.add("pe", lambda e, c=c, hb=hb: e.transpose(TP[:, c, :], hb[:, c * 128:(c + 1) * 128], IDENT[:]),
                          reads=[hk, "IDENT"], writes=[("TP", 0), ("TP", 1), "TPB"])
                S.add("act", lambda e, tt=tt: e.copy(dstT[:, :, tt * 128:(tt + 1) * 128], TP[:, :, :]),
                      reads=[("TP", 0), ("TP", 1)], writes=[(dstkey, tt), "TPB"] + (["HTall"] if dstkey == "HT" else []))

        STs = [ST8, ST8b]
        RTs = [RT8, RT8b]
        QKBs = [QKB, QKB2]
        TP_K = [("TP", 0), ("TP", 1)]

        def prepA(k, sqb, sqkey, psv, nh, rkeys):
            sq3 = sqb[:, 0:nh * 64].rearrange("p (a b) -> p a b", a=nh)
            S.add("act", lambda e: e.activation(sq3, psv, AF.Square), reads=rkeys, writes=[sqkey])
            S.add("dve", lambda e: e.tensor_reduce(STs[k][:, 0:nh], sq3, AX.X, ALU.add), reads=[sqkey], writes=[("ST8", k)])
            rsqrt_pool(RTs[k][:, 0:nh], STs[k][:, 0:nh], nh, 1.0 / 64, [("ST8", k)], [("RT8", k)])

        def prepB(k, psv, nh, gainv, cs, outs, rkeys, wkeys, gkey):
            n = nh * 64
            qn3 = QN[:, 0:n].rearrange("p (a b) -> p a b", a=nh)
            rt3 = RTs[k][:, 0:nh].unsqueeze(2).broadcast_to([128, nh, 64])
            S.add("dve", lambda e: e.tensor_tensor(qn3, psv, gainv, ALU.mult), reads=rkeys + [gkey], writes=["QN"])
            if cs is not None:
                ccv, ssv = cs
                cc3 = ccv.unsqueeze(1).broadcast_to([128, nh, 16])
                sa3 = ssv[:, 0:8].unsqueeze(1).broadcast_to([128, nh, 8])
                sb3 = ssv[:, 8:16].unsqueeze(1).broadcast_to([128, nh, 8])
                t = qn3[:, :, 0:16]
                ra = RA[:, 0:nh * 16].rearrange("p (a b) -> p a b", a=nh)
                rb = RB[:, 0:nh * 16].rearrange("p (a b) -> p a b", a=nh)
                S.add("dve", lambda e: e.tensor_tensor(ra, t, cc3, ALU.mult), reads=["QN", "ROPE"], writes=[("ROPESCR", 0)])
                S.add("dve", lambda e: e.tensor_tensor(rb[:, :, 0:8], qn3[:, :, 8:16], sa3, ALU.mult), reads=["QN", "ROPE"], writes=[("ROPESCR", 1)])
                S.add("dve", lambda e: e.tensor_tensor(rb[:, :, 8:16], qn3[:, :, 0:8], sb3, ALU.mult), reads=["QN", "ROPE"], writes=[("ROPESCR", 2)])
                S.add("dve", lambda e: e.tensor_tensor(t, ra, rb, ALU.add), reads=[("ROPESCR", 0), ("ROPESCR", 1), ("ROPESCR", 2)], writes=["QN"])
            for (h0, h1, oap, vf) in outs:
                S.add("dve", lambda e, h0=h0, h1=h1, oap=oap, vf=vf: e.tensor_tensor(oap, vf(qn3[:, h0:h1, :]), vf(rt3[:, h0:h1, :]), ALU.mult),
                      reads=["QN", ("RT8", k)], writes=wkeys)

        def skew(n, stageA, stageB):
            for t in range(n + 1):
                if t < n:
                    stageA(t)
                if t >= 1:
                    stageB(t - 1)

        ident_v = (lambda v: v)

        def proj_tok(bank, bkey, tok_ap_fn, wv, ncols, wkey):
            for c in range(DC):
                S.add("pe", lambda e, c=c: e.matmul(bank[:, 0:ncols], tok_ap_fn(c), wv[:, c, 0:ncols],
                                                    start=(c == 0), stop=(c == DC - 1)),
                      reads=["HTall"] + wkey, writes=[bkey])

        HT_KEYS = [("HT", t) for t in range(TT)]

        def mark_ht_ready():
            pass

        for s in range(nseq):
            for q4 in range(4):
                S.dma(lambda e, s=s, q4=q4: e.dma_start(
                    out=X[:, q4 * 4:(q4 + 1) * 4, :],
                    in_=x_d[s, q4 * 512:(q4 + 1) * 512, :].rearrange("(t p) d -> p t d", p=128)),
                    writes=[("X", t) for t in range(q4 * 4, q4 * 4 + 4)])
            for li in layers:
                j = li // 2
                is_a = (li % 2 == 0)
                nmix = 8 if is_a else 4
                nch = nmix + 2
                S.dma(lambda e, li=li: e.dma_start(out=HG[:], in_=hg_d[li]), writes=["HG"])
                S.dma(lambda e, li=li: e.dma_start(out=CW[:].rearrange("p a b -> p (a b)"), in_=cw_d[li]), writes=["CW"])
                XQG = HG[:, 768:1024].rearrange("p (a b) -> p a b", a=4)
                XKG = HG[:, 1024:1280].rearrange("p (a b) -> p a b", a=4)
                S.dma(lambda e, li=li: e.dma_start(out=GAIN, in_=gains_d[li, 2]), writes=["GAIN"])
                S.dma(lambda e, s=s: e.dma_start(out=MEMX, in_=mem_d[s].rearrange("(t p) d -> p t d", p=128)),
                      writes=[("MEMX", 0), ("MEMX", 1)])
                load_w(WIN[:, :, 0:512], wkv_d[li].rearrange("(c p) n -> p c n", p=128), WIN_K)
                MEMT = QTB[:, 0, 0:DC * 256].rearrange("p (a b) -> p a b", a=DC)
                norm_to_HT(MEMX, 2, "GAIN", [HB0, HB1], JK, "JK", ["HB0", "HB1"], MEMT, "QTB", "MEMX", GAIN)
                pjb = [(PJ[:, :], "PJ"), (ACC[:, 0, :], ("ACC", 0))]
                sqb = [(SCB[:, 0, :], ("SCB", 0)), (SCB[:, 1, :], ("SCB", 1))]

                def memA(mt):
                    bank, bk = pjb[mt % 2]
                    for c in range(DC):
                        S.add("pe", lambda e, c=c, mt=mt, bank=bank: e.matmul(bank, MEMT[:, c, mt * 128:(mt + 1) * 128], WIN[:, c, 0:512],
                                                                          start=(c == 0), stop=(c == DC - 1)),
                              reads=[("QTB", 0), ("QTB", 1)] + WIN_K, writes=[bk])
                    prepA(mt % 2, sqb[mt % 2][0], sqb[mt % 2][1], bank[:, 0:256].rearrange("p (a b) -> p a b", a=4), 4, [bk])
                    S.add("act", lambda e, mt=mt, bank=bank: e.copy(MV1[:, mt, :, 0:64], bank[:, 256:512].rearrange("p (a b) -> p a b", a=4)),
                          reads=[bk], writes=["MV1"])

                def memB(mt):
                    bank, bk = pjb[mt % 2]
                    k = mt % 2
                    prepB(k, bank[:, 0:256].rearrange("p (a b) -> p a b", a=4), 4, XKG, None,
                          [(0, 4, QKBs[k][:, 0:256].rearrange("p (a b) -> p a b", a=4), ident_v)], [bk], [("QKB", k)], "HG")
                    for c2 in range(2):
                        S.add("pe", lambda e, c2=c2, k=k: e.transpose(TP[:, 4 * k + c2, :], QKBs[k][:, c2 * 128:(c2 + 1) * 128], IDENT[:]),
                              reads=[("QKB", k), "IDENT"], writes=[("TP", k), "TPB"])
                    S.add("act", lambda e, mt=mt, k=k: e.copy(MKT[:, :, mt * 128:(mt + 1) * 128], TP[:, 4 * k:4 * k + 2, :]),
                          reads=[("TP", k)], writes=["MKT", "TPB"])

                skew(2, memA, memB)
                S.dma(lambda e, li=li: e.dma_start(out=GAIN, in_=gains_d[li, 0]), writes=["GAIN"])
                norm_to_HT(X, TT, "GAIN", [HB0, HB1], JK, "JK", ["HB0", "HB1"], HT, "HT", "X", GAIN)
                mark_ht_ready()

                win_d = awin_d if is_a else bwin_d
                xq_col0 = 3072 if is_a else 4608
                wview = win_d[j].rearrange("(c p) n -> p c n", p=128)
                load_w(WIN[:, :, 0:256], wview[:, :, xq_col0:xq_col0 + 256], WIN_K)
                XQT = [QTB[:, 0, :], QTB[:, 1, :]]
                XQK = [("QTB", 0), ("QTB", 1)]
                def xqA(tt):
                    bank, bk = pjb[tt % 2]
                    proj_tok(bank, bk, lambda c, tt=tt: HT[:, c, tt * 128:(tt + 1) * 128], WIN, 256, WIN_K)
                    prepA(tt % 2, sqb[tt % 2][0], sqb[tt % 2][1], bank[:, 0:256].rearrange("p (a b) -> p a b", a=4), 4, [bk])

                def xqB(tt):
                    bank, bk = pjb[tt % 2]
                    k = tt % 2
                    prepB(k, bank[:, 0:256].rearrange("p (a b) -> p a b", a=4), 4, XQG, None,
                          [(0, 4, QKBs[k][:, 0:256].rearrange("p (a b) -> p a b", a=4), ident_v)], [bk], [("QKB", k)], "HG")
                    for c2 in range(2):
                        S.add("pe", lambda e, c2=c2, k=k: e.transpose(TP[:, 4 * k + c2, :], QKBs[k][:, c2 * 128:(c2 + 1) * 128], IDENT[:]),
                              reads=[("QKB", k), "IDENT"], writes=[("TP", k), "TPB"])
                    for c2 in range(2):
                        S.add("act", lambda e, tt=tt, c2=c2, k=k: e.copy(XQT[c2][:, tt * 128:(tt + 1) * 128], TP[:, 4 * k + c2, :]),
                              reads=[("TP", k)], writes=[XQK[c2], "TPB"])

                skew(TT, xqA, xqB)

                def load_pair(hp_):
                    for seg, (c0, w) in enumerate([(128 * hp_, 128), (512 + 128 * hp_, 128), (1024 + 128 * hp_, 128),
                                                   (1536 + 128 * hp_, 128)]):
                        load_w(WIN[:, :, seg * 128:(seg + 1) * 128], wview[:, :, c0:c0 + w], WIN_K)
                    load_w(WIN[:, :, 512:768], wview[:, :, 2048 + 256 * hp_:2048 + 256 * hp_ + 256], WIN_K)

                def load_bround(hp_, g_):
                    gb_ = g_ % 2
                    for kind in range(3):
                        c0 = kind * 1536 + g_ * 512 + hp_ * 128
                        load_w(WIN[:, :, gb_ * 384 + kind * 128:gb_ * 384 + (kind + 1) * 128], wview[:, :, c0:c0 + 128], [("WIN", gb_)])

                if is_a:
                    load_pair(0)
                else:
                    load_bround(0, 0)
                xits = [(c2, qc, hl) for c2 in range(2) for qc in range(4) for hl in range(2)]

                def x_score(i):
                    c2, qc, hl = xits[i]
                    r0 = hl * 64
                    sc, sk = (SCA, SCA_K) if i % 2 == 0 else (SCB, SCB_K)
                    et, ek = (ET0, "ET0") if i % 2 == 0 else (ET1, "ET1")
                    for mt in range(2):
                        S.add("pe", lambda e, mt=mt, sc=sc, c2=c2, r0=r0, qc=qc: e.matmul(
                            sc[:, mt, :], MKT[r0:r0 + 64, c2, mt * 128:(mt + 1) * 128],
                            XQT[c2][r0:r0 + 64, qc * 512:(qc + 1) * 512], start=True, stop=True),
                            reads=["MKT", XQK[c2]], writes=sk)
                    S.add("act", lambda e, sc=sc, et=et: e.activation(et[:, :, :], sc[:, :, :], AF.Exp, scale=0.125),
                          reads=sk, writes=[ek])

                def x_av(i):
                    c2, qc, hl = xits[i]
                    h = 2 * c2 + hl
                    et, ek = (ET0, "ET0") if i % 2 == 0 else (ET1, "ET1")
                    for qt in range(4):
                        for mt in range(2):
                            S.add("pe", lambda e, qt=qt, mt=mt, et=et, h=h, hl=hl: e.matmul(
                                ACC[:, hl, qt * 65:(qt + 1) * 65], et[:, mt, qt * 128:(qt + 1) * 128], MV1[:, mt, h, :],
                                start=(qt == 0 and mt == 0), stop=(mt == 1), skip_group_check=True),
                                reads=[ek, "MV1"], writes=[("ACC", hl)])
                    accv = ACC[:, hl, 0:260].rearrange("p (a b) -> p a b", a=4)
                    rr = R0 if hl == 0 else R1
                    S.add("dve", lambda e, accv=accv, rr=rr: e.reciprocal(rr[:, :], accv[:, :, 64]),
                          reads=[("ACC", hl)], writes=[("EPI", hl)])
                    S.add("dve", lambda e, accv=accv, hl=hl, rr=rr: e.tensor_tensor(
                        MB[:, :, hl * 64:(hl + 1) * 64], accv[:, :, 0:64],
                        rr[:, :].unsqueeze(2).broadcast_to([128, 4, 64]), ALU.mult),
                        reads=[("ACC", hl), ("EPI", hl)], writes=["MB"])
                    if hl == 1:
                        k = (i // 2) % 2
                        for qt in range(4):
                            S.add("pe", lambda e, qt=qt, k=k: e.transpose(TP[:, 4 * k + qt, :], MB[:, qt, :], IDENT[:]),
                                  reads=["MB", "IDENT"], writes=[("TP", k), "TPB"])
                        S.add("act", lambda e, qc=qc, c2=c2, nmix=nmix, k=k: e.copy(
                            MIXT[:, nmix + c2, qc * 512:(qc + 1) * 512], TP[:, 4 * k:4 * k + 4, :].rearrange("p a b -> p (a b)")),
                            reads=[("TP", k)], writes=[mixkey(nmix + c2), "TPB"])

                x_score(0)
                for i in range(len(xits)):
                    if i + 1 < len(xits):
                        x_score(i + 1)
                    x_av(i)

                if is_a:
                    QKG = HG[:, 0:512].rearrange("p (a b) -> p a b", a=8)
                    lam_init = 0.8 - 0.6 * float(np.exp(-0.3 * li))
                    lp = HG[:, 1408:1664].rearrange("p (a b) -> p a b", a=4)
                    S.add("dve", lambda e: e.tensor_tensor(QN[:, 0:128].rearrange("p (a b) -> p a b", a=2), lp[:, 0:4:2, :], lp[:, 1:4:2, :], ALU.mult),
                          reads=["HG"], writes=["QN"])
                    S.add("dve", lambda e: e.tensor_reduce(LAM[:, 0:2], QN[:, 0:128].rearrange("p (a b) -> p a b", a=2), AX.X, ALU.add),
                          reads=["QN"], writes=["LAM"])
                    S.add("act", lambda e: e.activation(LAM[:, 2:4], LAM[:, 0:2], AF.Exp), reads=["LAM"], writes=["LAM"])
                    S.add("dve", lambda e: e.scalar_tensor_tensor(LAM[:, 4:5], LAM[:, 2:3], -1.0, LAM[:, 3:4], ALU.mult, ALU.add),
                          reads=["LAM"], writes=["LAM"])
                    S.add("dve", lambda e, lam_init=lam_init: e.tensor_scalar(LAM[:, 5:6], LAM[:, 4:5], -lam_init, None, ALU.add),
                          reads=["LAM"], writes=["LAM"])
                    NEGLAM = LAM[:, 5:6]
                    S.add("dve", lambda e, lam_init=lam_init: e.tensor_scalar(SLGS[:], HG[:, 1280:1408], 1.0 - lam_init, None, ALU.mult),
                          reads=["HG"], writes=["SLGS"])
                    for hp in range(4):
                        vbk = [(SCA[:, 0, :], ("SCA", 0)), (SCA[:, 1, :], ("SCA", 1))]

                        def aA(tt):
                            bank, bk = pjb[tt % 2]
                            proj_tok(bank, bk, lambda c, tt=tt: HT[:, c, tt * 128:(tt + 1) * 128], WIN, 512, WIN_K)
                            prepA(tt % 2, sqb[tt % 2][0], sqb[tt % 2][1], bank[:, 0:512].rearrange("p (a b) -> p a b", a=8), 8, [bk])
                            vb_, vk_ = vbk[tt % 2]
                            for c in range(DC):
                                S.add("pe", lambda e, c=c, tt=tt, vb_=vb_: e.matmul(vb_[:, 0:256], HT[:, c, tt * 128:(tt + 1) * 128],
                                                                                WIN[:, c, 512:768], start=(c == 0), stop=(c == DC - 1)),
                                      reads=["HTall"] + WIN_K, writes=[vk_])
                            S.add("act", lambda e, tt=tt, vb_=vb_: e.copy(V1A[:, tt, :, 0:128], vb_[:, 0:256].rearrange("p (a b) -> p a b", a=2)),
                                  reads=[vk_], writes=["V1A"])

                        def aB(tt):
                            bank, bk = pjb[tt % 2]
                            k = tt % 2
                            vfa = (lambda v: v.rearrange("p (c h) d -> p c h d", c=2))
                            outs_a = [(0, 4, QKBs[k][:, 0:256].rearrange("p (h c d) -> p c h d", h=2, c=2), vfa),
                                      (4, 8, QKBs[k][:, 256:512].rearrange("p (h c d) -> p c h d", h=2, c=2), vfa)]
                            prepB(k, bank[:, 0:512].rearrange("p (a b) -> p a b", a=8), 8, QKG,
                                  (ROPE[:, 0, 0, tt, :], ROPE[:, 0, 1, tt, :]), outs_a, [bk], [("QKB", k)], "HG")
                            for c4 in range(4):
                                S.add("pe", lambda e, c4=c4, k=k: e.transpose(TP[:, 4 * k + c4, :], QKBs[k][:, c4 * 128:(c4 + 1) * 128], IDENT[:]),
                                      reads=[("QKB", k), "IDENT"], writes=[("TP", k), "TPB"])
                            S.add("act", lambda e, tt=tt, k=k: e.copy(QTB[:, 0:2, tt * 128:(tt + 1) * 128], TP[:, 4 * k:4 * k + 2, :]),
                                  reads=[("TP", k)], writes=[("QTB", 0), ("QTB", 1), "TPB"])
                            S.add("act", lambda e, tt=tt, k=k: e.copy(KTB[:, 0:2, tt * 128:(tt + 1) * 128], TP[:, 4 * k + 2:4 * k + 4, :]),
                                  reads=[("TP", k)], writes=[("KTB", 0), ("KTB", 1), "TPB"])

                        skew(TT, aA, aB)
                        if hp + 1 < 4:
                            load_pair(hp + 1)
                        if hp == 0:
                            S.add("pool", lambda e: e.memset(V1A[:, :, :, 128:129], 1.0), writes=["V1A"])
                        aits = [(hl, qc, comp, kp) for hl in range(2) for qc in range(4) for comp in range(2) for kp in range(8)]

                        def a_score(i):
                            hl, qc, comp, kp = aits[i]
                            r0 = comp * 64
                            sc, sk = (SCA, SCA_K) if i % 2 == 0 else (SCB, SCB_K)
                            et, ek = (ET0, "ET0") if i % 2 == 0 else (ET1, "ET1")
                            for k2 in range(2):
                                kt = 2 * kp + k2
                                S.add("pe", lambda e, sc=sc, k2=k2, kt=kt, r0=r0, hl=hl, qc=qc: e.matmul(
                                    sc[:, k2, :], KTB[r0:r0 + 64, hl, kt * 128:(kt + 1) * 128],
                                    QTB[r0:r0 + 64, hl, qc * 512:(qc + 1) * 512], start=True, stop=True),
                                    reads=[("KTB", hl), ("QTB", hl)], writes=sk)
                            S.add("act", lambda e, sc=sc, et=et: e.activation(et[:, :, :], sc[:, :, :], AF.Exp, scale=0.125),
                                  reads=sk, writes=[ek])

                        def a_av(i):
                            hl, qc, comp, kp = aits[i]
                            h = 2 * hp + hl
                            et, ek = (ET0, "ET0") if i % 2 == 0 else (ET1, "ET1")
                            for k2 in range(2):
                                kt = 2 * kp + k2
                                for qt in range(4):
                                    bank, off = (0, qt * 129) if qt < 3 else (1, 0)
                                    S.add("pe", lambda e, et=et, k2=k2, kt=kt, qt=qt, bank=bank, off=off, hl=hl: e.matmul(
                                        ACC[:, bank, off:off + 129], et[:, k2, qt * 128:(qt + 1) * 128], V1A[:, kt, hl, :],
                                        start=(kt == 0 and qt in (0, 3)), stop=(kt == 15), skip_group_check=True),
                                        reads=[ek, "V1A"], writes=ACC_K)
                            if kp != 7:
                                return
                            S.add("dve", lambda e: e.tensor_copy(SQ[:, 0:387], ACC[:, 0, 0:387]), reads=[("ACC", 0)], writes=["SQ"])
                            S.add("dve", lambda e: e.tensor_copy(QN[:, 256:385], ACC[:, 1, 0:129]), reads=[("ACC", 1)], writes=["QN"])
                            a3 = SQ[:, 0:387].rearrange("p (a b) -> p a b", a=3)
                            a1 = QN[:, 256:385]
                            rr = R0 if comp == 0 else R1
                            S.add("dve", lambda e, rr=rr, a3=a3: e.reciprocal(rr[:, 0:3], a3[:, :, 128]), reads=["SQ"], writes=["EPI"])
                            S.add("dve", lambda e, rr=rr, a1=a1: e.reciprocal(rr[:, 3:4], a1[:, 128:129]), reads=["QN"], writes=["EPI"])
                            if comp == 0:
                                S.add("dve", lambda e, a3=a3: e.tensor_tensor(A0[:, 0:3, :], a3[:, :, 0:128],
                                                                          R0[:, 0:3].unsqueeze(2).broadcast_to([128, 3, 128]), ALU.mult),
                                      reads=["SQ", "EPI"], writes=["A0"])
                                S.add("dve", lambda e, a1=a1: e.tensor_scalar(A0[:, 3, :], a1[:, 0:128], R0[:, 3:4], None, ALU.mult),
                                      reads=["QN", "EPI"], writes=["A0"])
                                return
                            S.add("dve", lambda e: e.tensor_scalar(R1[:, :], R1[:, :], NEGLAM, None, ALU.mult),
                                  reads=["EPI", "LAM"], writes=["EPI"])
                            for qt in range(4):
                                src_ = a3[:, qt, 0:128] if qt < 3 else a1[:, 0:128]
                                S.add("dve", lambda e, qt=qt, src_=src_: e.scalar_tensor_tensor(
                                    A0[:, qt, :], src_, R1[:, qt:qt + 1], A0[:, qt, :], ALU.mult, ALU.add),
                                    reads=["SQ", "QN", "EPI", "A0"], writes=["A0"])
                            for qt in range(4):
                                S.add("dve", lambda e, qt=qt: e.scalar_tensor_tensor(
                                    QN[:, 0:128], A0[:, qt, :], 1.0, A0[:, qt, :], ALU.mult, ALU.mult, accum_out=SSE[:, qt:qt + 1]),
                                    reads=["A0"], writes=["QN", ("SSE", qt)])
                            rsqrt_pool(RSE[:, :], SSE[:, :], 4, 1.0 / 128, [("SSE", q) for q in range(4)], ["RSE"])
                            for qt in range(4):
                                S.add("dve", lambda e, qt=qt: e.scalar_tensor_tensor(
                                    MB[:, qt, :], A0[:, qt, :], RSE[:, qt:qt + 1], SLGS[:], ALU.mult, ALU.mult),
                                    reads=["A0", "RSE", "SLGS"], writes=["MB"])
                            k = (i // 16) % 2
                            for qt in range(4):
                                S.add("pe", lambda e, qt=qt, k=k: e.transpose(TP[:, 4 * k + qt, :], MB[:, qt, :], IDENT[:]),
                                      reads=["MB", "IDENT"], writes=[("TP", k), "TPB"])
                            S.add("act", lambda e, qc=qc, h=h, k=k: e.copy(
                                MIXT[:, h, qc * 512:(qc + 1) * 512], TP[:, 4 * k:4 * k + 4, :].rearrange("p a b -> p (a b)")),
                                reads=[("TP", k)], writes=[mixkey(h), "TPB"])

                        a_score(0)
                        for i in range(len(aits)):
                            if i + 1 < len(aits):
                                a_score(i + 1)
                            a_av(i)
                else:
                    BQKG = HG[:, 0:768].rearrange("p (g a b) -> p g a b", g=3, a=4)
                    abanks = [(ACC, 0, ("ACC", 0)), (ACC, 1, ("ACC", 1)), (SCB, 0, ("SCB", 0))]
                    it = 0
                    for hp in range(4):
                        for g, (window, dil) in enumerate(B_GROUPS):
                            gb = g % 2
                            wo_ = gb * 384
                            if not (hp == 0 and g == 0):
                                load_bround(hp, g)
                            L = SEQ // dil
                            nst = L // 128
                            pjb_b = [(PJ[:, :], "PJ"), (SCB[:, 1, :], ("SCB", 1))]
                            sqb_b = [(SCA[:, 0, :], ("SCA", 0)), (SCA[:, 1, :], ("SCA", 1))]

                            def bA(tj, g=g, gb=gb, dil=dil, nst=nst, wo_=wo_):
                                r, i0 = tj // nst, (tj % nst) * 128
                                lo = r + dil * i0
                                bank, bk = pjb_b[tj % 2]
                                for c in range(DC):
                                    S.add("pe", lambda e, c=c, lo=lo, bank=bank: e.matmul(
                                        bank[:, 0:384], HT[:, c, lo:lo + dil * 127 + 1:dil], WIN[:, c, wo_:wo_ + 384],
                                        start=(c == 0), stop=(c == DC - 1)),
                                        reads=["HTall", ("WIN", gb)], writes=[bk])
                                prepA(tj % 2, sqb_b[tj % 2][0], sqb_b[tj % 2][1], bank[:, 0:256].rearrange("p (a b) -> p a b", a=4), 4, [bk])
                                S.add("act", lambda e, tj=tj, bank=bank: e.copy(VB[:, gb, tj, :, 0:64],
                                                                             bank[:, 256:384].rearrange("p (a b) -> p a b", a=2)),
                                      reads=[bk], writes=[("V1A", gb)])

                            def bB(tj, g=g, gb=gb):
                                bank, bk = pjb_b[tj % 2]
                                k = tj % 2
                                prepB(k, bank[:, 0:256].rearrange("p (a b) -> p a b", a=4), 4, BQKG[:, g],
                                      (ROPE[:, g, 0, tj, :], ROPE[:, g, 1, tj, :]),
                                      [(0, 4, QKBs[k][:, 0:256].rearrange("p (a b) -> p a b", a=4), ident_v)], [bk], [("QKB", k)], "HG")
                                for c2 in range(2):
                                    S.add("pe", lambda e, c2=c2, k=k: e.transpose(TP[:, 4 * k + c2, :], QKBs[k][:, c2 * 128:(c2 + 1) * 128], IDENT[:]),
                                          reads=[("QKB", k), "IDENT"], writes=[("TP", k), "TPB"])
                                S.add("act", lambda e, tj=tj, k=k: e.copy(QTB[:, gb, tj * 128:(tj + 1) * 128], TP[:, 4 * k, :]),
                                      reads=[("TP", k)], writes=[("QTB", gb), "TPB"])
                                S.add("act", lambda e, tj=tj, k=k: e.copy(KTB[:, gb, tj * 128:(tj + 1) * 128], TP[:, 4 * k + 1, :]),
                                      reads=[("TP", k)], writes=[("KTB", gb), "TPB"])

                            skew(TT, bA, bB)
                            S.add("pool", lambda e, gb=gb: e.memset(VB[:, gb, :, :, 64:65], 1.0), writes=[("V1A", gb)])
                            bits = []
                            for hl in range(2):
                                started = set()
                                for tj in range(TT):
                                    seg, lj = tj // nst, tj % nst
                                    qlo, qhi = max(lj - 1, 0), min(lj + 1, nst - 1)
                                    firsts = []
                                    for qi in range(qlo, qhi + 1):
                                        slot = seg * nst + qi
                                        firsts.append((slot // 7) not in started)
                                        started.add(slot // 7)
                                    bits.append((hl, tj, seg, lj, qlo, qhi, firsts))

                            etb = [(ET0[:, 0, :], ("ETB", 0)), (ET0[:, 1, :], ("ETB", 1)), (ET1[:, 0, :], ("ETB", 2)), (ET1[:, 1, :], ("ETB", 3))]

                            def b_score(i, gb=gb, nst=nst):
                                hl, tj, seg, lj, qlo, qhi, firsts = bits[i]
                                r0 = hl * 64
                                n = (qhi - qlo + 1) * 128
                                m0 = (qlo - lj + 1) * 128
                                q0 = (seg * nst + qlo) * 128
                                k2 = i % 2
                                et, ek = etb[i % 4]
                                S.add("pe", lambda e: e.matmul(
                                    SCA[:, k2, 0:n], KTB[r0:r0 + 64, gb, tj * 128:(tj + 1) * 128], QTB[r0:r0 + 64, gb, q0:q0 + n],
                                    start=True, stop=True),
                                    reads=[("KTB", gb), ("QTB", gb)], writes=[("SCA", k2)])
                                S.add("act", lambda e: e.activation(et[:, 0:n], SCA[:, k2, 0:n], AF.Exp, scale=0.125),
                                      reads=[("SCA", k2)], writes=[ek])
                                S.add("dve", lambda e: e.tensor_tensor(et[:, 0:n], et[:, 0:n], MASK[:, m0:m0 + n], ALU.mult),
                                      reads=[ek, "MASK"], writes=[ek])

                            def b_av(i, g=g, gb=gb, nst=nst):
                                hl, tj, seg, lj, qlo, qhi, firsts = bits[i]
                                et, ek = etb[i % 4]
                                for n_, qi in enumerate(range(qlo, qhi + 1)):
                                    slot = seg * nst + qi
                                    bt, bi, bk = abanks[slot // 7]
                                    off = (slot % 7) * 65
                                    first = firsts[n_]
                                    S.add("pe", lambda e, qi=qi, bt=bt, bi=bi, off=off, first=first: e.matmul(
                                        bt[:, bi, off:off + 65], et[:, (qi - qlo) * 128:(qi - qlo + 1) * 128], VB[:, gb, tj, hl, :],
                                        start=first, stop=True, skip_group_check=True),
                                        reads=[ek, ("V1A", gb)], writes=[bk])
                                if tj != TT - 1:
                                    return
                                for b3 in range(3):
                                    bt, bi, bk = abanks[b3]
                                    ns = 7 if b3 < 2 else 2
                                    av = bt[:, bi, 0:ns * 65].rearrange("p (a b) -> p a b", a=ns)
                                    S.add("dve", lambda e, av=av, b3=b3, ns=ns: e.tensor_copy(
                                        NUMB[:, g, b3 * 7:b3 * 7 + ns, hl * 64:(hl + 1) * 64], av[:, :, 0:64]),
                                        reads=[bk], writes=["NUMB"])
                                    S.add("dve", lambda e, av=av, b3=b3, ns=ns: e.tensor_copy(
                                        DENF[:, g, b3 * 7:b3 * 7 + ns, hl], av[:, :, 64]),
                                        reads=[bk], writes=["DENF"])

                            b_score(0)
                            b_score(1)
                            for i in range(len(bits)):
                                if i + 2 < len(bits):
                                    b_score(i + 2)
                                b_av(i)
                        for w in range(4):
                            mm = []
                            for jj in range(4):
                                mm.append((0, 4 * w + jj, slice(jj * 128, (jj + 1) * 128), slice(0, 128)))
                            for r in range(4):
                                mm.append((1, r * 4 + w, slice(r, 512, 4), slice(0, 128)))
                            for r in range(16):
                                mm.append((2, r, slice(r, 512, 16), slice(32 * w, 32 * w + 32)))
                            for n_, (g, tj, osl, isl) in enumerate(mm):
                                S.add("pe", lambda e, g=g, tj=tj, osl=osl, isl=isl, n_=n_: e.matmul(
                                    PJ[:, osl], NUMB[:, g, tj, :], IDENT[:, isl], start=(n_ == 0), stop=(n_ == len(mm) - 1),
                                    skip_group_check=True),
                                    reads=["NUMB", "IDENT"], writes=["PJ"])
                            for n_, (g, tj, osl, isl) in enumerate(mm):
                                S.add("pe", lambda e, g=g, tj=tj, osl=osl, isl=isl, n_=n_: e.matmul(
                                    SCB[0:2, 1, osl], DENF[:, g, tj, :], IDENTF[:, isl], start=(n_ == 0), stop=(n_ == len(mm) - 1),
                                    skip_group_check=True),
                                    reads=["DENF", "IDENTF"], writes=[("SCB", 1)])
                            S.add("dve", lambda e: e.reciprocal(RDEN[:, :], SCB[0:2, 1, :]), reads=[("SCB", 1)], writes=["RDEN"])
                            S.add("pe", lambda e: e.matmul(SCA[:, 0, :], SEL[:, :], RDEN[:, :], start=True, stop=True),
                                  reads=["SEL", "RDEN"], writes=[("SCA", 0)])
                            S.add("act", lambda e: e.copy(RDB[:, :], SCA[:, 0, :]), reads=[("SCA", 0)], writes=["RDB"])
                            S.add("dve", lambda e, w=w, hp=hp: e.tensor_tensor(MIXT[:, hp, w * 512:(w + 1) * 512], PJ[:, :], RDB[:, :], ALU.mult),
                                  reads=["PJ", "RDB"], writes=[("MIXT", hp)])

                wo_d = (awout_d if is_a else bwout_d)[j].rearrange("(c p) n -> p c n", p=128)
                load_w(WO[:, 0:nch, :], wo_d, "WO")
                for tt in range(TT):
                    for half in range(2):
                        for c in range(nch):
                            S.add("pe", lambda e, tt=tt, half=half, c=c, nch=nch: e.matmul(
                                ACC[:, half, :], MIXT[:, c, tt * 128:(tt + 1) * 128], WO[:, c, half * 512:(half + 1) * 512],
                                start=(c == 0), stop=(c == nch - 1)),
                                reads=[mixkey(c), "WO"], writes=[("ACC", half)])
                    S.add("dve", lambda e, tt=tt: e.tensor_tensor(X[:, tt, :], X[:, tt, :], ACC[:, :, :].rearrange("p a b -> p (a b)"), ALU.add),
                          reads=[("X", tt), ("ACC", 0), ("ACC", 1)], writes=[("X", tt)])

                S.dma(lambda e, li=li: e.dma_start(out=FGAIN, in_=gains_d[li, 1]), writes=["FGAIN"])
                norm_to_HT(X, TT, "FGAIN", [FHB0, FHB1], FJK, "FJK", ["FHB0", "FHB1"], HT, "HT", "X", FGAIN)
                mark_ht_ready()
                wu_d = wup_d[li].rearrange("(c p) n -> p c n", p=128)
                wd_d = wdown_d[li]
                WUs = [(WU0, "WU0"), (WU1, "WU1")]
                Gs = [(G0, "G0"), (G1, "G1")]
                S.add("pool", lambda e: e.memset(U[:, 0:1], 0.0), writes=["U"])
                S.add("pool", lambda e: e.memset(U[:, SEQ + 1:SEQ + 2], 0.0), writes=["U"])
                upbanks = [(PJ[:, :], "PJ"), (SCA[:, 0, :], ("SCA", 0)), (SCA[:, 1, :], ("SCA", 1))]
                dnbanks = [(ACC, ACC_K), (SCB, SCB_K)]
                ub = [0]
                db = [0]

                def load_up(gi):
                    fc0, n = FFN_GROUPS[gi]
                    wu, wk = WUs[gi % 2]
                    load_w(wu[:, :, 0, 0:n * 128], wu_d[:, :, fc0 * 128:(fc0 + n) * 128], wk)
                    load_w(wu[:, :, 1, 0:n * 128], wu_d[:, :, DFF + fc0 * 128:DFF + (fc0 + n) * 128], wk)

                def up(gi):
                    fc0, n = FFN_GROUPS[gi]
                    wu, wk = WUs[gi % 2]
                    gt, gk = Gs[gi % 2]
                    for l in range(n):
                        for ab in range(2):
                            ch = ab * NFC + fc0 + l
                            for tq in range(4):
                                bank, bkey = upbanks[ub[0] % 3]
                                ub[0] += 1
                                for c in range(DC):
                                    S.add("pe", lambda e, c=c, bank=bank, wu=wu, ab=ab, l=l, tq=tq: e.matmul(
                                        bank, wu[:, c, ab, l * 128:(l + 1) * 128], HT[:, c, tq * 512:(tq + 1) * 512],
                                        start=(c == 0), stop=(c == DC - 1)),
                                        reads=["HTall", wk], writes=[bkey])
                                S.add("act", lambda e, bank=bank, tq=tq: e.copy(U[:, 1 + tq * 512:1 + (tq + 1) * 512], bank),
                                      reads=[bkey], writes=["U"])
                            Cc, ck = (CA, "CA") if ab == 0 else (CB, "CB")
                            S.add("dve", lambda e, Cc=Cc, ch=ch: e.tensor_scalar(Cc[:, :], U[:, 1:SEQ + 1], CW[:, ch, 1:2], CW[:, ch, 3:4],
                                                                             ALU.mult, ALU.add),
                                  reads=["U", "CW"], writes=[ck])
                            S.add("dve", lambda e, Cc=Cc, ch=ch: e.scalar_tensor_tensor(Cc[:, :], U[:, 0:SEQ], CW[:, ch, 0:1], Cc[:, :],
                                                                                    ALU.mult, ALU.add),
                                  reads=["U", "CW", ck], writes=[ck])
                            S.add("dve", lambda e, Cc=Cc, ch=ch: e.scalar_tensor_tensor(Cc[:, :], U[:, 2:SEQ + 2], CW[:, ch, 2:3], Cc[:, :],
                                                                                    ALU.mult, ALU.add),
                                  reads=["U", "CW", ck], writes=[ck])
                            if ab == 0:
                                S.add("act", lambda e: e.activation(CA[:, :], CA[:, :], AF.Silu), reads=["CA"], writes=["CA"])
                            else:
                                S.add("pool", lambda e, gt=gt, l=l: e.tensor_tensor(gt[:, l, :], CA[:, :], CB[:, :], ALU.mult),
                                      reads=["CA", "CB"], writes=[gk])

                def load_down(gi):
                    fc0, n = FFN_GROUPS[gi]
                    load_w(WD[:, 0:n, :], wd_d[fc0 * 128:(fc0 + n) * 128, :].rearrange("(c p) n -> p c n", p=128), "WD")

                def down(gi):
                    fc0, n = FFN_GROUPS[gi]
                    gt, gk = Gs[gi % 2]
                    for tt in range(TT):
                        bt, bkey = dnbanks[db[0] % 2]
                        db[0] += 1
                        for half in range(2):
                            for l in range(n):
                                S.add("pe", lambda e, bt=bt, half=half, l=l, tt=tt, gt=gt: e.matmul(
                                    bt[:, half, :], gt[:, l, tt * 128:(tt + 1) * 128], WD[:, l, half * 512:(half + 1) * 512],
                                    start=(l == 0), stop=(l == n - 1)),
                                    reads=[gk, "WD"], writes=bkey)
                        S.add("dve", lambda e, tt=tt, bt=bt: e.tensor_tensor(X[:, tt, :], X[:, tt, :], bt[:, :, :].rearrange("p a b -> p (a b)"), ALU.add),
                              reads=[("X", tt)] + bkey, writes=[("X", tt)])

                ng = len(FFN_GROUPS)
                load_up(0)
                load_up(1)
                up(0)
                load_down(0)
                for gi in range(1, ng):
                    up(gi)
                    if gi + 1 < ng:
                        load_up(gi + 1)
                    down(gi - 1)
                    load_down(gi)
                down(ng - 1)
            for q4 in range(4):
                S.dma(lambda e, s=s, q4=q4: e.dma_start(
                    out=y_d[s, q4 * 512:(q4 + 1) * 512, :].rearrange("(t p) d -> p t d", p=128),
                    in_=X[:, q4 * 4:(q4 + 1) * 4, :]),
                    reads=[("X", t) for t in range(q4 * 4, q4 * 4 + 4)], is_output=True)
        S.emit()
    return nc


def _const_tables():
    rot = 16
    half = 8
    inv = (np.float32(500000.0) ** (-(np.arange(half, dtype=np.float32) * np.float32(2.0) / np.float32(rot)))).astype(np.float32)
    rope = np.zeros((3, 2, 128, TT, 16), np.float32)
    for g, (window, dil) in enumerate(B_GROUPS):
        L = SEQ // dil
        nst = L // 128
        for tj in range(TT):
            r, i0 = tj // nst, (tj % nst) * 128
            pos = (r + dil * (i0 + np.arange(128))).astype(np.float32)
            ang = (pos[:, None] * inv[None, :]).astype(np.float32)
            cs_, sn_ = np.cos(ang), np.sin(ang)
            rope[g, 0, :, tj, :] = np.concatenate([cs_, cs_], axis=1)
            rope[g, 1, :, tj, :] = np.concatenate([-sn_, sn_], axis=1)
    rope = rope.reshape(3, 2, 128, TT * 16)
    ident = np.eye(128, dtype=np.float32)
    k = np.arange(128)[:, None]
    c = np.arange(384)[None, :]
    rel = (c // 128 - 1) * 128 + (c % 128) - k
    mask = np.where(np.abs(rel) <= 64, 1.0, 0.0).astype(np.float32)
    sel = np.zeros((2, 128), np.float32)
    sel[0, :64] = 1.0
    sel[1, 64:] = 1.0
    return rope, ident, mask, sel


def _prep_shared(inp):
    f = lambda a: np.ascontiguousarray(np.asarray(a, dtype=np.float32))
    gains = np.zeros((4, 3, 128, D), np.float32)
    hg = np.zeros((4, 128, 1664), np.float32)
    cw = np.zeros((4, 128, 44, 4), np.float32)
    for i in range(4):
        j = i // 2
        gains[i, 0] = np.broadcast_to(f(inp["norm_mix"])[i][None, :], (128, D))
        gains[i, 1] = np.broadcast_to(f(inp["norm_ffn"])[i][None, :], (128, D))
        gains[i, 2] = np.broadcast_to(f(inp["norm_mem"])[i][None, :], (128, D))
        row = np.zeros(1664, np.float32)
        if i % 2 == 0:
            qg = f(inp["a_q_norm"])[j]
            kg = f(inp["a_k_norm"])[j]
            row[0:512] = np.concatenate([qg] * 4 + [kg] * 4)
            row[1280:1408] = f(inp["a_subln"])[j]
            row[1408:1664] = f(inp["a_lambda"])[j].reshape(-1)
        else:
            for g in range(3):
                qg = f(inp["b_q_norm"])[j, g]
                kg = f(inp["b_k_norm"])[j, g]
                row[g * 256:(g + 1) * 256] = np.concatenate([qg, qg, kg, kg])
        row[768:1024] = np.tile(f(inp["xq_norm"])[i], 4)
        row[1024:1280] = np.tile(f(inp["xk_norm"])[i], 4)
        hg[i] = np.broadcast_to(row[None, :], (128, 1664))
        cwi = f(inp["conv_w"])[i].reshape(3, 44, 128)
        cbi = f(inp["conv_b"])[i].reshape(44, 128)
        cw[i, :, :, 0:3] = cwi.transpose(2, 1, 0)
        cw[i, :, :, 3] = cbi.T
    rope, ident, mask, sel = _const_tables()
    shared = {
        "w_mem_kv": f(inp["w_mem_kv"]), "a_w_in": f(inp["a_w_in"]), "a_w_out": f(inp["a_w_out"]),
        "b_w_in": f(inp["b_w_in"]), "b_w_out": f(inp["b_w_out"]), "w_up": f(inp["w_up"]), "w_down": f(inp["w_down"]),
        "gains": gains, "hg": hg, "cw": cw.reshape(4, 128, 176), "rope": rope, "ident": ident, "mask": mask, "sel": sel,
    }
    return shared


_PROGRAM_CACHE = {}


def _get_program(nseq, layers):
    key = (nseq, tuple(layers))
    if key not in _PROGRAM_CACHE:
        _PROGRAM_CACHE[key] = build_program(nseq, list(layers))
    return _PROGRAM_CACHE[key]


def kernel(**inp):
    xp = np.asarray(inp["x_prompt"], dtype=np.float32)
    xs = np.asarray(inp["x_sample"], dtype=np.float32)
    mp = np.asarray(inp["mem_prompt"], dtype=np.float32)
    ms = np.asarray(inp["mem_sample"], dtype=np.float32)
    x_all = np.concatenate([xp, xs], axis=0)
    m_all = np.concatenate([mp, ms], axis=0)
    nb = xp.shape[0]
    shared = _prep_shared(inp)
    nc = _get_program(SEQ_PER_CORE, (0, 1, 2, 3))
    in_maps = []
    for c in range(N_CORES):
        d = dict(shared)
        d["x"] = np.ascontiguousarray(x_all[c * SEQ_PER_CORE:(c + 1) * SEQ_PER_CORE])
        d["mem"] = np.ascontiguousarray(m_all[c * SEQ_PER_CORE:(c + 1) * SEQ_PER_CORE])
        in_maps.append(d)
    res = run_bass_kernel_spmd(nc, in_maps, core_ids=list(range(N_CORES)))
    y = np.concatenate([np.asarray(r["y"], dtype=np.float32) for r in res.results], axis=0)
    return (y[:nb], y[nb:])
```

```python
import contextlib
import numpy as np
import ml_dtypes
import concourse.bass as bass
import concourse.mybir as mybir
from concourse.bass_utils import run_bass_kernel_spmd

F32 = mybir.dt.float32
BF16 = mybir.dt.bfloat16
AF = mybir.ActivationFunctionType
ALU = mybir.AluOpType
AX = mybir.AxisListType

ENGS = ("pe", "act", "dve", "pool", "sp")
EPS = 1e-6


class Op:
    __slots__ = ("eng", "fn", "deps", "flag", "cnt", "idx", "dma", "dsem", "dval", "dprev")

    def __init__(self, eng, fn):
        self.eng = eng
        self.fn = fn
        self.deps = []
        self.flag = False
        self.cnt = 0
        self.idx = 0
        self.dma = False
        self.dsem = None
        self.dval = 0
        self.dprev = None


def _base(k):
    return k[0] if isinstance(k, tuple) else k


class Sched:
    def __init__(self, nc, n_dma_sems=32):
        self.nc = nc
        self.ops = {e: [] for e in ENGS}
        self.last_w = {}
        self.readers = {}
        self.by_base = {}
        self.alias = {}
        self.n_dma_sems = n_dma_sems
        self.dma_rr = 0
        self.dma_rr_sw = 0
        self.dma_cnt = [0] * n_dma_sems
        self.dma_last = [None] * n_dma_sems
        self.all_dma_out = []

    def set_alias(self, a, b):
        self.alias.setdefault(a, set()).add(b)
        self.alias.setdefault(b, set()).add(a)

    def _deps(self, op, reads, writes):
        deps = {}

        def add(d):
            if d is None or d is op:
                return
            if d.dma:
                deps[("dma", id(d))] = d
                return
            if d.eng == op.eng:
                if op.eng == "pe":
                    return
                if op.idx - d.idx > 2:
                    return
            k = d.eng
            if k not in deps or deps[k].idx < d.idx:
                deps[k] = d

        for r in reads:
            add(self.last_w.get(r))
            al = self.alias.get(_base(r))
            if al:
                for ab in al:
                    for k2 in self.by_base.get(ab, ()):
                        add(self.last_w.get(k2))
        for w in writes:
            add(self.last_w.get(w))
            for rd in self.readers.get(w, {}).values():
                add(rd)
            al = self.alias.get(_base(w))
            if al:
                for ab in al:
                    for k2 in self.by_base.get(ab, ()):
                        add(self.last_w.get(k2))
                        for rd in self.readers.get(k2, {}).values():
                            add(rd)
        rk = ("dma", id(op)) if op.dma else op.eng
        for r in reads:
            self.readers.setdefault(r, {})[rk] = op
            self.by_base.setdefault(_base(r), set()).add(r)
        for w in writes:
            self.last_w[w] = op
            self.readers[w] = {}
            self.by_base.setdefault(_base(w), set()).add(w)
        op.deps = list(deps.values())
        for d in op.deps:
            d.flag = True

    PSUM_BASES = ("SCA", "SCB", "ACC", "PJ", "TP")

    def add(self, eng, fn, reads=(), writes=()):
        op = Op(eng, fn)
        op.idx = len(self.ops[eng])
        rl = [("rl", r) for r in reads if _base(r) in self.PSUM_BASES]
        if rl:
            writes = list(writes) + rl
        self._deps(op, reads, writes)
        self.ops[eng].append(op)
        return op

    def dma(self, fn, reads=(), writes=(), queue="sp", is_output=False):
        op = Op(queue, fn)
        op.dma = True
        op.idx = len(self.ops[queue])
        half = self.n_dma_sems // 2
        if queue == "pool":
            s = half + self.dma_rr_sw
            self.dma_rr_sw = (self.dma_rr_sw + 1) % (self.n_dma_sems - half)
        else:
            s = self.dma_rr
            self.dma_rr = (self.dma_rr + 1) % half
        self.dma_cnt[s] += 1
        op.dsem = s
        op.dval = 16 * self.dma_cnt[s]
        op.dprev = self.dma_last[s]
        self.dma_last[s] = op
        self._deps(op, reads, writes)
        self.ops[queue].append(op)
        if is_output:
            self.all_dma_out.append(op)
        return op

    def emit(self):
        nc = self.nc
        for e in ENGS:
            c = 0
            for op in self.ops[e]:
                if op.dma:
                    continue
                if op.flag:
                    c += 1
                    op.cnt = c
        with contextlib.ExitStack() as st:
            esem = {e: st.enter_context(nc.semaphore("s_" + e)) for e in ENGS}
            dsem = [st.enter_context(nc.semaphore("d_%d" % i)) for i in range(self.n_dma_sems)]
            block = st.enter_context(nc.Block())

            def run(e, eng):
                seen = {}
                for op in self.ops[e]:
                    waits = []
                    for d in op.deps:
                        if d.dma:
                            key = ("d", d.dsem)
                            if seen.get(key, 0) < d.dval:
                                seen[key] = d.dval
                                waits.append((dsem[d.dsem], d.dval))
                        else:
                            key = ("e", d.eng)
                            if seen.get(key, 0) < d.cnt:
                                seen[key] = d.cnt
                                waits.append((esem[d.eng], d.cnt))
                    if op.dma and op.dprev is not None:
                        key = ("d", op.dsem)
                        if seen.get(key, 0) < op.dprev.dval:
                            seen[key] = op.dprev.dval
                            waits.append((dsem[op.dsem], op.dprev.dval))
                    for (s, v) in waits:
                        eng.wait_ge(s, v)
                    ins = op.fn(eng)
                    if op.dma:
                        ins.then_inc(dsem[op.dsem], 16)
                    elif op.flag:
                        ins.then_inc(esem[e], 1)
                if e == "sp":
                    fin = {}
                    for op in self.all_dma_out:
                        fin[op.dsem] = max(fin.get(op.dsem, 0), op.dval)
                    for s, v in fin.items():
                        eng.wait_ge(dsem[s], v)

            @block.tensor
            def _(t):
                run("pe", t)

            @block.scalar
            def _(a):
                run("act", a)

            @block.vector
            def _(v):
                run("dve", v)

            @block.gpsimd
            def _(g):
                run("pool", g)

            @block.sync
            def _(s):
                run("sp", s)


D = 1024
SEQ = 2048
TT = 16
DC = 8
NMEM = 256
DFF = 2816
NFC = 22
B_GROUPS = ((128, 1), (512, 4), (2048, 16))
N_CORES = 8
SEQ_PER_CORE = 5
FFN_GROUPS = [(0, 3), (3, 3), (6, 3), (9, 3), (12, 3), (15, 3), (18, 3), (21, 1)]


def build_program(nseq, layers):
    nc = bass.Bass("TRN2", target_bir_lowering=False)

    def din(name, shape, dt=F32):
        return nc.dram_tensor(name, list(shape), dt, kind="ExternalInput").ap()

    x_d = din("x", [nseq, SEQ, D])
    mem_d = din("mem", [nseq, NMEM, D])
    y_d = nc.dram_tensor("y", [nseq, SEQ, D], F32, kind="ExternalOutput").ap()
    wkv_d = din("w_mem_kv", [4, D, 512])
    awin_d = din("a_w_in", [2, D, 3328])
    awout_d = din("a_w_out", [2, 1280, D])
    bwin_d = din("b_w_in", [2, D, 4864])
    bwout_d = din("b_w_out", [2, 768, D])
    wup_d = din("w_up", [4, D, 2 * DFF])
    wdown_d = din("w_down", [4, DFF, D])
    gains_d = din("gains", [4, 3, 128, D])
    hg_d = din("hg", [4, 128, 1664])
    cw_d = din("cw", [4, 128, 44 * 4])
    rope_d = din("rope", [3, 2, 128, TT * 16])
    ident_d = din("ident", [128, 128])
    mask_d = din("mask", [128, 384])
    sel_d = din("sel", [2, 128])

    with contextlib.ExitStack() as st:
        def sb(name, shape, dt):
            return st.enter_context(nc.sbuf_tensor(name, list(shape), dt))

        def ps(name, shape, dt):
            return st.enter_context(nc.psum_tensor(name, list(shape), dt))

        S = Sched(nc)
        X = sb("X", [128, TT, D], F32)
        HT = sb("HT", [128, DC, SEQ], BF16)
        HG = sb("HG", [128, 1664], F32)
        CW = sb("CW", [128, 44, 4], F32)
        ROPE = sb("ROPE", [128, 3, 2, TT, 16], F32)
        IDENTF = sb("IDENTF", [128, 128], F32)
        IDENT = sb("IDENT", [128, 128], BF16)
        MASK = sb("MASK", [128, 384], BF16)
        SEL = sb("SEL", [2, 128], F32)
        MKT = sb("MKT", [128, 2, 256], BF16)
        MV1 = sb("MV1", [128, 2, 4, 65], BF16)
        SS = sb("SS", [128, 16], F32)
        RS = sb("RS", [128, 16], F32)
        NH = sb("NH", [128, 16], F32)
        ST8 = sb("ST8", [128, 8], F32)
        RT8 = sb("RT8", [128, 8], F32)
        ST8b = sb("ST8b", [128, 8], F32)
        RT8b = sb("RT8b", [128, 8], F32)
        LAM = sb("LAM", [128, 8], F32)
        SLGS = sb("SLGS", [128, 128], F32)
        ARENA_ELEMS = 46400
        AR = sb("AR", [128, ARENA_ELEMS], BF16)
        cursor = {"attn": 0, "ffn": 0}

        def carve(phase, nelem_bf16, shape, dt):
            off = cursor[phase]
            n = int(nelem_bf16)
            n = (n + 15) // 16 * 16
            cursor[phase] = off + n
            assert cursor[phase] <= ARENA_ELEMS, (phase, cursor[phase])
            v = AR[:, off:off + int(nelem_bf16)]
            if dt == F32:
                v = v.bitcast(F32)
            if len(shape) == 2:
                return v
            if len(shape) == 3:
                return v.rearrange("p (a b) -> p a b", a=shape[1])
            if len(shape) == 4:
                return v.rearrange("p (a b c) -> p a b c", a=shape[1], b=shape[2])
            if len(shape) == 5:
                return v.rearrange("p (a b c d) -> p a b c d", a=shape[1], b=shape[2], c=shape[3])
            raise ValueError

        def cb(phase, shape):
            return carve(phase, int(np.prod(shape[1:])), shape, BF16)

        def cf(phase, shape):
            return carve(phase, 2 * int(np.prod(shape[1:])), shape, F32)

        mixt_off = cursor["attn"]
        MIXT = cb("attn", [128, 10, SEQ])
        WIN = cb("attn", [128, DC, 768])
        QTB = cb("attn", [128, 2, SEQ])
        KTB = cb("attn", [128, 2, SEQ])
        v_off = cursor["attn"]
        _V = cb("attn", [128, 4160])
        V1A = AR[:, v_off:v_off + TT * 2 * 129].rearrange("p (a b c) -> p a b c", a=TT, b=2)
        VB = AR[:, v_off:v_off + 2 * TT * 2 * 65].rearrange("p (g a b c) -> p g a b c", g=2, a=TT, b=2)
        sq_off = cursor["attn"]
        SQ = cf("attn", [128, 512])
        QN = cf("attn", [128, 512])
        GAIN = AR[:, sq_off:sq_off + 2048].bitcast(F32)
        QKB = cb("attn", [128, 512])
        QKB2 = cb("attn", [128, 512])
        RA = cf("attn", [128, 128])
        RB = cf("attn", [128, 128])
        et_off = cursor["attn"]
        ET0 = cb("attn", [128, 2, 512])
        ET1 = cb("attn", [128, 2, 512])
        JK = AR[:, et_off:et_off + 1024]
        HB0 = AR[:, et_off + 1024:et_off + 2048]
        a0_off = cursor["attn"]
        A0 = cf("attn", [128, 4, 128])
        HB1 = AR[:, a0_off:a0_off + 1024]
        R0 = cf("attn", [128, 4])
        R1 = cf("attn", [128, 4])
        SSE = cf("attn", [128, 4])
        RSE = cf("attn", [128, 4])
        MB = cb("attn", [128, 4, 128])
        attn_end = cursor["attn"]
        nb_off = mixt_off + 6 * SEQ
        NUMB = AR[:, nb_off:nb_off + 3 * TT * 128].rearrange("p (g a b) -> p g a b", g=3, a=TT)
        df_off = nb_off + 3 * TT * 128
        DENF = AR[:, df_off:df_off + 2 * 3 * TT * 2].bitcast(F32).rearrange("p (g a b) -> p g a b", g=3, a=TT)
        RDEN = QN[0:2, :]
        RDB = SQ
        MEMX = AR[:, mixt_off:mixt_off + 2 * SEQ].bitcast(F32).rearrange("p (a b) -> p a b", a=2)
        WO_off = mixt_off + 10 * SEQ
        WO = AR[:, WO_off:WO_off + 10 * D].rearrange("p (a b) -> p a b", a=10)
        assert 10 * D <= DC * 768 + 2 * SEQ
        WU0 = cb("ffn", [128, DC, 2, 384])
        WU1 = cb("ffn", [128, DC, 2, 384])
        WD = cb("ffn", [128, 3, D])
        G0 = cb("ffn", [128, 3, SEQ])
        G1 = cb("ffn", [128, 3, SEQ])
        u_off = cursor["ffn"]
        U = cf("ffn", [128, SEQ + 2])
        FGAIN = AR[:, u_off:u_off + 2048].bitcast(F32)
        ca_off = cursor["ffn"]
        CA = cf("ffn", [128, SEQ])
        FJK = AR[:, ca_off:ca_off + 1024]
        cb_off = cursor["ffn"]
        CB = cf("ffn", [128, SEQ])
        FHB0 = AR[:, cb_off:cb_off + 1024]
        FHB1 = AR[:, cb_off + 1024:cb_off + 2048]
        ATTN_KEYS = ["MIXT", "MIXH", "WIN", "QTB", "KTB", "V1A", "SQ", "QN", "QKB", "ROPESCR", "ET0", "ET1", "ETB", "EPI", "A0", "MB",
                     "JK", "HB0", "HB1", "NUMB", "DENF", "RDEN", "RDB", "MEMX", "WO", "GAIN", "SSE", "RSE"]
        FFN_KEYS = ["WU0", "WU1", "WD", "G0", "G1", "U", "CA", "CB", "FJK", "FHB0", "FHB1", "FGAIN"]
        for a_ in ATTN_KEYS:
            for b_ in FFN_KEYS:
                S.set_alias(a_, b_)
        for a_ in ("WIN", "QTB", "KTB"):
            S.set_alias("WO", a_)
        for a_, b_ in [("ETB", "ET0"), ("ETB", "ET1"), ("JK", "ETB"), ("HB0", "ETB"), ("GAIN", "SQ"), ("GAIN", "QN"), ("JK", "ET0"), ("HB0", "ET1"), ("HB1", "A0"), ("NUMB", "MIXH"),
                       ("DENF", "MIXH"), ("MEMX", "MIXT"), ("RDEN", "QN"), ("RDB", "SQ"),
                       ("FGAIN", "U"), ("FJK", "CA"), ("FHB0", "CB"), ("FHB1", "CB")]:
            S.set_alias(a_, b_)

        def mixkey(c):
            return ("MIXT", c) if c < 6 else ("MIXH", c)

        SCA = ps("SCA", [128, 2, 512], F32)
        SCB = ps("SCB", [128, 2, 512], F32)
        ACC = ps("ACC", [128, 2, 512], F32)
        PJ = ps("PJ", [128, 512], F32)
        TP = ps("TP", [128, 8, 128], BF16)

        SCA_K = [("SCA", 0), ("SCA", 1)]
        SCB_K = [("SCB", 0), ("SCB", 1)]
        ACC_K = [("ACC", 0), ("ACC", 1)]
        S.dma(lambda e: e.dma_start(out=IDENTF[:], in_=ident_d), writes=["IDENTF"])
        S.dma(lambda e: nc.gpsimd.dma_start(out=MASK[:], in_=mask_d), writes=["MASK"], queue="pool")
        S.dma(lambda e: e.dma_start(out=SEL[:], in_=sel_d), writes=["SEL"])
        S.dma(lambda e: e.dma_start(out=ROPE[:].rearrange("p a b c d -> p a b (c d)"),
                                    in_=rope_d.rearrange("a b p n -> p a b n")), writes=["ROPE"])
        S.add("dve", lambda e: e.tensor_copy(IDENT[:], IDENTF[:]), reads=["IDENTF"], writes=["IDENT"])
        S.add("pool", lambda e: e.memset(NH[:], -0.5), writes=["NH"])
        S.add("pool", lambda e: e.memset(MV1[:], 1.0), writes=["MV1"])

        def rsqrt_pool(dst, src, n, scale, rkeys, wkeys):
            S.add("pool", lambda e: e.tensor_scalar(dst, src, scale, EPS, ALU.mult, ALU.add), reads=rkeys, writes=wkeys)
            S.add("pool", lambda e: e.tensor_tensor(dst, dst, NH[:, 0:n], ALU.pow), reads=wkeys + ["NH"], writes=wkeys)

        WIN_K = [("WIN", 0), ("WIN", 1)]

        def load_w(dst, src, wkeys):
            if not isinstance(wkeys, list):
                wkeys = [wkeys]
            S.dma(lambda e: nc.gpsimd.dma_start(out=dst, in_=src), writes=wkeys, queue="pool")

        def norm_to_HT(src3, ntiles, gain_key, hbs, jk, jkkey, hbkeys, dstT, dstkey, xkey, gain_ap):
            for tt in range(ntiles):
                S.add("act", lambda e, tt=tt: e.activation(jk, src3[:, tt, :], AF.Square, accum_out=SS[:, tt:tt + 1]),
                      reads=[(xkey, tt)], writes=[jkkey, ("SS", tt)])
            rsqrt_pool(RS[:, 0:ntiles], SS[:, 0:ntiles], ntiles, 1.0 / D, [("SS", t) for t in range(ntiles)], ["RS"])
            for tt in range(ntiles):
                hb = hbs[tt % 2]
                hk = hbkeys[tt % 2]
                S.add("dve", lambda e, tt=tt, hb=hb: e.scalar_tensor_tensor(hb, src3[:, tt, :], RS[:, tt:tt + 1], gain_ap,
                                                                        ALU.mult, ALU.mult),
                      reads=[(xkey, tt), "RS", gain_key], writes=[hk])
                for c in range(DC):
                    S.add("pe", lambda e, c=c, hb=hb: e.transpose(TP[:, c, :], hb[:, c * 128:(c + 1) * 128], IDENT[:]),
                          reads=[hk, "IDENT"], writes=[("TP", 0), ("TP", 1), "TPB"])
                S.add("act", lambda e, tt=tt: e.copy(dstT[:, :, tt * 128:(tt + 1) * 128], TP[:, :, :]),
                      reads=[("TP", 0), ("TP", 1)], writes=[(dstkey, tt), "TPB"] + (["HTall"] if dstkey == "HT" else []))

        STs = [ST8, ST8b]
        RTs = [RT8, RT8b]
        QKBs = [QKB, QKB2]
        TP_K = [("TP", 0), ("TP", 1)]

        def prepA(k, sqb, sqkey, psv, nh, rkeys):
            sq3 = sqb[:, 0:nh * 64].rearrange("p (a b) -> p a b", a=nh)
            S.add("act", lambda e: e.activation(sq3, psv, AF.Square), reads=rkeys, writes=[sqkey])
            S.add("dve", lambda e: e.tensor_reduce(STs[k][:, 0:nh], sq3, AX.X, ALU.add), reads=[sqkey], writes=[("ST8", k)])
            rsqrt_pool(RTs[k][:, 0:nh], STs[k][:, 0:nh], nh, 1.0 / 64, [("ST8", k)], [("RT8", k)])

        def prepB(k, psv, nh, gainv, cs, outs, rkeys, wkeys, gkey):
            n = nh * 64
            qn3 = QN[:, 0:n].rearrange("p (a b) -> p a b", a=nh)
            rt3 = RTs[k][:, 0:nh].unsqueeze(2).broadcast_to([128, nh, 64])
            S.add("dve", lambda e: e.tensor_tensor(qn3, psv, gainv, ALU.mult), reads=rkeys + [gkey], writes=["QN"])
            if cs is not None:
                ccv, ssv = cs
                cc3 = ccv.unsqueeze(1).broadcast_to([128, nh, 16])
                sa3 = ssv[:, 0:8].unsqueeze(1).broadcast_to([128, nh, 8])
                sb3 = ssv[:, 8:16].unsqueeze(1).broadcast_to([128, nh, 8])
                t = qn3[:, :, 0:16]
                ra = RA[:, 0:nh * 16].rearrange("p (a b) -> p a b", a=nh)
                rb = RB[:, 0:nh * 16].rearrange("p (a b) -> p a b", a=nh)
                S.add("dve", lambda e: e.tensor_tensor(ra, t, cc3, ALU.mult), reads=["QN", "ROPE"], writes=[("ROPESCR", 0)])
                S.add("dve", lambda e: e.tensor_tensor(rb[:, :, 0:8], qn3[:, :, 8:16], sa3, ALU.mult), reads=["QN", "ROPE"], writes=[("ROPESCR", 1)])
                S.add("dve", lambda e: e.tensor_tensor(rb[:, :, 8:16], qn3[:, :, 0:8], sb3, ALU.mult), reads=["QN", "ROPE"], writes=[("ROPESCR", 2)])
                S.add("dve", lambda e: e.tensor_tensor(t, ra, rb, ALU.add), reads=[("ROPESCR", 0), ("ROPESCR", 1), ("ROPESCR", 2)], writes=["QN"])
            for (h0, h1, oap, vf) in outs:
                S.add("dve", lambda e, h0=h0, h1=h1, oap=oap, vf=vf: e.tensor_tensor(oap, vf(qn3[:, h0:h1, :]), vf(rt3[:, h0:h1, :]), ALU.mult),
                      reads=["QN", ("RT8", k)], writes=wkeys)

        def skew(n, stageA, stageB):
            for t in range(n + 1):
                if t < n:
                    stageA(t)
                if t >= 1:
                    stageB(t - 1)

        ident_v = (lambda v: v)

        def proj_tok(bank, bkey, tok_ap_fn, wv, ncols, wkey):
            for c in range(DC):
                S.add("pe", lambda e, c=c: e.matmul(bank[:, 0:ncols], tok_ap_fn(c), wv[:, c, 0:ncols],
                                                    start=(c == 0), stop=(c == DC - 1)),
                      reads=["HTall"] + wkey, writes=[bkey])

        HT_KEYS = [("HT", t) for t in range(TT)]

        def mark_ht_ready():
            pass

        for s in range(nseq):
            for q4 in range(4):
                S.dma(lambda e, s=s, q4=q4: e.dma_start(
                    out=X[:, q4 * 4:(q4 + 1) * 4, :],
                    in_=x_d[s, q4 * 512:(q4 + 1) * 512, :].rearrange("(t p) d -> p t d", p=128)),
                    writes=[("X", t) for t in range(q4 * 4, q4 * 4 + 4)])
            for li in layers:
                j = li // 2
                is_a = (li % 2 == 0)
                nmix = 8 if is_a else 4
                nch = nmix + 2
                S.dma(lambda e, li=li: e.dma_start(out=HG[:], in_=hg_d[li]), writes=["HG"])
                S.dma(lambda e, li=li: e.dma_start(out=CW[:].rearrange("p a b -> p (a b)"), in_=cw_d[li]), writes=["CW"])
                XQG = HG[:, 768:1024].rearrange("p (a b) -> p a b", a=4)
                XKG = HG[:, 1024:1280].rearrange("p (a b) -> p a b", a=4)
                S.dma(lambda e, li=li: e.dma_start(out=GAIN, in_=gains_d[li, 2]), writes=["GAIN"])
                S.dma(lambda e, s=s: e.dma_start(out=MEMX, in_=mem_d[s].rearrange("(t p) d -> p t d", p=128)),
                      writes=[("MEMX", 0), ("MEMX", 1)])
                load_w(WIN[:, :, 0:512], wkv_d[li].rearrange("(c p) n -> p c n", p=128), WIN_K)
                MEMT = QTB[:, 0, 0:DC * 256].rearrange("p (a b) -> p a b", a=DC)
                norm_to_HT(MEMX, 2, "GAIN", [HB0, HB1], JK, "JK", ["HB0", "HB1"], MEMT, "QTB", "MEMX", GAIN)
                pjb = [(PJ[:, :], "PJ"), (ACC[:, 0, :], ("ACC", 0))]
                sqb = [(SCB[:, 0, :], ("SCB", 0)), (SCB[:, 1, :], ("SCB", 1))]

                def memA(mt):
                    bank, bk = pjb[mt % 2]
                    for c in range(DC):
                        S.add("pe", lambda e, c=c, mt=mt, bank=bank: e.matmul(bank, MEMT[:, c, mt * 128:(mt + 1) * 128], WIN[:, c, 0:512],
                                                                          start=(c == 0), stop=(c == DC - 1)),
                              reads=[("QTB", 0), ("QTB", 1)] + WIN_K, writes=[bk])
                    prepA(mt % 2, sqb[mt % 2][0], sqb[mt % 2][1], bank[:, 0:256].rearrange("p (a b) -> p a b", a=4), 4, [bk])
                    S.add("act", lambda e, mt=mt, bank=bank: e.copy(MV1[:, mt, :, 0:64], bank[:, 256:512].rearrange("p (a b) -> p a b", a=4)),
                          reads=[bk], writes=["MV1"])

                def memB(mt):
                    bank, bk = pjb[mt % 2]
                    k = mt % 2
                    prepB(k, bank[:, 0:256].rearrange("p (a b) -> p a b", a=4), 4, XKG, None,
                          [(0, 4, QKBs[k][:, 0:256].rearrange("p (a b) -> p a b", a=4), ident_v)], [bk], [("QKB", k)], "HG")
                    for c2 in range(2):
                        S.add("pe", lambda e, c2=c2, k=k: e.transpose(TP[:, 4 * k + c2, :], QKBs[k][:, c2 * 128:(c2 + 1) * 128], IDENT[:]),
                              reads=[("QKB", k), "IDENT"], writes=[("TP", k), "TPB"])
                    S.add("act", lambda e, mt=mt, k=k: e.copy(MKT[:, :, mt * 128:(mt + 1) * 128], TP[:, 4 * k:4 * k + 2, :]),
                          reads=[("TP", k)], writes=["MKT", "TPB"])

                skew(2, memA, memB)
                S.dma(lambda e, li=li: e.dma_start(out=GAIN, in_=gains_d[li, 0]), writes=["GAIN"])
                norm_to_HT(X, TT, "GAIN", [HB0, HB1], JK, "JK", ["HB0", "HB1"], HT, "HT", "X", GAIN)
                mark_ht_ready()

                win_d = awin_d if is_a else bwin_d
                xq_col0 = 3072 if is_a else 4608
                wview = win_d[j].rearrange("(c p) n -> p c n", p=128)
                load_w(WIN[:, :, 0:256], wview[:, :, xq_col0:xq_col0 + 256], WIN_K)
                XQT = [QTB[:, 0, :], QTB[:, 1, :]]
                XQK = [("QTB", 0), ("QTB", 1)]
                def xqA(tt):
                    bank, bk = pjb[tt % 2]
                    proj_tok(bank, bk, lambda c, tt=tt: HT[:, c, tt * 128:(tt + 1) * 128], WIN, 256, WIN_K)
                    prepA(tt % 2, sqb[tt % 2][0], sqb[tt % 2][1], bank[:, 0:256].rearrange("p (a b) -> p a b", a=4), 4, [bk])

                def xqB(tt):
                    bank, bk = pjb[tt % 2]
                    k = tt % 2
                    prepB(k, bank[:, 0:256].rearrange("p (a b) -> p a b", a=4), 4, XQG, None,
                          [(0, 4, QKBs[k][:, 0:256].rearrange("p (a b) -> p a b", a=4), ident_v)], [bk], [("QKB", k)], "HG")
                    for c2 in range(2):
                        S.add("pe", lambda e, c2=c2, k=k: e.transpose(TP[:, 4 * k + c2, :], QKBs[k][:, c2 * 128:(c2 + 1) * 128], IDENT[:]),
                              reads=[("QKB", k), "IDENT"], writes=[("TP", k), "TPB"])
                    for c2 in range(2):
                        S.add("act", lambda e, tt=tt, c2=c2, k=k: e.copy(XQT[c2][:, tt * 128:(tt + 1) * 128], TP[:, 4 * k + c2, :]),
                              reads=[("TP", k)], writes=[XQK[c2], "TPB"])

                skew(TT, xqA, xqB)

                def load_pair(hp_):
                    for seg, (c0, w) in enumerate([(128 * hp_, 128), (512 + 128 * hp_, 128), (1024 + 128 * hp_, 128),
                                                   (1536 + 128 * hp_, 128)]):
                        load_w(WIN[:, :, seg * 128:(seg + 1) * 128], wview[:, :, c0:c0 + w], WIN_K)
                    load_w(WIN[:, :, 512:768], wview[:, :, 2048 + 256 * hp_:2048 + 256 * hp_ + 256], WIN_K)

                def load_bround(hp_, g_):
                    gb_ = g_ % 2
                    for kind in range(3):
                        c0 = kind * 1536 + g_ * 512 + hp_ * 128
                        load_w(WIN[:, :, gb_ * 384 + kind * 128:gb_ * 384 + (kind + 1) * 128], wview[:, :, c0:c0 + 128], [("WIN", gb_)])

                if is_a:
                    load_pair(0)
                else:
                    load_bround(0, 0)
                xits = [(c2, qc, hl) for c2 in range(2) for qc in range(4) for hl in range(2)]

                def x_score(i):
                    c2, qc, hl = xits[i]
                    r0 = hl * 64
                    sc, sk = (SCA, SCA_K) if i % 2 == 0 else (SCB, SCB_K)
                    et, ek = (ET0, "ET0") if i % 2 == 0 else (ET1, "ET1")
                    for mt in range(2):
                        S.add("pe", lambda e, mt=mt, sc=sc, c2=c2, r0=r0, qc=qc: e.matmul(
                            sc[:, mt, :], MKT[r0:r0 + 64, c2, mt * 128:(mt + 1) * 128],
                            XQT[c2][r0:r0 + 64, qc * 512:(qc + 1) * 512], start=True, stop=True),
                            reads=["MKT", XQK[c2]], writes=sk)
                    S.add("act", lambda e, sc=sc, et=et: e.activation(et[:, :, :], sc[:, :, :], AF.Exp, scale=0.125),
                          reads=sk, writes=[ek])

                def x_av(i):
                    c2, qc, hl = xits[i]
                    h = 2 * c2 + hl
                    et, ek = (ET0, "ET0") if i % 2 == 0 else (ET1, "ET1")
                    for qt in range(4):
                        for mt in range(2):
                            S.add("pe", lambda e, qt=qt, mt=mt, et=et, h=h, hl=hl: e.matmul(
                                ACC[:, hl, qt * 65:(qt + 1) * 65], et[:, mt, qt * 128:(qt + 1) * 128], MV1[:, mt, h, :],
                                start=(qt == 0 and mt == 0), stop=(mt == 1), skip_group_check=True),
                                reads=[ek, "MV1"], writes=[("ACC", hl)])
                    accv = ACC[:, hl, 0:260].rearrange("p (a b) -> p a b", a=4)
                    rr = R0 if hl == 0 else R1
                    S.add("dve", lambda e, accv=accv, rr=rr: e.reciprocal(rr[:, :], accv[:, :, 64]),
                          reads=[("ACC", hl)], writes=[("EPI", hl)])
                    S.add("dve", lambda e, accv=accv, hl=hl, rr=rr: e.tensor_tensor(
                        MB[:, :, hl * 64:(hl + 1) * 64], accv[:, :, 0:64],
                        rr[:, :].unsqueeze(2).broadcast_to([128, 4, 64]), ALU.mult),
                        reads=[("ACC", hl), ("EPI", hl)], writes=["MB"])
                    if hl == 1:
                        k = (i // 2) % 2
                        for qt in range(4):
                            S.add("pe", lambda e, qt=qt, k=k: e.transpose(TP[:, 4 * k + qt, :], MB[:, qt, :], IDENT[:]),
                                  reads=["MB", "IDENT"], writes=[("TP", k), "TPB"])
                        S.add("act", lambda e, qc=qc, c2=c2, nmix=nmix, k=k: e.copy(
                            MIXT[:, nmix + c2, qc * 512:(qc + 1) * 512], TP[:, 4 * k:4 * k + 4, :].rearrange("p a b -> p (a b)")),
                            reads=[("TP", k)], writes=[mixkey(nmix + c2), "TPB"])

                x_score(0)
                for i in range(len(xits)):
                    if i + 1 < len(xits):
                        x_score(i + 1)
                    x_av(i)

                if is_a:
                    QKG = HG[:, 0:512].rearrange("p (a b) -> p a b", a=8)
                    lam_init = 0.8 - 0.6 * float(np.exp(-0.3 * li))
                    lp = HG[:, 1408:1664].rearrange("p (a b) -> p a b", a=4)
                    S.add("dve", lambda e: e.tensor_tensor(QN[:, 0:128].rearrange("p (a b) -> p a b", a=2), lp[:, 0:4:2, :], lp[:, 1:4:2, :], ALU.mult),
                          reads=["HG"], writes=["QN"])
                    S.add("dve", lambda e: e.tensor_reduce(LAM[:, 0:2], QN[:, 0:128].rearrange("p (a b) -> p a b", a=2), AX.X, ALU.add),
                          reads=["QN"], writes=["LAM"])
                    S.add("act", lambda e: e.activation(LAM[:, 2:4], LAM[:, 0:2], AF.Exp), reads=["LAM"], writes=["LAM"])
                    S.add("dve", lambda e: e.scalar_tensor_tensor(LAM[:, 4:5], LAM[:, 2:3], -1.0, LAM[:, 3:4], ALU.mult, ALU.add),
                          reads=["LAM"], writes=["LAM"])
                    S.add("dve", lambda e, lam_init=lam_init: e.tensor_scalar(LAM[:, 5:6], LAM[:, 4:5], -lam_init, None, ALU.add),
                          reads=["LAM"], writes=["LAM"])
                    NEGLAM = LAM[:, 5:6]
                    S.add("dve", lambda e, lam_init=lam_init: e.tensor_scalar(SLGS[:], HG[:, 1280:1408], 1.0 - lam_init, None, ALU.mult),
                          reads=["HG"], writes=["SLGS"])
                    for hp in range(4):
                        vbk = [(SCA[:, 0, :], ("SCA", 0)), (SCA[:, 1, :], ("SCA", 1))]

                        def aA(tt):
                            bank, bk = pjb[tt % 2]
                            proj_tok(bank, bk, lambda c, tt=tt: HT[:, c, tt * 128:(tt + 1) * 128], WIN, 512, WIN_K)
                            prepA(tt % 2, sqb[tt % 2][0], sqb[tt % 2][1], bank[:, 0:512].rearrange("p (a b) -> p a b", a=8), 8, [bk])
                            vb_, vk_ = vbk[tt % 2]
                            for c in range(DC):
                                S.add("pe", lambda e, c=c, tt=tt, vb_=vb_: e.matmul(vb_[:, 0:256], HT[:, c, tt * 128:(tt + 1) * 128],
                                                                                WIN[:, c, 512:768], start=(c == 0), stop=(c == DC - 1)),
                                      reads=["HTall"] + WIN_K, writes=[vk_])
                            S.add("act", lambda e, tt=tt, vb_=vb_: e.copy(V1A[:, tt, :, 0:128], vb_[:, 0:256].rearrange("p (a b) -> p a b", a=2)),
                                  reads=[vk_], writes=["V1A"])

                        def aB(tt):
                            bank, bk = pjb[tt % 2]
                            k = tt % 2
                            vfa = (lambda v: v.rearrange("p (c h) d -> p c h d", c=2))
                            outs_a = [(0, 4, QKBs[k][:, 0:256].rearrange("p (h c d) -> p c h d", h=2, c=2), vfa),
                                      (4, 8, QKBs[k][:, 256:512].rearrange("p (h c d) -> p c h d", h=2, c=2), vfa)]
                            prepB(k, bank[:, 0:512].rearrange("p (a b) -> p a b", a=8), 8, QKG,
                                  (ROPE[:, 0, 0, tt, :], ROPE[:, 0, 1, tt, :]), outs_a, [bk], [("QKB", k)], "HG")
                            for c4 in range(4):
                                S.add("pe", lambda e, c4=c4, k=k: e.transpose(TP[:, 4 * k + c4, :], QKBs[k][:, c4 * 128:(c4 + 1) * 128], IDENT[:]),
                                      reads=[("QKB", k), "IDENT"], writes=[("TP", k), "TPB"])
                            S.add("act", lambda e, tt=tt, k=k: e.copy(QTB[:, 0:2, tt * 128:(tt + 1) * 128], TP[:, 4 * k:4 * k + 2, :]),
                                  reads=[("TP", k)], writes=[("QTB", 0), ("QTB", 1), "TPB"])
                            S.add("act", lambda e, tt=tt, k=k: e.copy(KTB[:, 0:2, tt * 128:(tt + 1) * 128], TP[:, 4 * k + 2:4 * k + 4, :]),
                                  reads=[("TP", k)], writes=[("KTB", 0), ("KTB", 1), "TPB"])

                        skew(TT, aA, aB)
                        if hp + 1 < 4:
                            load_pair(hp + 1)
                        if hp == 0:
                            S.add("pool", lambda e: e.memset(V1A[:, :, :, 128:129], 1.0), writes=["V1A"])
                        aits = [(hl, qc, comp, kp) for hl in range(2) for qc in range(4) for comp in range(2) for kp in range(8)]

                        def a_score(i):
                            hl, qc, comp, kp = aits[i]
                            r0 = comp * 64
                            sc, sk = (SCA, SCA_K) if i % 2 == 0 else (SCB, SCB_K)
                            et, ek = (ET0, "ET0") if i % 2 == 0 else (ET1, "ET1")
                            for k2 in range(2):
                                kt = 2 * kp + k2
                                S.add("pe", lambda e, sc=sc, k2=k2, kt=kt, r0=r0, hl=hl, qc=qc: e.matmul(
                                    sc[:, k2, :], KTB[r0:r0 + 64, hl, kt * 128:(kt + 1) * 128],
                                    QTB[r0:r0 + 64, hl, qc * 512:(qc + 1) * 512], start=True, stop=True),
                                    reads=[("KTB", hl), ("QTB", hl)], writes=sk)
                            S.add("act", lambda e, sc=sc, et=et: e.activation(et[:, :, :], sc[:, :, :], AF.Exp, scale=0.125),
                                  reads=sk, writes=[ek])

                        def a_av(i):
                            hl, qc, comp, kp = aits[i]
                            h = 2 * hp + hl
                            et, ek = (ET0, "ET0") if i % 2 == 0 else (ET1, "ET1")
                            for k2 in range(2):
                                kt = 2 * kp + k2
                                for qt in range(4):
                                    bank, off = (0, qt * 129) if qt < 3 else (1, 0)
                                    S.add("pe", lambda e, et=et, k2=k2, kt=kt, qt=qt, bank=bank, off=off, hl=hl: e.matmul(
                                        ACC[:, bank, off:off + 129], et[:, k2, qt * 128:(qt + 1) * 128], V1A[:, kt, hl, :],
                                        start=(kt == 0 and qt in (0, 3)), stop=(kt == 15), skip_group_check=True),
                                        reads=[ek, "V1A"], writes=ACC_K)
                            if kp != 7:
                                return
                            S.add("dve", lambda e: e.tensor_copy(SQ[:, 0:387], ACC[:, 0, 0:387]), reads=[("ACC", 0)], writes=["SQ"])
                            S.add("dve", lambda e: e.tensor_copy(QN[:, 256:385], ACC[:, 1, 0:129]), reads=[("ACC", 1)], writes=["QN"])
                            a3 = SQ[:, 0:387].rearrange("p (a b) -> p a b", a=3)
                            a1 = QN[:, 256:385]
                            rr = R0 if comp == 0 else R1
                            S.add("dve", lambda e, rr=rr, a3=a3: e.reciprocal(rr[:, 0:3], a3[:, :, 128]), reads=["SQ"], writes=["EPI"])
                            S.add("dve", lambda e, rr=rr, a1=a1: e.reciprocal(rr[:, 3:4], a1[:, 128:129]), reads=["QN"], writes=["EPI"])
                            if comp == 0:
                                S.add("dve", lambda e, a3=a3: e.tensor_tensor(A0[:, 0:3, :], a3[:, :, 0:128],
                                                                          R0[:, 0:3].unsqueeze(2).broadcast_to([128, 3, 128]), ALU.mult),
                                      reads=["SQ", "EPI"], writes=["A0"])
                                S.add("dve", lambda e, a1=a1: e.tensor_scalar(A0[:, 3, :], a1[:, 0:128], R0[:, 3:4], None, ALU.mult),
                                      reads=["QN", "EPI"], writes=["A0"])
                                return
                            S.add("dve", lambda e: e.tensor_scalar(R1[:, :], R1[:, :], NEGLAM, None, ALU.mult),
                                  reads=["EPI", "LAM"], writes=["EPI"])
                            for qt in range(4):
                                src_ = a3[:, qt, 0:128] if qt < 3 else a1[:, 0:128]
                                S.add("dve", lambda e, qt=qt, src_=src_: e.scalar_tensor_tensor(
                                    A0[:, qt, :], src_, R1[:, qt:qt + 1], A0[:, qt, :], ALU.mult, ALU.add),
                                    reads=["SQ", "QN", "EPI", "A0"], writes=["A0"])
                            for qt in range(4):
                                S.add("dve", lambda e, qt=qt: e.scalar_tensor_tensor(
                                    QN[:, 0:128], A0[:, qt, :], 1.0, A0[:, qt, :], ALU.mult, ALU.mult, accum_out=SSE[:, qt:qt + 1]),
                                    reads=["A0"], writes=["QN", ("SSE", qt)])
                            rsqrt_pool(RSE[:, :], SSE[:, :], 4, 1.0 / 128, [("SSE", q) for q in range(4)], ["RSE"])
                            for qt in range(4):
                                S.add("dve", lambda e, qt=qt: e.scalar_tensor_tensor(
                                    MB[:, qt, :], A0[:, qt, :], RSE[:, qt:qt + 1], SLGS[:], ALU.mult, ALU.mult),
                                    reads=["A0", "RSE", "SLGS"], writes=["MB"])
                            k = (i // 16) % 2
                            for qt in range(4):
                                S.add("pe", lambda e, qt=qt, k=k: e.transpose(TP[:, 4 * k + qt, :], MB[:, qt, :], IDENT[:]),
                                      reads=["MB", "IDENT"], writes=[("TP", k), "TPB"])
                            S.add("act", lambda e, qc=qc, h=h, k=k: e.copy(
                                MIXT[:, h, qc * 512:(qc + 1) * 512], TP[:, 4 * k:4 * k + 4, :].rearrange("p a b -> p (a b)")),
                                reads=[("TP", k)], writes=[mixkey(h), "TPB"])

                        a_score(0)
                        for i in range(len(aits)):
                            if i + 1 < len(aits):
                                a_score(i + 1)
                            a_av(i)
                else:
                    BQKG = HG[:, 0:768].rearrange("p (g a b) -> p g a b", g=3, a=4)
                    abanks = [(ACC, 0, ("ACC", 0)), (ACC, 1, ("ACC", 1)), (SCB, 0, ("SCB", 0))]
                    it = 0
                    for hp in range(4):
                        for g, (window, dil) in enumerate(B_GROUPS):
                            gb = g % 2
                            wo_ = gb * 384
                            if not (hp == 0 and g == 0):
                                load_bround(hp, g)
                            L = SEQ // dil
                            nst = L // 128
                            pjb_b = [(PJ[:, :], "PJ"), (SCB[:, 1, :], ("SCB", 1))]
                            sqb_b = [(SCA[:, 0, :], ("SCA", 0)), (SCA[:, 1, :], ("SCA", 1))]

                            def bA(tj, g=g, gb=gb, dil=dil, nst=nst, wo_=wo_):
                                r, i0 = tj // nst, (tj % nst) * 128
                                lo = r + dil * i0
                                bank, bk = pjb_b[tj % 2]
                                for c in range(DC):
                                    S.add("pe", lambda e, c=c, lo=lo, bank=bank: e.matmul(
                                        bank[:, 0:384], HT[:, c, lo:lo + dil * 127 + 1:dil], WIN[:, c, wo_:wo_ + 384],
                                        start=(c == 0), stop=(c == DC - 1)),
                                        reads=["HTall", ("WIN", gb)], writes=[bk])
                                prepA(tj % 2, sqb_b[tj % 2][0], sqb_b[tj % 2][1], bank[:, 0:256].rearrange("p (a b) -> p a b", a=4), 4, [bk])
                                S.add("act", lambda e, tj=tj, bank=bank: e.copy(VB[:, gb, tj, :, 0:64],
                                                                             bank[:, 256:384].rearrange("p (a b) -> p a b", a=2)),
                                      reads=[bk], writes=[("V1A", gb)])

                            def bB(tj, g=g, gb=gb):
                                bank, bk = pjb_b[tj % 2]
                                k = tj % 2
                                prepB(k, bank[:, 0:256].rearrange("p (a b) -> p a b", a=4), 4, BQKG[:, g],
                                      (ROPE[:, g, 0, tj, :], ROPE[:, g, 1, tj, :]),
                                      [(0, 4, QKBs[k][:, 0:256].rearrange("p (a b) -> p a b", a=4), ident_v)], [bk], [("QKB", k)], "HG")
                                for c2 in range(2):
                                    S.add("pe", lambda e, c2=c2, k=k: e.transpose(TP[:, 4 * k + c2, :], QKBs[k][:, c2 * 128:(c2 + 1) * 128], IDENT[:]),
                                          reads=[("QKB", k), "IDENT"], writes=[("TP", k), "TPB"])
                                S.add("act", lambda e, tj=tj, k=k: e.copy(QTB[:, gb, tj * 128:(tj + 1) * 128], TP[:, 4 * k, :]),
                                      reads=[("TP", k)], writes=[("QTB", gb), "TPB"])
                                S.add("act", lambda e, tj=tj, k=k: e.copy(KTB[:, gb, tj * 128:(tj + 1) * 128], TP[:, 4 * k + 1, :]),
                                      reads=[("TP", k)], writes=[("KTB", gb), "TPB"])

                            skew(TT, bA, bB)
                            S.add("pool", lambda e, gb=gb: e.memset(VB[:, gb, :, :, 64:65], 1.0), writes=[("V1A", gb)])
                            bits = []
                            for hl in range(2):
                                started = set()
                                for tj in range(TT):
                                    seg, lj = tj // nst, tj % nst
                                    qlo, qhi = max(lj - 1, 0), min(lj + 1, nst - 1)
                                    firsts = []
                                    for qi in range(qlo, qhi + 1):
                                        slot = seg * nst + qi
                                        firsts.append((slot // 7) not in started)
                                        started.add(slot // 7)
                                    bits.append((hl, tj, seg, lj, qlo, qhi, firsts))

                            etb = [(ET0[:, 0, :], ("ETB", 0)), (ET0[:, 1, :], ("ETB", 1)), (ET1[:, 0, :], ("ETB", 2)), (ET1[:, 1, :], ("ETB", 3))]

                            def b_score(i, gb=gb, nst=nst):
                                hl, tj, seg, lj, qlo, qhi, firsts = bits[i]
                                r0 = hl * 64
                                n = (qhi - qlo + 1) * 128
                                m0 = (qlo - lj + 1) * 128
                                q0 = (seg * nst + qlo) * 128
                                k2 = i % 2
                                et, ek = etb[i % 4]
                                S.add("pe", lambda e: e.matmul(
                                    SCA[:, k2, 0:n], KTB[r0:r0 + 64, gb, tj * 128:(tj + 1) * 128], QTB[r0:r0 + 64, gb, q0:q0 + n],
                                    start=True, stop=True),
                                    reads=[("KTB", gb), ("QTB", gb)], writes=[("SCA", k2)])
                                S.add("act", lambda e: e.activation(et[:, 0:n], SCA[:, k2, 0:n], AF.Exp, scale=0.125),
                                      reads=[("SCA", k2)], writes=[ek])
                                S.add("dve", lambda e: e.tensor_tensor(et[:, 0:n], et[:, 0:n], MASK[:, m0:m0 + n], ALU.mult),
                                      reads=[ek, "MASK"], writes=[ek])

                            def b_av(i, g=g, gb=gb, nst=nst):
                                hl, tj, seg, lj, qlo, qhi, firsts = bits[i]
                                et, ek = etb[i % 4]
                                for n_, qi in enumerate(range(qlo, qhi + 1)):
                                    slot = seg * nst + qi
                                    bt, bi, bk = abanks[slot // 7]
                                    off = (slot % 7) * 65
                                    first = firsts[n_]
                                    S.add("pe", lambda e, qi=qi, bt=bt, bi=bi, off=off, first=first: e.matmul(
                                        bt[:, bi, off:off + 65], et[:, (qi - qlo) * 128:(qi - qlo + 1) * 128], VB[:, gb, tj, hl, :],
                                        start=first, stop=True, skip_group_check=True),
                                        reads=[ek, ("V1A", gb)], writes=[bk])
                                if tj != TT - 1:
                                    return
                                for b3 in range(3):
                                    bt, bi, bk = abanks[b3]
                                    ns = 7 if b3 < 2 else 2
                                    av = bt[:, bi, 0:ns * 65].rearrange("p (a b) -> p a b", a=ns)
                                    S.add("dve", lambda e, av=av, b3=b3, ns=ns: e.tensor_copy(
                                        NUMB[:, g, b3 * 7:b3 * 7 + ns, hl * 64:(hl + 1) * 64], av[:, :, 0:64]),
                                        reads=[bk], writes=["NUMB"])
                                    S.add("dve", lambda e, av=av, b3=b3, ns=ns: e.tensor_copy(
                                        DENF[:, g, b3 * 7:b3 * 7 + ns, hl], av[:, :, 64]),
                                        reads=[bk], writes=["DENF"])

                            b_score(0)
                            b_score(1)
                            for i in range(len(bits)):
                                if i + 2 < len(bits):
                                    b_score(i + 2)
                                b_av(i)
                        for w in range(4):
                            mm = []
                            for jj in range(4):
                                mm.append((0, 4 * w + jj, slice(jj * 128, (jj + 1) * 128), slice(0, 128)))
                            for r in range(4):
                                mm.append((1, r * 4 + w, slice(r, 512, 4), slice(0, 128)))
                            for r in range(16):
                                mm.append((2, r, slice(r, 512, 16), slice(32 * w, 32 * w + 32)))
                            for n_, (g, tj, osl, isl) in enumerate(mm):
                                S.add("pe", lambda e, g=g, tj=tj, osl=osl, isl=isl, n_=n_: e.matmul(
                                    PJ[:, osl], NUMB[:, g, tj, :], IDENT[:, isl], start=(n_ == 0), stop=(n_ == len(mm) - 1),
                                    skip_group_check=True),
                                    reads=["NUMB", "IDENT"], writes=["PJ"])
                            for n_, (g, tj, osl, isl) in enumerate(mm):
                                S.add("pe", lambda e, g=g, tj=tj, osl=osl, isl=isl, n_=n_: e.matmul(
                                    SCB[0:2, 1, osl], DENF[:, g, tj, :], IDENTF[:, isl], start=(n_ == 0), stop=(n_ == len(mm) - 1),
                                    skip_group_check=True),
                                    reads=["DENF", "IDENTF"], writes=[("SCB", 1)])
                            S.add("dve", lambda e: e.reciprocal(RDEN[:, :], SCB[0:2, 1, :]), reads=[("SCB", 1)], writes=["RDEN"])
                            S.add("pe", lambda e: e.matmul(SCA[:, 0, :], SEL[:, :], RDEN[:, :], start=True, stop=True),
                                  reads=["SEL", "RDEN"], writes=[("SCA", 0)])
                            S.add("act", lambda e: e.copy(RDB[:, :], SCA[:, 0, :]), reads=[("SCA", 0)], writes=["RDB"])
                            S.add("dve", lambda e, w=w, hp=hp: e.tensor_tensor(MIXT[:, hp, w * 512:(w + 1) * 512], PJ[:, :], RDB[:, :], ALU.mult),
                                  reads=["PJ", "RDB"], writes=[("MIXT", hp)])

                wo_d = (awout_d if is_a else bwout_d)[j].rearrange("(c p) n -> p c n", p=128)
                load_w(WO[:, 0:nch, :], wo_d, "WO")
                for tt in range(TT):
                    obt, obn = (ACC, "ACC") if tt % 2 == 0 else (SCB, "SCB")
                    for half in range(2):
                        for c in range(nch):
                            S.add("pe", lambda e, tt=tt, half=half, c=c, nch=nch, obt=obt: e.matmul(
                                obt[:, half, :], MIXT[:, c, tt * 128:(tt + 1) * 128], WO[:, c, half * 512:(half + 1) * 512],
                                start=(c == 0), stop=(c == nch - 1)),
                                reads=[mixkey(c), "WO"], writes=[(obn, half)])
                    S.add("dve", lambda e, tt=tt, obt=obt: e.tensor_tensor(X[:, tt, :], X[:, tt, :], obt[:, :, :].rearrange("p a b -> p (a b)"), ALU.add),
                          reads=[("X", tt), (obn, 0), (obn, 1)], writes=[("X", tt)])

                S.dma(lambda e, li=li: e.dma_start(out=FGAIN, in_=gains_d[li, 1]), writes=["FGAIN"])
                norm_to_HT(X, TT, "FGAIN", [FHB0, FHB1], FJK, "FJK", ["FHB0", "FHB1"], HT, "HT", "X", FGAIN)
                mark_ht_ready()
                wu_d = wup_d[li].rearrange("(c p) n -> p c n", p=128)
                wd_d = wdown_d[li]
                WUs = [(WU0, "WU0"), (WU1, "WU1")]
                Gs = [(G0, "G0"), (G1, "G1")]
                S.add("pool", lambda e: e.memset(U[:, 0:1], 0.0), writes=["U"])
                S.add("pool", lambda e: e.memset(U[:, SEQ + 1:SEQ + 2], 0.0), writes=["U"])
                upbanks = [(PJ[:, :], "PJ"), (SCA[:, 0, :], ("SCA", 0)), (SCA[:, 1, :], ("SCA", 1))]
                dnbanks = [(ACC, ACC_K), (SCB, SCB_K)]
                ub = [0]
                db = [0]

                def load_up(gi):
                    fc0, n = FFN_GROUPS[gi]
                    wu, wk = WUs[gi % 2]
                    load_w(wu[:, :, 0, 0:n * 128], wu_d[:, :, fc0 * 128:(fc0 + n) * 128], wk)
                    load_w(wu[:, :, 1, 0:n * 128], wu_d[:, :, DFF + fc0 * 128:DFF + (fc0 + n) * 128], wk)

                def up(gi):
                    fc0, n = FFN_GROUPS[gi]
                    wu, wk = WUs[gi % 2]
                    gt, gk = Gs[gi % 2]
                    for l in range(n):
                        for ab in range(2):
                            ch = ab * NFC + fc0 + l
                            for tq in range(4):
                                bank, bkey = upbanks[ub[0] % 3]
                                ub[0] += 1
                                for c in range(DC):
                                    S.add("pe", lambda e, c=c, bank=bank, wu=wu, ab=ab, l=l, tq=tq: e.matmul(
                                        bank, wu[:, c, ab, l * 128:(l + 1) * 128], HT[:, c, tq * 512:(tq + 1) * 512],
                                        start=(c == 0), stop=(c == DC - 1)),
                                        reads=["HTall", wk], writes=[bkey])
                                S.add("act", lambda e, bank=bank, tq=tq: e.copy(U[:, 1 + tq * 512:1 + (tq + 1) * 512], bank),
                                      reads=[bkey], writes=["U"])
                            Cc, ck = (CA, "CA") if ab == 0 else (CB, "CB")
                            S.add("dve", lambda e, Cc=Cc, ch=ch: e.tensor_scalar(Cc[:, :], U[:, 1:SEQ + 1], CW[:, ch, 1:2], CW[:, ch, 3:4],
                                                                             ALU.mult, ALU.add),
                                  reads=["U", "CW"], writes=[ck])
                            S.add("dve", lambda e, Cc=Cc, ch=ch: e.scalar_tensor_tensor(Cc[:, :], U[:, 0:SEQ], CW[:, ch, 0:1], Cc[:, :],
                                                                                    ALU.mult, ALU.add),
                                  reads=["U", "CW", ck], writes=[ck])
                            S.add("dve", lambda e, Cc=Cc, ch=ch: e.scalar_tensor_tensor(Cc[:, :], U[:, 2:SEQ + 2], CW[:, ch, 2:3], Cc[:, :],
                                                                                    ALU.mult, ALU.add),
                                  reads=["U", "CW", ck], writes=[ck])
                            if ab == 0:
                                S.add("act", lambda e: e.activation(CA[:, :], CA[:, :], AF.Silu), reads=["CA"], writes=["CA"])
                            else:
                                S.add("pool", lambda e, gt=gt, l=l: e.tensor_tensor(gt[:, l, :], CA[:, :], CB[:, :], ALU.mult),
                                      reads=["CA", "CB"], writes=[gk])

                def load_down(gi):
                    fc0, n = FFN_GROUPS[gi]
                    load_w(WD[:, 0:n, :], wd_d[fc0 * 128:(fc0 + n) * 128, :].rearrange("(c p) n -> p c n", p=128), "WD")

                def down(gi):
                    fc0, n = FFN_GROUPS[gi]
                    gt, gk = Gs[gi % 2]
                    for tt in range(TT):
                        bt, bkey = dnbanks[db[0] % 2]
                        db[0] += 1
                        for half in range(2):
                            for l in range(n):
                                S.add("pe", lambda e, bt=bt, half=half, l=l, tt=tt, gt=gt: e.matmul(
                                    bt[:, half, :], gt[:, l, tt * 128:(tt + 1) * 128], WD[:, l, half * 512:(half + 1) * 512],
                                    start=(l == 0), stop=(l == n - 1)),
                                    reads=[gk, "WD"], writes=bkey)
                        S.add("dve", lambda e, tt=tt, bt=bt: e.tensor_tensor(X[:, tt, :], X[:, tt, :], bt[:, :, :].rearrange("p a b -> p (a b)"), ALU.add),
                              reads=[("X", tt)] + bkey, writes=[("X", tt)])

                ng = len(FFN_GROUPS)
                load_up(0)
                load_up(1)
                up(0)
                load_down(0)
                for gi in range(1, ng):
                    up(gi)
                    if gi + 1 < ng:
                        load_up(gi + 1)
                    down(gi - 1)
                    load_down(gi)
                down(ng - 1)
            for q4 in range(4):
                S.dma(lambda e, s=s, q4=q4: e.dma_start(
                    out=y_d[s, q4 * 512:(q4 + 1) * 512, :].rearrange("(t p) d -> p t d", p=128),
                    in_=X[:, q4 * 4:(q4 + 1) * 4, :]),
                    reads=[("X", t) for t in range(q4 * 4, q4 * 4 + 4)], is_output=True)
        S.emit()
    return nc


def _const_tables():
    rot = 16
    half = 8
    inv = (np.float32(500000.0) ** (-(np.arange(half, dtype=np.float32) * np.float32(2.0) / np.float32(rot)))).astype(np.float32)
    rope = np.zeros((3, 2, 128, TT, 16), np.float32)
    for g, (window, dil) in enumerate(B_GROUPS):
        L = SEQ // dil
        nst = L // 128
        for tj in range(TT):
            r, i0 = tj // nst, (tj % nst) * 128
            pos = (r + dil * (i0 + np.arange(128))).astype(np.float32)
            ang = (pos[:, None] * inv[None, :]).astype(np.float32)
            cs_, sn_ = np.cos(ang), np.sin(ang)
            rope[g, 0, :, tj, :] = np.concatenate([cs_, cs_], axis=1)
            rope[g, 1, :, tj, :] = np.concatenate([-sn_, sn_], axis=1)
    rope = rope.reshape(3, 2, 128, TT * 16)
    ident = np.eye(128, dtype=np.float32)
    k = np.arange(128)[:, None]
    c = np.arange(384)[None, :]
    rel = (c // 128 - 1) * 128 + (c % 128) - k
    mask = np.where(np.abs(rel) <= 64, 1.0, 0.0).astype(np.float32)
    sel = np.zeros((2, 128), np.float32)
    sel[0, :64] = 1.0
    sel[1, 64:] = 1.0
    return rope, ident, mask, sel


def _prep_shared(inp):
    f = lambda a: np.ascontiguousarray(np.asarray(a, dtype=np.float32))
    gains = np.zeros((4, 3, 128, D), np.float32)
    hg = np.zeros((4, 128, 1664), np.float32)
    cw = np.zeros((4, 128, 44, 4), np.float32)
    for i in range(4):
        j = i // 2
        gains[i, 0] = np.broadcast_to(f(inp["norm_mix"])[i][None, :], (128, D))
        gains[i, 1] = np.broadcast_to(f(inp["norm_ffn"])[i][None, :], (128, D))
        gains[i, 2] = np.broadcast_to(f(inp["norm_mem"])[i][None, :], (128, D))
        row = np.zeros(1664, np.float32)
        if i % 2 == 0:
            qg = f(inp["a_q_norm"])[j]
            kg = f(inp["a_k_norm"])[j]
            row[0:512] = np.concatenate([qg] * 4 + [kg] * 4)
            row[1280:1408] = f(inp["a_subln"])[j]
            row[1408:1664] = f(inp["a_lambda"])[j].reshape(-1)
        else:
            for g in range(3):
                qg = f(inp["b_q_norm"])[j, g]
                kg = f(inp["b_k_norm"])[j, g]
                row[g * 256:(g + 1) * 256] = np.concatenate([qg, qg, kg, kg])
        row[768:1024] = np.tile(f(inp["xq_norm"])[i], 4)
        row[1024:1280] = np.tile(f(inp["xk_norm"])[i], 4)
        hg[i] = np.broadcast_to(row[None, :], (128, 1664))
        cwi = f(inp["conv_w"])[i].reshape(3, 44, 128)
        cbi = f(inp["conv_b"])[i].reshape(44, 128)
        cw[i, :, :, 0:3] = cwi.transpose(2, 1, 0)
        cw[i, :, :, 3] = cbi.T
    rope, ident, mask, sel = _const_tables()
    shared = {
        "w_mem_kv": f(inp["w_mem_kv"]), "a_w_in": f(inp["a_w_in"]), "a_w_out": f(inp["a_w_out"]),
        "b_w_in": f(inp["b_w_in"]), "b_w_out": f(inp["b_w_out"]), "w_up": f(inp["w_up"]), "w_down": f(inp["w_down"]),
        "gains": gains, "hg": hg, "cw": cw.reshape(4, 128, 176), "rope": rope, "ident": ident, "mask": mask, "sel": sel,
    }
    return shared


_PROGRAM_CACHE = {}


def _get_program(nseq, layers):
    key = (nseq, tuple(layers))
    if key not in _PROGRAM_CACHE:
        _PROGRAM_CACHE[key] = build_program(nseq, list(layers))
    return _PROGRAM_CACHE[key]


def kernel(**inp):
    xp = np.asarray(inp["x_prompt"], dtype=np.float32)
    xs = np.asarray(inp["x_sample"], dtype=np.float32)
    mp = np.asarray(inp["mem_prompt"], dtype=np.float32)
    ms = np.asarray(inp["mem_sample"], dtype=np.float32)
    x_all = np.concatenate([xp, xs], axis=0)
    m_all = np.concatenate([mp, ms], axis=0)
    nb = xp.shape[0]
    shared = _prep_shared(inp)
    nc = _get_program(SEQ_PER_CORE, (0, 1, 2, 3))
    in_maps = []
    for c in range(N_CORES):
        d = dict(shared)
        d["x"] = np.ascontiguousarray(x_all[c * SEQ_PER_CORE:(c + 1) * SEQ_PER_CORE])
        d["mem"] = np.ascontiguousarray(m_all[c * SEQ_PER_CORE:(c + 1) * SEQ_PER_CORE])
        in_maps.append(d)
    res = run_bass_kernel_spmd(nc, in_maps, core_ids=list(range(N_CORES)))
    y = np.concatenate([np.asarray(r["y"], dtype=np.float32) for r in res.results], axis=0)
    return (y[:nb], y[nb:])
```
